# Optimizing a Trainium2 kernel written in Bass

```python
import math
import jax, jax.numpy as jnp
from jax import lax
import numpy as np

D_MODEL = 1024
BATCH = 8
SEQ = 4096
DEPTH = 4

CTX_LEN = 256
GRID_W = 64
N_EVEN = (DEPTH + 1) // 2
N_ODD = DEPTH // 2
EPS = 1e-6

NA_HEADS = 8
NA_DIM = 64
NA_WIN_R = 8
NA_WIN_C = 16
A_W = NA_HEADS * NA_DIM

DN_HEADS = 4
DN_DIM = 128
DN_W = DN_HEADS * DN_DIM
DN_CHUNK = 64
CONV_K = 3
ROPE_BASE = 10000.0

EVEN_IN = 3 * A_W + 4 * DN_W + 4 * DN_HEADS

GM_HALF = 3 * D_MODEL
GM_GROUPS = 8
GM_CHUNK = 128

D_FF = 4 * D_MODEL

kernel_name = 'hybrid_na_gdn_gmlp_diffusion_block'


def _rmsnorm(x, g):
    xf = x.astype(jnp.float32)
    y = xf * lax.rsqrt(jnp.mean(xf * xf, axis=-1, keepdims=True) + EPS)
    return (y * g.astype(jnp.float32)).astype(x.dtype)


def _l2norm(x):
    xf = x.astype(jnp.float32)
    return (xf * lax.rsqrt(jnp.sum(xf * xf, axis=-1, keepdims=True) + EPS)).astype(x.dtype)


def _adaln(cond, w, b):
    m = jax.nn.silu(cond) @ w + b
    return [t[:, None, :] for t in jnp.split(m, 6, axis=-1)]


def _axial_rope(seq_len):
    t = jnp.arange(seq_len)
    row = (t // GRID_W).astype(jnp.float32)
    col = (t % GRID_W).astype(jnp.float32)
    n_freq = DN_DIM // 4
    inv = ROPE_BASE ** (-jnp.arange(n_freq, dtype=jnp.float32) / n_freq)
    ar = row[:, None] * inv
    ac = col[:, None] * inv
    ang = jnp.concatenate([ar, ar, ac, ac], axis=-1)
    return jnp.cos(ang), jnp.sin(ang)


def _apply_rope(x, cos, sin):
    x1, x2, x3, x4 = jnp.split(x, 4, axis=-1)
    rot = jnp.concatenate([-x2, x1, -x4, x3], axis=-1)
    return x * cos[None, :, None, :] + rot * sin[None, :, None, :]


def _short_conv(x, w):
    y = lax.conv_general_dilated(
        x, w[:, None, :].astype(x.dtype), window_strides=(1,),
        padding=[(CONV_K // 2, CONV_K // 2)],
        dimension_numbers=('NWC', 'WIO', 'NWC'), feature_group_count=x.shape[-1])
    return jax.nn.silu(y)


def _neighbourhood_attention(q, k, v, k_ctx, v_ctx, rpb):
    B, L, H, d = q.shape
    rows = L // GRID_W
    wr = min(NA_WIN_R, rows)
    qg = q.reshape(B, rows, GRID_W, H, d)
    kg = k.reshape(B, rows, GRID_W, H, d)
    vg = v.reshape(B, rows, GRID_W, H, d)
    scale = d ** -0.5
    col = jnp.arange(GRID_W)
    cstart = jnp.clip(col - NA_WIN_C // 2, 0, GRID_W - NA_WIN_C)
    col_ok = (col[None, :] >= cstart[:, None]) & (col[None, :] < cstart[:, None] + NA_WIN_C)
    dc_idx = jnp.clip(col[None, :] - col[:, None], -(NA_WIN_C - 1), NA_WIN_C - 1) + NA_WIN_C - 1
    rpb32 = rpb.astype(jnp.float32)

    def one_row(r):
        rs = jnp.clip(r - wr // 2, 0, rows - wr)
        q_r = lax.dynamic_index_in_dim(qg, r, axis=1, keepdims=False)
        k_b = lax.dynamic_slice_in_dim(kg, rs, wr, axis=1)
        v_b = lax.dynamic_slice_in_dim(vg, rs, wr, axis=1)
        dr_idx = rs + jnp.arange(wr) - r + NA_WIN_R - 1
        bias = rpb32[:, dr_idx[None, :, None], dc_idx[:, None, :]]
        bias = jnp.where(col_ok[:, None, :], bias, -jnp.inf)
        s_loc = jnp.einsum('bqhd,bwkhd->bhqwk', q_r, k_b).astype(jnp.float32) * scale + bias[None]
        s_ctx = jnp.einsum('bqhd,bchd->bhqc', q_r, k_ctx).astype(jnp.float32) * scale
        n_loc = wr * GRID_W
        s = jnp.concatenate([s_loc.reshape(B, H, GRID_W, n_loc), s_ctx], axis=-1)
        p = jax.nn.softmax(s, axis=-1).astype(v.dtype)
        o = jnp.einsum('bhqn,bnhd->bqhd', p[..., :n_loc], v_b.reshape(B, n_loc, H, d))
        o = o + jnp.einsum('bhqc,bchd->bqhd', p[..., n_loc:], v_ctx)
        return o

    out = lax.map(one_row, jnp.arange(rows))
    return out.transpose(1, 0, 2, 3, 4).reshape(B, L, H * d)


def _context_attention(q, k, v):
    B, Lc, H, d = q.shape
    s = jnp.einsum('bqhd,bkhd->bhqk', q, k).astype(jnp.float32) * d ** -0.5
    p = jax.nn.softmax(s, axis=-1).astype(v.dtype)
    return jnp.einsum('bhqk,bkhd->bqhd', p, v).reshape(B, Lc, H * d)


def _delta_chunked(q, k, v, beta, logg, s0):
    B, L, H, dk = q.shape
    dv = v.shape[-1]
    n = L // DN_CHUNK
    C = DN_CHUNK

    def blk(t):
        return t.astype(jnp.float32).reshape(B, n, C, H, -1).transpose(1, 0, 3, 2, 4)

    qb, kb, vb = blk(q), blk(k), blk(v)
    bb = beta.astype(jnp.float32).reshape(B, n, C, H).transpose(1, 0, 3, 2)
    gam = jnp.cumsum(logg.astype(jnp.float32).reshape(B, n, C, H).transpose(1, 0, 3, 2), axis=-1)
    idx = jnp.arange(C)
    incl = idx[:, None] >= idx[None, :]
    strict = idx[:, None] > idx[None, :]
    decay = jnp.exp(jnp.where(incl, gam[..., :, None] - gam[..., None, :], -jnp.inf))
    kk = jnp.einsum('nbhrd,nbhjd->nbhrj', kb, kb)
    lmat = jnp.where(strict, bb[..., :, None] * kk * decay, 0.0)
    eg = jnp.exp(gam)
    rhs = jnp.concatenate([bb[..., None] * vb, (bb * eg)[..., None] * kb], axis=-1)
    sol = lax.linalg.triangular_solve(lmat, rhs, left_side=True, lower=True, unit_diagonal=True)
    u_v, w_k = sol[..., :dv], sol[..., dv:]
    aqk = jnp.einsum('nbhrd,nbhjd->nbhrj', qb, kb) * decay
    qg = qb * eg[..., None]
    kdec = kb * jnp.exp(gam[..., -1:] - gam)[..., None]
    gl = jnp.exp(gam[..., -1])

    def step(S, xs):
        u_v_n, w_k_n, aqk_n, qg_n, kdec_n, gl_n = xs
        u = u_v_n - jnp.einsum('bhck,bhkv->bhcv', w_k_n, S)
        o = jnp.einsum('bhck,bhkv->bhcv', qg_n, S) + jnp.einsum('bhcj,bhjv->bhcv', aqk_n, u)
        S = gl_n[..., None, None] * S + jnp.einsum('bhck,bhcv->bhkv', kdec_n, u)
        return S, o

    S, o = lax.scan(step, s0.astype(jnp.float32), (u_v, w_k, aqk, qg, kdec, gl))
    o = o.transpose(1, 0, 3, 2, 4).reshape(B, L, H, dv)
    return o.astype(v.dtype), S


def _dn_inputs(p, conv_w, a_log, dt_bias, cos, sin):
    B, L, _ = p.shape
    qkv = _short_conv(p[..., :3 * DN_W], conv_w)
    q, k, v = [t.reshape(B, L, DN_HEADS, DN_DIM) for t in jnp.split(qkv, 3, axis=-1)]
    q, k = _l2norm(q), _l2norm(k)
    if cos is not None:
        q, k = _apply_rope(q, cos, sin), _apply_rope(k, cos, sin)
    q = q * DN_DIM ** -0.5
    gate = p[..., 3 * DN_W:4 * DN_W]
    off = 4 * DN_W
    beta = jax.nn.sigmoid(p[..., off:off + 2 * DN_HEADS].astype(jnp.float32)).reshape(B, L, 2, DN_HEADS)
    a = p[..., off + 2 * DN_HEADS:off + 4 * DN_HEADS].astype(jnp.float32).reshape(B, L, 2, DN_HEADS)
    logg = -jnp.exp(a_log.astype(jnp.float32)) * jax.nn.softplus(a + dt_bias.astype(jnp.float32))
    return q, k, v, beta, logg, gate


def _gated_deltanet(p_lat, p_ctx, conv_w, a_log, dt_bias, g_out, cos, sin):
    ql, kl, vl, bl, gl, gate_l = _dn_inputs(p_lat, conv_w, a_log, dt_bias, cos, sin)
    qc, kc, vc, bc, gc, gate_c = _dn_inputs(p_ctx, conv_w, a_log, dt_bias, None, None)
    B = p_lat.shape[0]
    zeros = jnp.zeros((B, DN_HEADS, DN_DIM, DN_DIM), jnp.float32)
    flip = lambda t: jnp.flip(t, axis=1)
    oc_f, sc_f = _delta_chunked(qc, kc, vc, bc[:, :, 0], gc[:, :, 0], zeros)
    ol_f, _ = _delta_chunked(ql, kl, vl, bl[:, :, 0], gl[:, :, 0], sc_f)
    oc_b, sc_b = _delta_chunked(flip(qc), flip(kc), flip(vc), flip(bc[:, :, 1]), flip(gc[:, :, 1]), zeros)
    ol_b, _ = _delta_chunked(flip(ql), flip(kl), flip(vl), flip(bl[:, :, 1]), flip(gl[:, :, 1]), sc_b)
    o_l = ol_f + flip(ol_b)
    o_c = oc_f + flip(oc_b)

    def finish(o, gate):
        Bq, Lq = o.shape[0], o.shape[1]
        y = _rmsnorm(o, g_out) * jax.nn.silu(gate.reshape(o.shape))
        return y.reshape(Bq, Lq, DN_W)

    return finish(o_l, gate_l), finish(o_c, gate_c)


def _even_mixer(h_lat, h_ctx, w_in, w_out, rpb, conv_w, a_log, dt_bias, g_out, cos, sin, ctx_out):
    p_lat = h_lat @ w_in
    p_ctx = h_ctx @ w_in
    B, L, _ = h_lat.shape
    Lc = h_ctx.shape[1]
    qa_l, ka_l, va_l = [t.reshape(B, L, NA_HEADS, NA_DIM) for t in jnp.split(p_lat[..., :3 * A_W], 3, axis=-1)]
    qa_c, ka_c, va_c = [t.reshape(B, Lc, NA_HEADS, NA_DIM) for t in jnp.split(p_ctx[..., :3 * A_W], 3, axis=-1)]
    att_l = _neighbourhood_attention(qa_l, ka_l, va_l, ka_c, va_c, rpb)
    dn_l, dn_c = _gated_deltanet(p_lat[..., 3 * A_W:], p_ctx[..., 3 * A_W:], conv_w, a_log, dt_bias, g_out, cos, sin)
    y_lat = jnp.concatenate([att_l, dn_l], axis=-1) @ w_out
    if not ctx_out:
        return y_lat, None
    att_c = _context_attention(qa_c, ka_c, va_c)
    y_ctx = jnp.concatenate([att_c, dn_c], axis=-1) @ w_out
    return y_lat, y_ctx


def _chunk_gmlp(h, w_in, g_v, ws, bs, w_out):
    z = jax.nn.gelu(h @ w_in, approximate=False)
    u, v = z[..., :GM_HALF], z[..., GM_HALF:]
    v = _rmsnorm(v, g_v)
    B, L, E = v.shape
    n = L // GM_CHUNK
    vg = v.reshape(B, n, GM_CHUNK, GM_GROUPS, E // GM_GROUPS)
    mixed = jnp.einsum('gpq,bnqgc->bnpgc', ws, vg) + bs.T[None, None, :, :, None]
    return (u * mixed.reshape(B, L, E)) @ w_out


def _sq_relu_mlp(h, w1, w2):
    a = jax.nn.relu(h @ w1)
    return (a * a) @ w2


def setup_inputs(seed: int = 0) -> dict:
    key = jax.random.key(seed)
    ks = jax.random.split(key, 24)
    f32 = jnp.float32
    D = D_MODEL

    def nrm(k, shape, s):
        return jax.random.normal(k, shape, f32) * s

    x = nrm(ks[0], (BATCH, SEQ, D), 1.0)
    c = nrm(ks[1], (BATCH, D), 1.0)
    ctx = nrm(ks[2], (BATCH, CTX_LEN, D), 1.0)
    c_ctx = nrm(ks[3], (D,), 1.0)
    w_ada = nrm(ks[4], (DEPTH, D, 6 * D), 0.5 * D ** -0.5)
    b_ada = nrm(ks[5], (DEPTH, 6 * D), 0.02)
    g_norm_mix = 1.0 + nrm(ks[6], (DEPTH, D), 0.02)
    g_norm_ffn = 1.0 + nrm(ks[7], (DEPTH, D), 0.02)
    w_in_even = nrm(ks[8], (N_EVEN, D, EVEN_IN), D ** -0.5)
    w_out_even = nrm(ks[9], (N_EVEN, A_W + DN_W, D), (A_W + DN_W) ** -0.5)
    na_rpb = nrm(ks[10], (N_EVEN, NA_HEADS, 2 * NA_WIN_R - 1, 2 * NA_WIN_C - 1), 0.1)
    dn_conv = nrm(ks[11], (N_EVEN, CONV_K, 3 * DN_W), CONV_K ** -0.5)
    dn_a_log = jnp.log(jax.random.uniform(ks[12], (N_EVEN, 2, DN_HEADS), f32, 1.0, 16.0))
    dt = jnp.exp(jax.random.uniform(ks[13], (N_EVEN, 2, DN_HEADS), f32, math.log(1e-3), math.log(1e-1)))
    dn_dt_bias = dt + jnp.log(-jnp.expm1(-dt))
    dn_g_out = 1.0 + nrm(ks[14], (N_EVEN, DN_DIM), 0.02)
    w_in_odd = nrm(ks[15], (N_ODD, D, 2 * GM_HALF), D ** -0.5)
    gm_g_v = 1.0 + nrm(ks[16], (N_ODD, GM_HALF), 0.02)
    gm_ws = nrm(ks[17], (N_ODD, GM_GROUPS, GM_CHUNK, GM_CHUNK), GM_CHUNK ** -0.5)
    gm_bs = 1.0 + nrm(ks[18], (N_ODD, GM_GROUPS, GM_CHUNK), 0.1)
    w_out_odd = nrm(ks[19], (N_ODD, GM_HALF, D), GM_HALF ** -0.5)
    w_ff1 = nrm(ks[20], (DEPTH, D, D_FF), D ** -0.5)
    w_ff2 = nrm(ks[21], (DEPTH, D_FF, D), D_FF ** -0.5)
    g_final = 1.0 + nrm(ks[22], (D,), 0.02)
    return {'x': x, 'c': c, 'ctx': ctx, 'c_ctx': c_ctx, 'w_ada': w_ada, 'b_ada': b_ada,
            'g_norm_mix': g_norm_mix, 'g_norm_ffn': g_norm_ffn, 'w_in_even': w_in_even,
            'w_out_even': w_out_even, 'na_rpb': na_rpb, 'dn_conv': dn_conv, 'dn_a_log': dn_a_log,
            'dn_dt_bias': dn_dt_bias, 'dn_g_out': dn_g_out, 'w_in_odd': w_in_odd, 'gm_g_v': gm_g_v,
            'gm_ws': gm_ws, 'gm_bs': gm_bs, 'w_out_odd': w_out_odd, 'w_ff1': w_ff1, 'w_ff2': w_ff2,
            'g_final': g_final}


def reference(x, c, ctx, c_ctx, w_ada, b_ada, g_norm_mix, g_norm_ffn, w_in_even, w_out_even, na_rpb,
              dn_conv, dn_a_log, dn_dt_bias, dn_g_out, w_in_odd, gm_g_v, gm_ws, gm_bs, w_out_odd,
              w_ff1, w_ff2, g_final):
    seq_len = x.shape[1]
    cos, sin = _axial_rope(seq_len)
    x_lat, x_ctx = x, ctx
    cc = c_ctx[None, :]
    for l in range(DEPTH):
        ctx_live = any(j % 2 == 0 for j in range(l + 1, DEPTH))
        sh1, sc1, gt1, sh2, sc2, gt2 = _adaln(c, w_ada[l], b_ada[l])
        h_lat = _rmsnorm(x_lat, g_norm_mix[l]) * (1 + sc1) + sh1
        if l % 2 == 0 or ctx_live:
            csh1, csc1, cgt1, csh2, csc2, cgt2 = _adaln(cc, w_ada[l], b_ada[l])
            h_ctx = _rmsnorm(x_ctx, g_norm_mix[l]) * (1 + csc1) + csh1
        if l % 2 == 0:
            e = l // 2
            y_lat, y_ctx = _even_mixer(h_lat, h_ctx, w_in_even[e], w_out_even[e], na_rpb[e], dn_conv[e],
                                       dn_a_log[e], dn_dt_bias[e], dn_g_out[e], cos, sin, ctx_live)
        else:
            o = l // 2
            y_lat = _chunk_gmlp(h_lat, w_in_odd[o], gm_g_v[o], gm_ws[o], gm_bs[o], w_out_odd[o])
            if ctx_live:
                y_ctx = _chunk_gmlp(h_ctx, w_in_odd[o], gm_g_v[o], gm_ws[o], gm_bs[o], w_out_odd[o])
        x_lat = x_lat + gt1 * y_lat
        x_lat = x_lat + gt2 * _sq_relu_mlp(_rmsnorm(x_lat, g_norm_ffn[l]) * (1 + sc2) + sh2, w_ff1[l], w_ff2[l])
        if ctx_live:
            x_ctx = x_ctx + cgt1 * y_ctx
            x_ctx = x_ctx + cgt2 * _sq_relu_mlp(_rmsnorm(x_ctx, g_norm_ffn[l]) * (1 + csc2) + csh2, w_ff1[l], w_ff2[l])
    return _rmsnorm(x_lat, g_final)
```

```python
import numpy as np
from contextlib import ExitStack
import concourse.bass as bass
import concourse.mybir as mybir
from concourse.bass_utils import run_bass_kernel_spmd

F32 = mybir.dt.float32
BF16 = mybir.dt.bfloat16
AF = mybir.ActivationFunctionType
ALU = mybir.AluOpType

D = 1024
L = 4096
LC = 256
NT = (L + LC) // 128
NTOK = L + LC
EPS = 1e-6
DEPTH = 4
EVEN_IN = 3600
NPAD = NTOK + 4
NEG = -30000.0


class Buf:
    __slots__ = ("name", "w", "r", "excl")

    def __init__(self, name="", excl=False):
        self.name = name
        self.w = None
        self.r = []
        self.excl = excl


class Prog:
    ENGS = ("pe", "act", "dve", "pool", "sp")

    def __init__(self, nc):
        self.nc = nc
        self.ops = []
        self.stream_cnt = {}
        self.stream_last = {}
        self.stream_R = {"ld": 8, "st": 8, "wl": 4}
        self.last_real = {}

    def sem_names(self):
        names = ["e_" + e for e in self.ENGS]
        for st, R in self.stream_R.items():
            names += ["s_%s%d" % (st, j) for j in range(R)]
        return names

    def add(self, eng, fn, reads=(), writes=(), stream=None, extra=()):
        i = len(self.ops)
        deps = set(extra)
        xr = [b for b in reads if b.excl and eng != "pe"]
        if xr:
            reads = [b for b in reads if not (b.excl and eng != "pe")]
            writes = list(writes) + xr
        for b in reads:
            if b.w is not None:
                deps.add(b.w)
        for b in writes:
            if b.w is not None:
                deps.add(b.w)
            deps.update(b.r)
        for b in reads:
            b.r.append(i)
        for b in writes:
            b.w = i
            b.r = []
        val = None
        if stream is not None:
            n = self.stream_cnt.get(stream, 0)
            self.stream_cnt[stream] = n + 1
            R = self.stream_R[stream]
            stream = "%s%d" % (stream, n % R)
            val = 16 * (n // R + 1)
            prev = self.stream_last.get(stream)
            if prev is not None:
                deps.add(prev)
            self.stream_last[stream] = i
        elif fn is not None:
            self.last_real[eng] = i
        self.ops.append([eng, fn, deps, stream, False, val])
        return i

    def barrier(self):
        ex = set(self.last_real.values()) | set(self.stream_last.values())
        for e in self.ENGS:
            self.add(e, None, extra=ex)

    def dma(self, q, out, in_, R=(), W=(), stream=None, **kw):
        if stream is None:
            stream = "st" if q == "pool" else "ld"
        return self.add(q, lambda e: e.dma_start(out=out, in_=in_, **kw), R, W, stream=stream)

    def mm(self, out, lhsT, rhs, start, stop, R, W):
        return self.add("pe", lambda e: e.matmul(out, lhsT=lhsT, rhs=rhs, start=start, stop=stop), R, W)

    def tr(self, out, in_, ident, R, W):
        return self.add("pe", lambda e: e.transpose(out=out, in_=in_, identity=ident), R, W)

    def act(self, out, in_, func, R, W, **kw):
        return self.add("act", lambda e: e.activation(out=out, in_=in_, func=func, **kw), R, W)

    def ts(self, eng, out, in0, s1, s2, op0, op1, R, W):
        if s2 is None:
            return self.add(eng, lambda e: e.tensor_scalar(out=out, in0=in0, scalar1=s1, scalar2=None, op0=op0), R, W)
        return self.add(eng, lambda e: e.tensor_scalar(out=out, in0=in0, scalar1=s1, scalar2=s2, op0=op0, op1=op1), R, W)

    def tt(self, eng, out, in0, in1, op, R, W):
        return self.add(eng, lambda e: e.tensor_tensor(out=out, in0=in0, in1=in1, op=op), R, W)

    def stt(self, out, in0, scalar, in1, op0, op1, R, W):
        return self.add("dve", lambda e: e.scalar_tensor_tensor(out=out, in0=in0, scalar=scalar, in1=in1, op0=op0, op1=op1), R, W)

    def copy(self, eng, out, in_, R, W):
        if eng == "act":
            return self.add("act", lambda e: e.activation(out=out, in_=in_, func=AF.Copy), R, W)
        return self.add(eng, lambda e: e.tensor_copy(out=out, in_=in_), R, W)

    def memset(self, eng, out, val, W):
        return self.add(eng, lambda e: e.memset(out, val), (), W)

    def recip(self, out, in_, R, W):
        return self.add("dve", lambda e: e.reciprocal(out=out, in_=in_), R, W)

    def emit(self, sems):
        ops = self.ops
        for (eng, fn, deps, stream, sig, val) in ops:
            for d in deps:
                po = ops[d]
                if po[3] is None:
                    if po[0] == "pe" and eng == "pe":
                        continue
                    po[4] = True
        cnt = {e: 0 for e in self.ENGS}
        for o in ops:
            if o[3] is None and o[4]:
                cnt[o[0]] += 1
                o[5] = cnt[o[0]]
        per_eng = {e: [] for e in self.ENGS}
        waited = {e: {} for e in self.ENGS}
        for (eng, fn, deps, stream, sig, val) in ops:
            need = {}
            for d in deps:
                po = ops[d]
                if po[3] is None:
                    if po[0] == "pe" and eng == "pe":
                        continue
                    if po[1] is None:
                        continue
                    key = "e_" + po[0]
                else:
                    key = "s_" + po[3]
                need[key] = max(need.get(key, 0), po[5])
            waits = []
            for key, v in need.items():
                if waited[eng].get(key, 0) >= v:
                    continue
                waited[eng][key] = v
                waits.append((key, v))
            per_eng[eng].append((waits, fn, stream, sig))
        self.n_inst = {e: len(v) for e, v in per_eng.items()}

        def run(engobj, lst, ename):
            for (waits, fn, stream, sig) in lst:
                for (key, v) in waits:
                    engobj.wait_ge(sems[key], v)
                if fn is None:
                    continue
                ins = fn(engobj)
                if stream is not None:
                    ins.then_inc(sems["s_" + stream], 16)
                elif sig:
                    ins.then_inc(sems["e_" + ename], 1)

        with self.nc.Block() as block:
            @block.tensor
            def _(e):
                run(e, per_eng["pe"], "pe")

            @block.scalar
            def _(e):
                run(e, per_eng["act"], "act")

            @block.vector
            def _(e):
                run(e, per_eng["dve"], "dve")

            @block.gpsimd
            def _(e):
                run(e, per_eng["pool"], "pool")

            @block.sync
            def _(e):
                run(e, per_eng["sp"], "sp")


def host_consts():
    c = {}
    c["ident"] = np.eye(128, dtype=np.float32)
    c["ones"] = np.ones((128, 128), np.float32)
    idx = np.arange(128)
    c["tri"] = np.stack([(idx[:, None] <= idx[None, :]), (idx[:, None] >= idx[None, :]),
                         (idx[:, None] > idx[None, :]), (idx[:, None] < idx[None, :])]).astype(np.float32)
    rm = np.zeros((128, 128), np.float32)
    for j in range(32):
        rm[32 + j, j] = -1.0
        rm[j, 32 + j] = 1.0
        rm[96 + j, 64 + j] = -1.0
        rm[64 + j, 96 + j] = 1.0
    c["rotm"] = rm
    t = np.arange(L)
    row = (t // 64).astype(np.float32)
    col = (t % 64).astype(np.float32)
    inv = (10000.0 ** (-np.arange(32, dtype=np.float32) / 32)).astype(np.float32)
    ar = row[:, None] * inv
    ac = col[:, None] * inv
    ang = np.concatenate([ar, ar, ac, ac], axis=-1)
    c["cosT"] = np.ascontiguousarray(np.cos(ang).T.astype(np.float32))
    c["sinT"] = np.ascontiguousarray(np.sin(ang).T.astype(np.float32))
    cc = np.arange(64)
    cstart = np.clip(cc - 8, 0, 48)
    ok = (cc[:, None] >= cstart[None, :]) & (cc[:, None] < cstart[None, :] + 16)
    c["namask"] = np.where(ok, 0.0, NEG).astype(np.float32)
    return c


def rpb_table(na_rpb):
    cc = np.arange(64)
    dc = np.clip(cc[:, None] - cc[None, :], -15, 15) + 15
    return np.ascontiguousarray(na_rpb[:, :, :, dc]).astype(np.float32)


class Ctx:
    pass


def build(cfg=None):
    cfg = cfg or {}
    layers = cfg.get("layers", list(range(DEPTH)))
    dbg = cfg.get("debug", ())
    nc = bass.Bass("TRN2", target_bir_lowering=False)
    g = Ctx()
    g.nc = nc
    g.cfg = cfg

    def din(name, shape, dt=F32):
        return nc.dram_tensor(name, list(shape), dt, kind="ExternalInput").ap()

    def dscr(name, shape, dt=F32):
        kind = "ExternalOutput" if name in dbg else "Internal"
        return nc.dram_tensor(name, list(shape), dt, kind=kind).ap()

    hc = host_consts()
    shapes = {"x": [L, D], "ctx": [LC, D], "cvec": [2, D], "w_ada": [4, D, 6 * D], "b_ada": [4, 6 * D],
              "g_norm_mix": [4, D], "g_norm_ffn": [4, D], "w_in_even": [2, D, EVEN_IN], "w_out_even": [2, D, D],
              "rpb_tab": [2, 8, 15, 64, 64], "dn_conv": [2, 3, 1536], "dn_a_log": [2, 8], "dn_dt_bias": [2, 8],
              "dn_g_out": [2, 128], "w_in_odd": [2, D, 6144], "gm_g_v": [2, 3072], "gm_wsT": [2, 128, 8, 128],
              "gm_bsT": [2, 128, 8], "w_out_odd": [2, 3072, D], "w_ff1": [4, D, 4096], "w_ff2": [4, 4096, D],
              "g_final": [D]}
    for k, v in hc.items():
        shapes[k] = list(v.shape)

    class LazyIn(dict):
        def __missing__(self, k):
            self[k] = din(k, shapes[k])
            return self[k]
    I = g.I = LazyIn()
    if not cfg.get("lazy"):
        for k in shapes:
            I[k]
    g.out = nc.dram_tensor("out", [L, D], F32, kind="ExternalOutput").ap()

    S = g.S = {}
    S["X"] = dscr("X", [NTOK, D])
    S["MODV"] = dscr("MODV", [4, 2, 6, D])
    S["QK"] = dscr("QK", [8, 128, NTOK], BF16)
    S["VA"] = dscr("VA", [NTOK, 520], BF16)
    S["DNRAW"] = dscr("DNRAW", [12, 128, NPAD])
    S["GATE"] = dscr("GATE", [NTOK, 512])
    S["BA"] = dscr("BA", [NTOK, 16])
    S["DNP"] = dscr("DNP", [NT, 8, 128, 648])
    S["OF"] = dscr("OF", [NTOK, 512])
    S["ZT"] = dscr("ZT", [8, 128, NTOK], BF16)
    S["MIX"] = dscr("MIX", [NTOK, 3072])

    p = g.p = Prog(nc)
    with ExitStack() as es:
        sems = {n: es.enter_context(nc.semaphore(n)) for n in p.sem_names()}
        NW = 51968
        g.big = es.enter_context(nc.sbuf_tensor("big", [128, NW], F32))
        g.NW = NW
        g.ps = [es.enter_context(nc.psum_tensor("ps%d" % i, [128, 512], F32)) for i in range(8)]
        g.persist = 0
        g.off = 0
        g.DB = {k: Buf(k) for k in list(S.keys()) + ["out"]}
        phase_setup(g)
        for l in layers:
            if "nomix" in cfg:
                pass
            elif l % 2 == 0:
                phase_even(g, l)
            else:
                phase_odd(g, l)
            if "noffn" not in cfg:
                phase_ffn(g, l)
        if "nofinal" not in cfg:
            phase_final(g)
        p.add("sp", None, extra=set(p.stream_last.values()) | set(p.last_real.values()))
        p.emit(sems)
    g.n_inst = p.n_inst
    return nc, g


def alloc(g, free_shape, dt=F32, persist=False):
    n = int(np.prod(free_shape))
    words = n if dt == F32 else (n + 1) // 2
    words = (words + 7) // 8 * 8
    a = g.big[:, g.off:g.off + words]
    g.off += words
    assert g.off <= g.NW, "SBUF overflow %d" % g.off
    if persist:
        g.persist = g.off
    if dt != F32:
        a = a.bitcast(dt)
    a = a[:, 0:n]
    if len(free_shape) == 2:
        a = a.rearrange("p (a b) -> p a b", a=free_shape[0])
    elif len(free_shape) == 3:
        a = a.rearrange("p (a b c) -> p a b c", a=free_shape[0], b=free_shape[1])
    return a


def new_phase(g):
    g.p.barrier()
    g.off = g.persist
    g.PB = [Buf("ps%d" % i, excl=True) for i in range(8)]
    g.bank_i = 0


def bank(g):
    i = g.bank_i % 8
    g.bank_i += 1
    return g.ps[i], g.PB[i]


def phase_setup(g):
    p, I, S = g.p, g.I, g.S
    g.PB = [Buf("ps%d" % i, excl=True) for i in range(8)]
    g.bank_i = 0
    g.identF = alloc(g, [128], F32, persist=True)
    g.identB = alloc(g, [128], BF16, persist=True)
    g.onesF = alloc(g, [128], F32, persist=True)
    g.Bconst = Buf("const")
    p.dma("sp", g.identF, I["ident"][:, :], W=[g.Bconst])
    p.dma("sp", g.onesF, I["ones"][:, :], W=[g.Bconst])
    p.copy("dve", g.identB, g.identF, [g.Bconst], [g.Bconst])
    p.dma("sp", S["X"][0:LC, :], I["ctx"][:, :], W=[g.DB["X"]])
    p.dma("sp", S["X"][LC:NTOK, :], I["x"][:, :], W=[g.DB["X"]])
    z = alloc(g, [12, 4], F32)
    Bz = Buf()
    p.memset("dve", z, 0.0, [Bz])
    for col in (0, 257, 258, NPAD - 1):
        p.dma("sp", S["DNRAW"][:, :, col:col + 1].rearrange("c p t -> p c t"), z[:, :, 0:1], R=[Bz], W=[g.DB["DNRAW"]],
              allow_slow_non_contiguous=True)
    if g.cfg.get("ntiles"):
        zz = alloc(g, [12, 512], F32)
        Bzz = Buf()
        p.memset("dve", zz, 0.0, [Bzz])
        for c0 in range(0, NPAD, 512):
            c1 = min(NPAD, c0 + 512)
            p.dma("sp", S["DNRAW"][:, :, c0:c1].rearrange("c p t -> p c t"), zz[:, :, 0:c1 - c0], R=[Bzz], W=[g.DB["DNRAW"]])
        zb = alloc(g, [8, 520], BF16)
        p.memset("dve", zb, 0.0, [Bzz])
        for c0 in range(0, NTOK, 512):
            c1 = min(NTOK, c0 + 512)
            p.dma("sp", S["QK"][:, :, c0:c1].rearrange("c p t -> p c t"), zb[:, :, 0:c1 - c0], R=[Bzz], W=[g.DB["QK"]])
        for t_ in range(NT):
            p.dma("sp", S["VA"][t_ * 128:(t_ + 1) * 128, :], zb[:, 0, :], R=[Bzz], W=[g.DB["VA"]])
    cf = alloc(g, [8, 2], F32)
    Bcf = Buf()
    for s_ in range(2):
        p.dma("sp", cf[:, :, s_], I["cvec"][s_, :].rearrange("(k p) -> p k", p=128), W=[Bcf], allow_slow_non_contiguous=True)
    p.act(cf, cf, AF.Silu, [Bcf], [Bcf])
    wb = [alloc(g, [8, 512], F32) for _ in range(3)]
    Bw = [Buf() for _ in range(3)]
    mrow = alloc(g, [6 * D], F32)
    brow = alloc(g, [6 * D], F32)
    grow = alloc(g, [2, D], F32)
    tmp = alloc(g, [D], F32)
    Bm, Bb, Bg, Bt = Buf(), Buf(), Buf(), Buf()
    it = 0
    for l in ([] if g.cfg.get("nomod") else g.cfg.get("layers", list(range(DEPTH)))):
        p.dma("sp", brow[0:2, :], I["b_ada"][l, :].partition_broadcast(2), W=[Bb])
        p.dma("sp", grow[0:2, 0, :], I["g_norm_mix"][l, :].partition_broadcast(2), W=[Bg])
        p.dma("sp", grow[0:2, 1, :], I["g_norm_ffn"][l, :].partition_broadcast(2), W=[Bg])
        for n in range(12):
            w_, bw_ = wb[it % 3], Bw[it % 3]
            it += 1
            p.dma("sp", w_, I["w_ada"][l, :, n * 512:(n + 1) * 512].rearrange("(k p) n -> p k n", p=128), W=[bw_])
            ps, pb = bank(g)
            for k in range(8):
                p.mm(ps[0:2, :], cf[:, k, :], w_[:, k, :], k == 0, k == 7, [Bcf, bw_], [pb])
            p.tt("dve", mrow[0:2, n * 512:(n + 1) * 512], ps[0:2, :], brow[0:2, n * 512:(n + 1) * 512], ALU.add, [pb, Bb], [Bm])
        for j, (sc_i, sh_i, gt_i) in enumerate([(1, 0, 2), (4, 3, 5)]):
            p.stt(tmp[0:2, :], mrow[0:2, sc_i * D:(sc_i + 1) * D], 1.0, grow[0:2, j, :], ALU.add, ALU.mult, [Bm, Bg], [Bt])
            p.dma("sp", S["MODV"][l, :, 3 * j + 0, :], tmp[0:2, :], R=[Bt], W=[g.DB["MODV"]])
            p.dma("sp", S["MODV"][l, :, 3 * j + 1, :], mrow[0:2, sh_i * D:(sh_i + 1) * D], R=[Bm], W=[g.DB["MODV"]])
            p.dma("sp", S["MODV"][l, :, 3 * j + 2, :], mrow[0:2, gt_i * D:(gt_i + 1) * D], R=[Bm], W=[g.DB["MODV"]])


def load_mod(g, l, stream, which):
    p, S = g.p, g.S
    A = alloc(g, [8], F32)
    sh = alloc(g, [8], F32)
    B = Buf()
    p.dma("sp", A, S["MODV"][l, stream, 3 * which + 0, :].rearrange("(k p) -> p k", p=128), R=[g.DB["MODV"]], W=[B],
          allow_slow_non_contiguous=True)
    p.dma("sp", sh, S["MODV"][l, stream, 3 * which + 1, :].rearrange("(k p) -> p k", p=128), R=[g.DB["MODV"]], W=[B],
          allow_slow_non_contiguous=True)
    return A, sh, B


def load_gate(g, l, stream, which):
    p, S = g.p, g.S
    gt = alloc(g, [D], F32)
    B = Buf()
    p.dma("sp", gt, S["MODV"][l, stream, 3 * which + 2, :].partition_broadcast(128), R=[g.DB["MODV"]], W=[B])
    return gt, B


class NormBufs:
    def __init__(self, g, nbuf=2, keep_x=True):
        self.n = nbuf
        self.x = [alloc(g, [D], F32) for _ in range(nbuf)]
        self.Bx = [Buf() for _ in range(nbuf)]
        self.xn = [alloc(g, [D], BF16) for _ in range(nbuf)]
        self.Bxn = [Buf() for _ in range(nbuf)]
        self.hT = [alloc(g, [8, 128], BF16) for _ in range(nbuf)]
        self.BhT = [Buf() for _ in range(nbuf)]
        self.st = [alloc(g, [4], F32) for _ in range(nbuf)]
        self.Bst = [Buf() for _ in range(nbuf)]
        self.junk = alloc(g, [D], BF16)
        self.Bjunk = Buf()


def norm_T(g, nb, i, t, mods, src=None):
    p, S = g.p, g.S
    x, Bx, xn, Bxn, hT, BhT, st, Bst = nb.x[i], nb.Bx[i], nb.xn[i], nb.Bxn[i], nb.hT[i], nb.BhT[i], nb.st[i], nb.Bst[i]
    A, sh, Bmod = mods
    p.dma("sp", x, S["X"][t * 128:(t + 1) * 128, :], R=[g.DB["X"]], W=[Bx])
    p.act(nb.junk, x, AF.Square, [Bx], [nb.Bjunk, Bst], accum_out=st[:, 0:1])
    p.act(st[:, 1:2], st[:, 0:1], AF.Sqrt, [Bst], [Bst], scale=1.0 / D, bias=EPS)
    p.recip(st[:, 2:3], st[:, 1:2], [Bst], [Bst])
    p.ts("pool", xn, x, st[:, 2:3], None, ALU.mult, None, [Bx, Bst], [Bxn])
    ps, pb = bank(g)
    psb = ps[:, :].bitcast(BF16)
    for k in range(8):
        p.tr(psb[:, k * 128:(k + 1) * 128], xn[:, k * 128:(k + 1) * 128], g.identB, [Bxn, g.Bconst], [pb])
    for k in range(8):
        if k % 2 == 0:
            p.act(hT[:, k, :], psb[:, k * 128:(k + 1) * 128], AF.Identity, [pb, Bmod], [BhT], scale=A[:, k:k + 1], bias=sh[:, k:k + 1])
        else:
            p.ts("dve", hT[:, k, :], psb[:, k * 128:(k + 1) * 128], A[:, k:k + 1], sh[:, k:k + 1], ALU.mult, ALU.add, [pb, Bmod], [BhT])
    return hT, BhT


def load_w_bf16(g, dst, src_ap, Bw, max_cols=2048):
    p = g.p
    K, N = dst.shape[1], dst.shape[2]
    step = max_cols
    for k in range(K):
        for n0 in range(0, N, step):
            n1 = min(N, n0 + step)
            p.dma("pool", dst[:, k, n0:n1], src_ap[:, k, n0:n1], W=[Bw], stream="wl")


def phase_ffn(g, l):
    p, I, S = g.p, g.I, g.S
    new_phase(g)
    tiles = tile_list(g, l, "ffn")
    W1 = alloc(g, [8, 4096], BF16)
    W2 = alloc(g, [32, 1024], BF16)
    BW1, BW2 = Buf(), Buf()
    load_w_bf16(g, W1, I["w_ff1"][l].rearrange("(k p) n -> p k n", p=128), BW1)
    load_w_bf16(g, W2, I["w_ff2"][l].rearrange("(k p) n -> p k n", p=128), BW2)
    mods = [load_mod(g, l, s_, 1) for s_ in range(2)]
    gates = [load_gate(g, l, s_, 1) for s_ in range(2)]
    nb = NormBufs(g)
    r_ = [alloc(g, [512], F32) for _ in range(2)]
    Br = [Buf() for _ in range(2)]
    aT = [alloc(g, [32, 128], BF16) for _ in range(2)]
    BaT = [Buf() for _ in range(2)]
    tmp = [alloc(g, [512], F32) for _ in range(2)]
    Btmp = [Buf() for _ in range(2)]
    ri = 0
    for it, t in enumerate(tiles):
        i = it % 2
        s_ = 1 if t < 2 else 0
        hT, BhT = norm_T(g, nb, i, t, mods[s_])
        for mb in range(8):
            ps, pb = bank(g)
            for mm_ in range(4):
                m = mb * 4 + mm_
                for k in range(8):
                    p.mm(ps[:, mm_ * 128:(mm_ + 1) * 128], W1[:, k, m * 128:(m + 1) * 128], hT[:, k, :], k == 0, k == 7, [BW1, BhT], [pb])
            rr, brr = r_[ri % 2], Br[ri % 2]
            ri += 1
            p.act(rr, ps[:, :], AF.Relu, [pb], [brr])
            p.tt("pool", aT[i][:, mb * 4:(mb + 1) * 4, :], rr.rearrange("p (a b) -> p a b", a=4), rr.rearrange("p (a b) -> p a b", a=4), ALU.mult, [brr], [BaT[i]])
        gt, Bgt = gates[s_]
        for n in range(2):
            ps, pb = bank(g)
            for k in range(32):
                p.mm(ps[:, :], aT[i][:, k, :], W2[:, k, n * 512:(n + 1) * 512], k == 0, k == 31, [BaT[i], BW2], [pb])
            p.tt("dve", tmp[n], ps[:, :], gt[:, n * 512:(n + 1) * 512], ALU.mult, [pb, Bgt], [Btmp[n]])
            p.tt("pool", nb.x[i][:, n * 512:(n + 1) * 512], tmp[n], nb.x[i][:, n * 512:(n + 1) * 512], ALU.add, [Btmp[n], nb.Bx[i]], [nb.Bx[i]])
        p.dma("pool", S["X"][t * 128:(t + 1) * 128, :], nb.x[i], R=[nb.Bx[i]], W=[g.DB["X"]])


def tile_list(g, l, kind):
    lim = g.cfg.get("ntiles")
    lat = list(range(2, NT))
    if lim:
        lat = lat[:lim]
    if kind in ("ffn", "mixout"):
        ctx = [0, 1] if l < 2 else []
    elif kind == "proj":
        ctx = [0, 1] if l < 3 else []
    else:
        ctx = []
    return ctx + lat


def phase_final(g):
    p, I, S = g.p, g.I, g.S
    new_phase(g)
    gf = alloc(g, [D], F32)
    Bg = Buf()
    p.dma("sp", gf, I["g_final"].partition_broadcast(128), W=[Bg])
    x = [alloc(g, [D], F32) for _ in range(2)]
    Bx = [Buf() for _ in range(2)]
    y = [alloc(g, [D], F32) for _ in range(2)]
    By = [Buf() for _ in range(2)]
    st = [alloc(g, [4], F32) for _ in range(2)]
    Bst = [Buf() for _ in range(2)]
    junk = alloc(g, [D], BF16)
    Bj = Buf()
    for it, t in enumerate(tile_list(g, 3, "lat")):
        i = it % 2
        p.dma("sp", x[i], S["X"][t * 128:(t + 1) * 128, :], R=[g.DB["X"]], W=[Bx[i]])
        p.act(junk, x[i], AF.Square, [Bx[i]], [Bj, Bst[i]], accum_out=st[i][:, 0:1])
        p.act(st[i][:, 1:2], st[i][:, 0:1], AF.Sqrt, [Bst[i]], [Bst[i]], scale=1.0 / D, bias=EPS)
        p.recip(st[i][:, 2:3], st[i][:, 1:2], [Bst[i]], [Bst[i]])
        p.stt(y[i], x[i], st[i][:, 2:3], gf, ALU.mult, ALU.mult, [Bx[i], Bst[i], Bg], [By[i]])
        p.dma("pool", g.out[(t - 2) * 128:(t - 1) * 128, :], y[i], R=[By[i]], W=[g.DB["out"]])


def phase_even(g, l):
    e = l // 2
    ctx_out = (l == 0)
    st = g.cfg.get("even_stages", "ABCND")
    if "A" in st:
        even_proj(g, l, e)
    if "B" in st:
        even_dnprep(g, l, e)
    if "C" in st:
        even_dnscan(g, l, e, ctx_out)
    if "N" in st:
        even_na(g, l, e, ctx_out)
    if "D" in st:
        even_out(g, l, e)


def dn_col0(t):
    return 1 + t * 128 if t < 2 else 259 + (t - 2) * 128


def even_proj(g, l, e):
    p, I, S = g.p, g.I, g.S
    new_phase(g)
    tiles = tile_list(g, l, "proj")
    W = alloc(g, [8, EVEN_IN], BF16)
    BW = Buf()
    load_w_bf16(g, W, I["w_in_even"][e].rearrange("(k p) n -> p k n", p=128), BW, max_cols=1800)
    mods = [load_mod(g, l, s_, 0) for s_ in range(2)]
    nb = NormBufs(g)
    qk_sb = [alloc(g, [8, 128], BF16) for _ in range(2)]
    va_sb = [alloc(g, [8, 65], BF16) for _ in range(2)]
    dn_sb = [alloc(g, [12, 128], F32) for _ in range(2)]
    gt_sb = [alloc(g, [512], F32) for _ in range(2)]
    ba_sb = [alloc(g, [16], F32) for _ in range(2)]
    Bqk, Bva, Bdn, Bgt, Bba = [[Buf() for _ in range(2)] for _ in range(5)]
    for i in range(2):
        p.memset("pool", va_sb[i][:, :, 64:65], 1.0, [Bva[i]])
    for it, t in enumerate(tiles):
        i = it % 2
        s_ = 1 if t < 2 else 0
        hT, BhT = norm_T(g, nb, i, t, mods[s_])
        def fm_bank(col0, nch):
            ps, pb = bank(g)
            for cc in range(nch):
                for k in range(8):
                    p.mm(ps[:, cc * 128:(cc + 1) * 128], W[:, k, col0 + cc * 128: col0 + (cc + 1) * 128], hT[:, k, :], k == 0, k == 7, [BW, BhT], [pb])
            return ps, pb
        ps, pb = fm_bank(0, 4)
        p.act(qk_sb[i][:, 0:4, :], ps[:, :].rearrange("p (a b) -> p a b", a=4), AF.Copy, [pb], [Bqk[i]], scale=0.125)
        ps, pb = fm_bank(512, 4)
        p.copy("dve", qk_sb[i][:, 4:8, :], ps[:, :].rearrange("p (a b) -> p a b", a=4), [pb], [Bqk[i]])
        p.dma("pool", S["QK"][:, :, t * 128:(t + 1) * 128].rearrange("c p t -> p c t"), qk_sb[i], R=[Bqk[i]], W=[g.DB["QK"]])
        ps, pb = bank(g)
        for k in range(8):
            p.mm(ps[:, :], hT[:, k, :], W[:, k, 1024:1536], k == 0, k == 7, [BhT, BW], [pb])
        p.copy("dve", va_sb[i][:, :, 0:64], ps[:, :].rearrange("p (a b) -> p a b", a=8), [pb], [Bva[i]])
        p.dma("pool", S["VA"][t * 128:(t + 1) * 128, :], va_sb[i].rearrange("p a b -> p (a b)"), R=[Bva[i]], W=[g.DB["VA"]])
        for q in range(3):
            ps, pb = fm_bank(1536 + q * 512, 4)
            dst = dn_sb[i][:, q * 4:(q + 1) * 4, :]
            if q == 1:
                p.copy("dve", dst, ps[:, :].rearrange("p (a b) -> p a b", a=4), [pb], [Bdn[i]])
            else:
                p.copy("act", dst, ps[:, :].rearrange("p (a b) -> p a b", a=4), [pb], [Bdn[i]])
        c0 = dn_col0(t)
        p.dma("pool", S["DNRAW"][:, :, c0:c0 + 128].rearrange("c p t -> p c t"), dn_sb[i], R=[Bdn[i]], W=[g.DB["DNRAW"]])
        ps, pb = bank(g)
        for k in range(8):
            p.mm(ps[:, :], hT[:, k, :], W[:, k, 3072:3584], k == 0, k == 7, [BhT, BW], [pb])
        p.copy("act", gt_sb[i], ps[:, :], [pb], [Bgt[i]])
        p.dma("pool", S["GATE"][t * 128:(t + 1) * 128, :], gt_sb[i], R=[Bgt[i]], W=[g.DB["GATE"]])
        ps, pb = bank(g)
        for k in range(8):
            p.mm(ps[:, 0:16], hT[:, k, :], W[:, k, 3584:3600], k == 0, k == 7, [BhT, BW], [pb])
        p.copy("dve", ba_sb[i], ps[:, 0:16], [pb], [Bba[i]])
        p.dma("pool", S["BA"][t * 128:(t + 1) * 128, :], ba_sb[i], R=[Bba[i]], W=[g.DB["BA"]])


def even_dnprep(g, l, e):
    p, I, S = g.p, g.I, g.S
    new_phase(g)
    tiles = tile_list(g, l, "proj")
    Bc = Buf()
    cw = alloc(g, [3, 12], F32)
    cwr = alloc(g, [128], F32)
    Bcw = Buf()
    p.dma("sp", cwr[0:36, :], I["dn_conv"][e].rearrange("j (c p) -> (j c) p", p=128), W=[Bcw])
    ps, pb = bank(g)
    p.tr(ps[:, 0:36], cwr[0:36, :], g.identF[0:36, 0:36], [Bcw, g.Bconst], [pb])
    p.copy("dve", cw.rearrange("p a b -> p (a b)"), ps[:, 0:36], [pb], [Bc])
    tri = alloc(g, [4, 128], F32)
    p.dma("sp", tri, I["tri"].rearrange("m a b -> a m b"), W=[Bc])
    LE, GE, GT, LT = [tri[:, m_, :] for m_ in range(4)]
    rotm = alloc(g, [128], F32)
    p.dma("sp", rotm, I["rotm"][:, :], W=[Bc])
    dtb = alloc(g, [8], F32)
    nexpA = alloc(g, [8], F32)
    p.dma("sp", dtb, I["dn_dt_bias"][e, :].partition_broadcast(128), W=[Bc])
    p.dma("sp", nexpA, I["dn_a_log"][e, :].partition_broadcast(128), W=[Bc])
    p.act(nexpA, nexpA, AF.Exp, [Bc], [Bc])
    p.ts("pool", nexpA, nexpA, -1.0, None, ALU.mult, None, [Bc], [Bc])
    raw = [alloc(g, [12, 130], F32) for _ in range(2)]
    ba = [alloc(g, [16], F32) for _ in range(2)]
    cs = [alloc(g, [2, 128], F32) for _ in range(2)]
    Braw, Bba, Bcs = [[Buf() for _ in range(2)] for _ in range(3)]
    pk = [alloc(g, [8, 648], F32) for _ in range(2)]
    Bpk = [Buf() for _ in range(2)]
    cv = alloc(g, [12, 128], F32)
    tmpc = alloc(g, [128], F32)
    sqb = alloc(g, [1024], F32)
    rn = alloc(g, [1024], F32)
    qkr = alloc(g, [8, 128], F32)
    t1 = alloc(g, [8, 128], F32)
    ktok = alloc(g, [4, 128], F32)
    vtok = alloc(g, [4, 128], F32)
    sm = alloc(g, [12, 8], F32)
    lrep = alloc(g, [8, 128], F32)
    egB = alloc(g, [8, 128], F32)
    mmx = alloc(g, [8, 128], F32)
    dec = alloc(g, [8, 128], F32)
    decI = alloc(g, [8, 128], F32)
    decS = alloc(g, [8, 128], F32)
    aqk = alloc(g, [8, 128], F32)
    bv = alloc(g, [8, 128], F32)
    bek = alloc(g, [8, 128], F32)
    Pb = [alloc(g, [8, 128], F32) for _ in range(2)]
    Qb = [alloc(g, [8, 128], F32) for _ in range(2)]
    Nb = [alloc(g, [8, 128], F32) for _ in range(2)]
    Bcv, Btc, Bsq, Brn, Bqkr, Bt1, Bkt, Bvt, Bsm, Blr, BeB, Bmx, Bdec, BdI, BdS, Baqk, Bbv, Bbek = [Buf() for _ in range(18)]
    BP = [Buf() for _ in range(2)]
    BQ = [Buf() for _ in range(2)]
    BN = [Buf() for _ in range(2)]
    v4 = lambda ps: ps[:, :].rearrange("p (a b) -> p a b", a=4)
    for it, t in enumerate(tiles):
        i = it % 2
        lat = t >= 2
        c0 = dn_col0(t) - 1
        p.dma("sp", raw[i], S["DNRAW"][:, :, c0:c0 + 130].rearrange("c p t -> p c t"), R=[g.DB["DNRAW"]], W=[Braw[i]])
        p.dma("sp", ba[i], S["BA"][t * 128:(t + 1) * 128, :], R=[g.DB["BA"]], W=[Bba[i]])
        if lat:
            p.dma("sp", cs[i][:, 0, :], I["cosT"][:, (t - 2) * 128:(t - 1) * 128], W=[Bcs[i]])
            p.dma("sp", cs[i][:, 1, :], I["sinT"][:, (t - 2) * 128:(t - 1) * 128], W=[Bcs[i]])
        for c in range(12):
            if c % 2 == 0:
                p.ts("dve", cv[:, c, :], raw[i][:, c, 0:128], cw[:, 0, c:c + 1], None, ALU.mult, None, [Braw[i], Bc], [Bcv])
                p.stt(cv[:, c, :], raw[i][:, c, 1:129], cw[:, 1, c:c + 1], cv[:, c, :], ALU.mult, ALU.add, [Braw[i], Bc, Bcv], [Bcv])
                p.stt(cv[:, c, :], raw[i][:, c, 2:130], cw[:, 2, c:c + 1], cv[:, c, :], ALU.mult, ALU.add, [Braw[i], Bc, Bcv], [Bcv])
            else:
                p.ts("pool", cv[:, c, :], raw[i][:, c, 0:128], cw[:, 0, c:c + 1], None, ALU.mult, None, [Braw[i], Bc], [Bcv])
                for j in (1, 2):
                    p.ts("pool", tmpc, raw[i][:, c, j:j + 128], cw[:, j, c:c + 1], None, ALU.mult, None, [Braw[i], Bc], [Btc])
                    p.tt("pool", cv[:, c, :], cv[:, c, :], tmpc, ALU.add, [Bcv, Btc], [Bcv])
        p.act(cv, cv, AF.Silu, [Bcv], [Bcv])
        if g.cfg.get("cutB", 99) <= 1:
            continue
        cvf = cv.rearrange("p a b -> p (a b)")
        p.act(sqb, cvf[:, 0:1024], AF.Square, [Bcv], [Bsq])
        for n in range(2):
            ps, pb = bank(g)
            p.mm(ps[:, :], g.onesF, sqb[:, n * 512:(n + 1) * 512], True, True, [g.Bconst, Bsq], [pb])
            if n == 0:
                p.act(rn[:, 0:512], ps[:, :], AF.Sqrt, [pb], [Brn], scale=128.0, bias=EPS * 128.0)
            else:
                p.act(rn[:, 512:1024], ps[:, :], AF.Sqrt, [pb], [Brn], scale=1.0, bias=EPS)
        p.recip(rn, rn, [Brn], [Brn])
        qk0 = cv[:, 0:8, :]
        p.tt("pool", qk0, qk0, rn.rearrange("p (a b) -> p a b", a=8), ALU.mult, [Bcv, Brn], [Bcv])
        if g.cfg.get("cutB", 99) <= 2:
            continue
        if lat:
            for n in range(2):
                ps, pb = bank(g)
                p.mm(ps[:, :], rotm, cvf[:, n * 512:(n + 1) * 512], True, True, [Bc, Bcv], [pb])
                p.tt("dve", t1[:, n * 4:(n + 1) * 4, :], v4(ps), cs[i][:, 1:2, :].broadcast_to([128, 4, 128]), ALU.mult, [pb, Bcs[i]], [Bt1])
            p.tt("pool", qkr, qk0, cs[i][:, 0:1, :].broadcast_to([128, 8, 128]), ALU.mult, [Bcv, Bcs[i]], [Bqkr])
            p.tt("pool", qkr, qkr, t1, ALU.add, [Bqkr, Bt1], [Bqkr])
            QK_, BQK_ = qkr, Bqkr
        else:
            QK_, BQK_ = qk0, Bcv
        qT = lambda h: QK_[:, h, :]
        kT = lambda h: QK_[:, 4 + h, :]
        if g.cfg.get("cutB", 99) <= 3:
            continue
        ps, pb = bank(g)
        for h in range(4):
            p.tr(ps[:, h * 128:(h + 1) * 128], kT(h), g.identF, [BQK_, g.Bconst], [pb])
        p.copy("act", ktok, v4(ps), [pb], [Bkt])
        ps, pb = bank(g)
        for h in range(4):
            p.tr(ps[:, h * 128:(h + 1) * 128], cv[:, 8 + h, :], g.identF, [Bcv, g.Bconst], [pb])
        p.copy("dve", vtok, v4(ps), [pb], [Bvt])
        if g.cfg.get("cutB", 99) <= 4:
            continue
        beta, nbeta, z, logg, gam, eg, be, glg, kds, glv = [sm[:, r_, :] for r_ in range(10)]
        p.act(beta, ba[i][:, 0:8], AF.Sigmoid, [Bba[i]], [Bsm])
        p.ts("pool", nbeta, beta, -1.0, None, ALU.mult, None, [Bsm], [Bsm])
        p.tt("pool", z, ba[i][:, 8:16], dtb, ALU.add, [Bba[i], Bc], [Bsm])
        p.act(z, z, AF.Exp, [Bsm], [Bsm])
        p.act(z, z, AF.Ln, [Bsm], [Bsm], bias=1.0)
        p.tt("pool", logg, z, nexpA, ALU.mult, [Bsm, Bc], [Bsm])
        ps, pb = bank(g)
        p.mm(ps[:, 0:4], LE, logg[:, 0:4], True, True, [Bc, Bsm], [pb])
        p.mm(ps[:, 4:8], GE, logg[:, 4:8], True, True, [Bc, Bsm], [pb])
        p.copy("dve", gam, ps[:, 0:8], [pb], [Bsm])
        if g.cfg.get("cutB", 99) <= 4.1:
            continue
        p.copy("pool", lrep, logg.unsqueeze(2).broadcast_to([128, 8, 128]), [Bsm], [Blr])
        gps = []
        for d_ in range(2):
            ps, pb = bank(g)
            for h in range(4):
                p.mm(ps[:, h * 128:(h + 1) * 128], lrep[:, d_ * 4 + h, :], LE if d_ == 0 else GE, True, True, [Blr, Bc], [pb])
            gps.append((ps, pb))
        if g.cfg.get("cutB", 99) <= 4.2:
            continue
        p.act(eg, gam, AF.Exp, [Bsm], [Bsm])
        p.tt("pool", be, beta, eg, ALU.mult, [Bsm], [Bsm])
        for d_ in range(2):
            ps, pb = gps[d_]
            last = 127 if d_ == 0 else 0
            p.act(egB[:, d_ * 4:(d_ + 1) * 4, :], v4(ps), AF.Exp, [pb], [BeB])
            p.copy("dve", glg[:, d_ * 4:(d_ + 1) * 4], v4(ps)[:, :, last], [pb], [Bsm])
            for h in range(4):
                dh = d_ * 4 + h
                p.ts("dve", mmx[:, dh, :], ps[:, h * 128:(h + 1) * 128], gam[:, dh:dh + 1], 0.0, ALU.subtract, ALU.max, [pb, Bsm], [Bmx])
        if g.cfg.get("cutB", 99) <= 4.3:
            continue
        p.tt("pool", kds, glg, gam, ALU.subtract, [Bsm], [Bsm])
        p.act(kds, kds, AF.Exp, [Bsm], [Bsm])
        p.act(glv, glg, AF.Exp, [Bsm], [Bsm])
        if g.cfg.get("cutB", 99) <= 4.4:
            continue
        p.act(dec, mmx, AF.Exp, [Bmx], [Bdec], scale=-1.0)
        for d_ in range(2):
            sl_ = slice(d_ * 4, (d_ + 1) * 4)
            p.tt("pool", decI[:, sl_, :], dec[:, sl_, :], (GE if d_ == 0 else LE).unsqueeze(1).broadcast_to([128, 4, 128]), ALU.mult, [Bdec, Bc], [BdI])
            p.tt("pool", decS[:, sl_, :], dec[:, sl_, :], (GT if d_ == 0 else LT).unsqueeze(1).broadcast_to([128, 4, 128]), ALU.mult, [Bdec, Bc], [BdS])
        if g.cfg.get("cutB", 99) <= 5:
            continue
        ps_kk, pb_kk = bank(g)
        for h in range(4):
            p.mm(ps_kk[:, h * 128:(h + 1) * 128], kT(h), kT(h), True, True, [BQK_], [pb_kk])
        ps_qk, pb_qk = bank(g)
        for h in range(4):
            p.mm(ps_qk[:, h * 128:(h + 1) * 128], qT(h), kT(h), True, True, [BQK_], [pb_qk])
        Q, P_, N_ = Qb[0], Pb[0], Nb[0]
        for d_ in range(2):
            for h in range(4):
                dh = d_ * 4 + h
                p.stt(Q[:, dh, :], ps_kk[:, h * 128:(h + 1) * 128], nbeta[:, dh:dh + 1], decS[:, dh, :], ALU.mult, ALU.mult, [pb_kk, Bsm, BdS], [BQ[0]])
            p.tt("dve", aqk[:, d_ * 4:(d_ + 1) * 4, :], v4(ps_qk), decI[:, d_ * 4:(d_ + 1) * 4, :], ALU.mult, [pb_qk, BdI], [Baqk])
        if g.cfg.get("cutB", 99) <= 6:
            continue
        for d_ in range(2):
            ps, pb = bank(g)
            for h in range(4):
                p.tr(ps[:, h * 128:(h + 1) * 128], Q[:, d_ * 4 + h, :], g.identF, [BQ[0], g.Bconst], [pb])
            p.copy("act", P_[:, d_ * 4:(d_ + 1) * 4, :], v4(ps), [pb], [BP[0]])
            ps, pb = bank(g)
            for h in range(4):
                p.tr(ps[:, h * 128:(h + 1) * 128], aqk[:, d_ * 4 + h, :], g.identF, [Baqk, g.Bconst], [pb])
            p.copy("dve", pk[i][:, d_ * 4:(d_ + 1) * 4, 256:384], v4(ps), [pb], [Bpk[i]])
        if g.cfg.get("cutB", 99) <= 7:
            continue
        p.tt("pool", N_, P_, g.identF.unsqueeze(1).broadcast_to([128, 8, 128]), ALU.add, [BP[0], g.Bconst], [BN[0]])
        cur = 0
        for lev in range(6):
            nxt = 1 - cur
            lastlev = (lev == 5)
            for d_ in range(2):
                sl_ = slice(d_ * 4, (d_ + 1) * 4)
                if not lastlev:
                    ps, pb = bank(g)
                    for h in range(4):
                        dh = d_ * 4 + h
                        p.mm(ps[:, h * 128:(h + 1) * 128], Qb[cur][:, dh, :], Pb[cur][:, dh, :], True, True, [BQ[cur], BP[cur]], [pb])
                    p.copy("act", Pb[nxt][:, sl_, :], v4(ps), [pb], [BP[nxt]])
                ps, pb = bank(g)
                for h in range(4):
                    dh = d_ * 4 + h
                    p.mm(ps[:, h * 128:(h + 1) * 128], Pb[cur][:, dh, :], Qb[cur][:, dh, :], True, True, [BQ[cur], BP[cur]], [pb])
                p.copy("act" if lastlev else "dve", Qb[nxt][:, sl_, :], v4(ps), [pb], [BQ[nxt]])
            for d_ in range(2):
                sl_ = slice(d_ * 4, (d_ + 1) * 4)
                ps, pb = bank(g)
                for h in range(4):
                    dh = d_ * 4 + h
                    p.mm(ps[:, h * 128:(h + 1) * 128], Qb[nxt][:, dh, :], Nb[cur][:, dh, :], True, True, [BQ[nxt], BN[cur]], [pb])
                p.tt("dve", Nb[nxt][:, sl_, :], v4(ps), Nb[cur][:, sl_, :], ALU.add, [pb, BN[cur]], [BN[nxt]])
            cur = nxt
        TT, BTT = Nb[cur], BN[cur]
        if g.cfg.get("cutB", 99) <= 8:
            continue
        for d_ in range(2):
            for h in range(4):
                dh = d_ * 4 + h
                p.ts("pool", bv[:, dh, :], vtok[:, h, :], beta[:, dh:dh + 1], None, ALU.mult, None, [Bvt, Bsm], [Bbv])
                p.ts("pool", bek[:, dh, :], ktok[:, h, :], be[:, dh:dh + 1], None, ALU.mult, None, [Bkt, Bsm], [Bbek])
                p.ts("pool", pk[i][:, dh, 512:640], ktok[:, h, :], kds[:, dh:dh + 1], None, ALU.mult, None, [Bkt, Bsm], [Bpk[i]])
            p.tt("pool", pk[i][:, d_ * 4:(d_ + 1) * 4, 384:512], QK_[:, 0:4, :], egB[:, d_ * 4:(d_ + 1) * 4, :], ALU.mult, [BQK_, BeB], [Bpk[i]])
        p.copy("pool", pk[i][:, :, 640:641], glv.unsqueeze(2), [Bsm], [Bpk[i]])
        for d_ in range(2):
            ps, pb = bank(g)
            for h in range(4):
                dh = d_ * 4 + h
                p.mm(ps[:, h * 128:(h + 1) * 128], TT[:, dh, :], bv[:, dh, :], True, True, [BTT, Bbv], [pb])
            p.copy("act", pk[i][:, d_ * 4:(d_ + 1) * 4, 0:128], v4(ps), [pb], [Bpk[i]])
            ps, pb = bank(g)
            for h in range(4):
                dh = d_ * 4 + h
                p.mm(ps[:, h * 128:(h + 1) * 128], bek[:, dh, :], TT[:, dh, :], True, True, [BTT, Bbek], [pb])
            p.copy("dve", pk[i][:, d_ * 4:(d_ + 1) * 4, 128:256], v4(ps), [pb], [Bpk[i]])
        if g.cfg.get("cutB", 99) <= 9:
            continue
        p.dma("pool", S["DNP"][t].rearrange("d p c -> p d c"), pk[i], R=[Bpk[i]], W=[g.DB["DNP"]])


def even_dnscan(g, l, e, ctx_out):
    p, I, S = g.p, g.I, g.S
    new_phase(g)
    lat_tiles = [t for t in tile_list(g, l, "proj") if t >= 2]
    Bc = Buf()
    goutb = alloc(g, [128], F32)
    p.dma("sp", goutb, I["dn_g_out"][e, :].partition_broadcast(128), W=[Bc])
    Sst = alloc(g, [8, 128], F32)
    BS = [Buf() for _ in range(8)]
    for dh in range(8):
        p.memset("pool", Sst[:, dh, :], 0.0, [BS[dh]])
    NPK = 6
    pk = [alloc(g, [648], F32) for _ in range(NPK)]
    Bpk = [Buf() for _ in range(NPK)]
    u_sb = [alloc(g, [128], F32) for _ in range(4)]
    Bu = [Buf() for _ in range(4)]
    o_sb = [alloc(g, [4, 128], F32) for _ in range(2)]
    Bo = [Buf() for _ in range(2)]
    of_sb = [alloc(g, [4, 128], F32) for _ in range(2)]
    Bof = [Buf() for _ in range(2)]
    gate = [alloc(g, [512], F32) for _ in range(2)]
    Bgate = [Buf() for _ in range(2)]
    y1 = alloc(g, [4, 128], F32)
    ydn = alloc(g, [512], BF16)
    zt = [alloc(g, [4, 128], BF16) for _ in range(2)]
    Bzt = [Buf() for _ in range(2)]
    st = alloc(g, [3, 4], F32)
    junk = alloc(g, [128], BF16)
    By1, Bydn, Bst, Bj = Buf(), Buf(), Buf(), Buf()
    ipk = 0
    iu = 0
    for d_ in range(2):
        order = [0, 1] + lat_tiles if d_ == 0 else [1, 0] + lat_tiles[::-1]
        for it, t in enumerate(order):
            i = it % 2
            want_o = (t >= 2) or ctx_out
            if d_ == 1 and want_o:
                p.dma("sp", of_sb[i].rearrange("p a b -> p (a b)"), S["OF"][t * 128:(t + 1) * 128, :], R=[g.DB["OF"]], W=[Bof[i]])
                p.dma("sp", gate[i], S["GATE"][t * 128:(t + 1) * 128, :], R=[g.DB["GATE"]], W=[Bgate[i]])
            for h in range(4):
                dh = d_ * 4 + h
                pk_, bpk_ = pk[ipk % NPK], Bpk[ipk % NPK]
                ipk += 1
                p.dma("sp", pk_, S["DNP"][t, dh], R=[g.DB["DNP"]], W=[bpk_])
                u_, bu_ = u_sb[iu % 4], Bu[iu % 4]
                iu += 1
                ps1, pb1 = bank(g)
                p.mm(ps1[:, 0:128], pk_[:, 128:256], Sst[:, dh, :], True, True, [bpk_, BS[dh]], [pb1])
                p.tt("dve", u_, pk_[:, 0:128], ps1[:, 0:128], ALU.subtract, [bpk_, pb1], [bu_])
                if want_o:
                    ps2, pb2 = bank(g)
                    p.mm(ps2[:, 0:128], pk_[:, 384:512], Sst[:, dh, :], True, False, [bpk_, BS[dh]], [pb2])
                    p.mm(ps2[:, 0:128], pk_[:, 256:384], u_, False, True, [bpk_, bu_], [pb2])
                ps3, pb3 = bank(g)
                p.mm(ps3[:, 0:128], pk_[:, 512:640], u_, True, True, [bpk_, bu_], [pb3])
                p.stt(Sst[:, dh, :], Sst[:, dh, :], pk_[:, 640:641], ps3[:, 0:128], ALU.mult, ALU.add, [BS[dh], bpk_, pb3], [BS[dh]])
                if want_o:
                    if d_ == 0:
                        p.copy("act", o_sb[i][:, h, :], ps2[:, 0:128], [pb2], [Bo[i]])
                    else:
                        p.tt("dve", o_sb[i][:, h, :], ps2[:, 0:128], of_sb[i][:, h, :], ALU.add, [pb2, Bof[i]], [Bo[i]])
            if not want_o:
                continue
            if d_ == 0:
                p.dma("pool", S["OF"][t * 128:(t + 1) * 128, :], o_sb[i].rearrange("p a b -> p (a b)"), R=[Bo[i]], W=[g.DB["OF"]])
                continue
            for h in range(4):
                p.act(junk, o_sb[i][:, h, :], AF.Square, [Bo[i]], [Bj, Bst], accum_out=st[:, 0, h:h + 1])
            p.act(st[:, 1, :], st[:, 0, :], AF.Sqrt, [Bst], [Bst], scale=1.0 / 128, bias=EPS)
            p.recip(st[:, 2, :], st[:, 1, :], [Bst], [Bst])
            p.act(gate[i], gate[i], AF.Silu, [Bgate[i]], [Bgate[i]])
            for h in range(4):
                p.stt(y1[:, h, :], o_sb[i][:, h, :], st[:, 2, h:h + 1], goutb, ALU.mult, ALU.mult, [Bo[i], Bst, Bc], [By1])
            p.tt("pool", ydn, y1.rearrange("p a b -> p (a b)"), gate[i], ALU.mult, [By1, Bgate[i]], [Bydn])
            ps, pb = bank(g)
            psb = ps[:, :].bitcast(BF16)
            for h in range(4):
                p.tr(psb[:, h * 128:(h + 1) * 128], ydn[:, h * 128:(h + 1) * 128], g.identB, [Bydn, g.Bconst], [pb])
            p.copy("act", zt[i], psb[:, 0:512].rearrange("p (a b) -> p a b", a=4), [pb], [Bzt[i]])
            p.dma("pool", S["ZT"][4:8, :, t * 128:(t + 1) * 128].rearrange("c p t -> p c t"), zt[i], R=[Bzt[i]], W=[g.DB["ZT"]])


def even_na(g, l, e, ctx_out):
    p, I, S = g.p, g.I, g.S
    new_phase(g)
    nrows = 64
    lim = g.cfg.get("ntiles")
    if lim:
        nrows = g.cfg.get("narows") or 2 * lim
    Bc = Buf()
    KT = alloc(g, [4, NTOK], BF16)
    p.dma("sp", KT, S["QK"][4:8, :, :].rearrange("c p t -> p c t"), R=[g.DB["QK"]], W=[Bc])
    VAs = alloc(g, [NT, 520], BF16)
    p.dma("sp", VAs, S["VA"].rearrange("(t p) f -> p t f", p=128), R=[g.DB["VA"]], W=[Bc])
    tabf = alloc(g, [8, 16, 64], F32)
    maskf = alloc(g, [64], F32)
    TT = alloc(g, [8, 16, 64], BF16)
    p.memset("pool", tabf, 0.0, [Bc])
    for h in range(8):
        p.dma("sp", tabf[0:64, h, 0:15, :], I["rpb_tab"][e, h].rearrange("b k c -> k b c"), W=[Bc])
        p.dma("sp", tabf[64:128, h, 0:14, :], I["rpb_tab"][e, h, 1:15].rearrange("b k c -> k b c"), W=[Bc])
    p.dma("sp", maskf[0:64, :], I["namask"][:, :], W=[Bc])
    p.dma("sp", maskf[64:128, :], I["namask"][:, :], W=[Bc])
    for h in range(8):
        p.tt("pool", TT[:, h, :, :], tabf[:, h, :, :], maskf.unsqueeze(1).broadcast_to([128, 16, 64]), ALU.add, [Bc], [Bc])
    qT = [alloc(g, [4, 128], BF16) for _ in range(2)]
    BqT = [Buf() for _ in range(2)]
    PT = [alloc(g, [512], BF16) for _ in range(3)]
    BPT = [Buf() for _ in range(3)]
    att = [alloc(g, [8, 64], BF16) for _ in range(2)]
    Batt = [Buf() for _ in range(2)]
    rinv = [alloc(g, [8], F32) for _ in range(2)]
    Brinv = [Buf() for _ in range(2)]
    zt = [alloc(g, [4, 128], BF16) for _ in range(2)]
    Bzt = [Buf() for _ in range(2)]
    ipt = 0
    stc = [0]

    def st_bank():
        i_ = 4 + stc[0] % 3
        stc[0] += 1
        return g.ps[i_], g.PB[i_]

    def oa_banks(k):
        b0 = (k % 2) * 2
        return [(g.ps[b0], g.PB[b0]), (g.ps[b0 + 1], g.PB[b0 + 1])]

    def finish(oa, i, tq):
        for bnk in range(2):
            ps, pb = oa[bnk]
            v = ps[:, 0:260].rearrange("p (a b) -> p a b", a=4)
            p.recip(rinv[i][:, bnk * 4:(bnk + 1) * 4], v[:, :, 64], [pb], [Brinv[i]])
            p.tt("dve", att[i][:, bnk * 4:(bnk + 1) * 4, :], v[:, :, 0:64],
                 rinv[i][:, bnk * 4:(bnk + 1) * 4].unsqueeze(2).broadcast_to([128, 4, 64]), ALU.mult, [pb, Brinv[i]], [Batt[i]])
        ps, pb = g.ps[7], g.PB[7]
        psb = ps[:, :].bitcast(BF16)
        af = att[i].rearrange("p a b -> p (a b)")
        for c in range(4):
            p.tr(psb[:, c * 128:(c + 1) * 128], af[:, c * 128:(c + 1) * 128], g.identB, [Batt[i], g.Bconst], [pb])
        p.copy("act", zt[i], psb[:, 0:512].rearrange("p (a b) -> p a b", a=4), [pb], [Bzt[i]])
        p.dma("pool", S["ZT"][0:4, :, tq * 128:(tq + 1) * 128].rearrange("c p t -> p c t"), zt[i], R=[Bzt[i]], W=[g.DB["ZT"]])

    if ctx_out:
        qc = alloc(g, [4, 256], BF16)
        Bqc = Buf()
        p.dma("sp", qc, S["QK"][0:4, :, 0:256].rearrange("c p t -> p c t"), R=[g.DB["QK"]], W=[Bqc])
        PTc = [alloc(g, [512], BF16) for _ in range(2)]
        BPTc = [Buf() for _ in range(2)]
        oa = [oa_banks(0), oa_banks(1)]
        for h in range(8):
            pr, pb_ = h // 2, (h % 2) * 64
            ps, pb = st_bank()
            for j in range(2):
                p.mm(ps[:, j * 256:(j + 1) * 256], KT[pb_:pb_ + 64, pr, j * 128:(j + 1) * 128], qc[pb_:pb_ + 64, pr, :], True, True, [Bc, Bqc], [pb])
            P_, BP_ = PTc[h % 2], BPTc[h % 2]
            p.act(P_, ps[:, :], AF.Exp, [pb], [BP_])
            for qt in range(2):
                ops_, opb = oa[qt][h // 4]
                hh = h % 4
                for j in range(2):
                    p.mm(ops_[:, hh * 65:(hh + 1) * 65], P_[:, j * 256 + qt * 128: j * 256 + (qt + 1) * 128], VAs[:, j, h * 65:(h + 1) * 65],
                         j == 0, j == 1, [BP_, Bc], [opb])
        for qt in range(2):
            finish(oa[qt], qt, qt)
    for r in range(nrows):
        tq = 2 + r // 2
        iq = (r // 2) % 2
        hq = r % 2
        if hq == 0:
            p.dma("sp", qT[iq], S["QK"][0:4, :, tq * 128:(tq + 1) * 128].rearrange("c p t -> p c t"), R=[g.DB["QK"]], W=[BqT[iq]])
            oa = oa_banks(r // 2)
        rs = min(max(r - 4, 0), 56)
        units = []
        wr = rs
        while wr < rs + 8:
            if wr % 2 == 0 and wr + 1 < rs + 8:
                units.append((wr, 2))
                wr += 2
            else:
                units.append((wr, 1))
                wr += 1
        nloc = (rs + 7) // 2 - rs // 2 + 1
        for h in range(8):
            pr, pb_ = h // 2, (h % 2) * 64
            ps, pb = st_bank()
            q_ap = qT[iq][pb_:pb_ + 64, pr, hq * 64:(hq + 1) * 64]
            geo = []
            for (wr, n) in units:
                slot = wr // 2 - rs // 2
                col = 256 + wr * 64
                if n == 2:
                    lo, hi, b = 0, 128, wr - r + 7
                elif wr % 2 == 1:
                    lo, hi, b = 64, 128, wr - r + 6
                else:
                    lo, hi, b = 0, 64, wr - r + 7
                geo.append((wr, slot, lo, hi))
                out = ps[lo:hi, slot * 64:(slot + 1) * 64]
                p.mm(out, KT[pb_:pb_ + 64, pr, col:col + (hi - lo)], q_ap, True, False, [Bc, BqT[iq]], [pb])
                p.mm(out, g.identB[:, lo:hi], TT[:, h, b, :], False, True, [g.Bconst, Bc], [pb])
            for j in range(2):
                slot = nloc + j
                p.mm(ps[:, slot * 64:(slot + 1) * 64], KT[pb_:pb_ + 64, pr, j * 128:(j + 1) * 128], q_ap, True, True, [Bc, BqT[iq]], [pb])
            ncol = (nloc + 2) * 64
            P_, BP_ = PT[ipt % 3], BPT[ipt % 3]
            ipt += 1
            p.act(P_[:, 0:ncol], ps[:, 0:ncol], AF.Exp, [pb], [BP_])
            ops_, opb = oa[h // 4]
            hh = h % 4
            out = ops_[hq * 64:(hq + 1) * 64, hh * 65:(hh + 1) * 65]
            nmm = len(geo) + 2
            for ii, (wr, slot, lo, hi) in enumerate(geo):
                p.mm(out, P_[lo:hi, slot * 64:(slot + 1) * 64], VAs[lo:hi, 2 + wr // 2, h * 65:(h + 1) * 65], ii == 0, False, [BP_, Bc], [opb])
            for j in range(2):
                slot = nloc + j
                p.mm(out, P_[:, slot * 64:(slot + 1) * 64], VAs[:, j, h * 65:(h + 1) * 65], False, j == 1, [BP_, Bc], [opb])
        if hq == 1:
            finish(oa, iq, tq)


def even_out(g, l, e):
    p, I, S = g.p, g.I, g.S
    new_phase(g)
    tiles = tile_list(g, l, "mixout")
    Wo = alloc(g, [8, 1024], BF16)
    BWo = Buf()
    load_w_bf16(g, Wo, I["w_out_even"][e].rearrange("(k p) n -> p k n", p=128), BWo)
    gates = [load_gate(g, l, s_, 0) for s_ in range(2)]
    x = [alloc(g, [D], F32) for _ in range(2)]
    Bx = [Buf() for _ in range(2)]
    zt = [alloc(g, [8, 128], BF16) for _ in range(2)]
    Bzt = [Buf() for _ in range(2)]
    tmp = [alloc(g, [512], F32) for _ in range(2)]
    Btmp = [Buf() for _ in range(2)]
    for it, t in enumerate(tiles):
        i = it % 2
        s_ = 1 if t < 2 else 0
        p.dma("sp", x[i], S["X"][t * 128:(t + 1) * 128, :], R=[g.DB["X"]], W=[Bx[i]])
        p.dma("sp", zt[i], S["ZT"][:, :, t * 128:(t + 1) * 128].rearrange("c p t -> p c t"), R=[g.DB["ZT"]], W=[Bzt[i]])
        pss = []
        for n in range(2):
            ps, pb = bank(g)
            for k in range(8):
                p.mm(ps[:, :], zt[i][:, k, :], Wo[:, k, n * 512:(n + 1) * 512], k == 0, k == 7, [Bzt[i], BWo], [pb])
            pss.append((ps, pb))
        gt, Bgt = gates[s_]
        resid_update(g, pss, x[i], Bx[i], gt, Bgt, tmp, Btmp)
        p.dma("pool", S["X"][t * 128:(t + 1) * 128, :], x[i], R=[Bx[i]], W=[g.DB["X"]])


def resid_update(g, ps_list, x, Bx, gt, Bgt, tmp, Btmp):
    p = g.p
    for n, (ps, pb) in enumerate(ps_list):
        p.tt("dve", tmp[n], ps[:, :], gt[:, n * 512:(n + 1) * 512], ALU.mult, [pb, Bgt], [Btmp[n]])
        p.tt("pool", x[:, n * 512:(n + 1) * 512], tmp[n], x[:, n * 512:(n + 1) * 512], ALU.add, [Btmp[n], Bx], [Bx])


def phase_odd(g, l):
    p, I, S = g.p, g.I, g.S
    o = l // 2
    tiles = tile_list(g, l, "mixout")
    w_in = I["w_in_odd"][o].rearrange("(k p) n -> p k n", p=128)
    new_phase(g)
    Wv = alloc(g, [8, 3072], BF16)
    BWv = Buf()
    load_w_bf16(g, Wv, w_in[:, :, 3072:6144], BWv, max_cols=1024)
    wsT = alloc(g, [8, 128], BF16)
    bsT = alloc(g, [8], F32)
    gvb = alloc(g, [3072], F32)
    Bc = Buf()
    p.dma("pool", wsT, I["gm_wsT"][o], W=[Bc], stream="wl")
    p.dma("sp", bsT, I["gm_bsT"][o], W=[Bc])
    p.dma("sp", gvb, I["gm_g_v"][o, :].partition_broadcast(128), W=[Bc])
    mods = [load_mod(g, l, s_, 0) for s_ in range(2)]
    nb = NormBufs(g)
    vf = [alloc(g, [3072], F32) for _ in range(2)]
    Bvf = [Buf() for _ in range(2)]
    vn = [alloc(g, [3072], BF16) for _ in range(2)]
    Bvn = [Buf() for _ in range(2)]
    junk = alloc(g, [3072], BF16)
    Bj = Buf()
    st = [alloc(g, [4], F32) for _ in range(2)]
    Bst = [Buf() for _ in range(2)]
    mix = [alloc(g, [3072], F32) for _ in range(2)]
    Bmix = [Buf() for _ in range(2)]
    tmpg = [alloc(g, [384], F32) for _ in range(2)]
    Btg = [Buf() for _ in range(2)]
    for it, t in enumerate(tiles):
        i = it % 2
        s_ = 1 if t < 2 else 0
        hT, BhT = norm_T(g, nb, i, t, mods[s_])
        for n in range(6):
            ps, pb = bank(g)
            for k in range(8):
                p.mm(ps[:, :], hT[:, k, :], Wv[:, k, n * 512:(n + 1) * 512], k == 0, k == 7, [BhT, BWv], [pb])
            p.act(vf[i][:, n * 512:(n + 1) * 512], ps[:, :], AF.Gelu, [pb], [Bvf[i]])
        p.act(junk, vf[i], AF.Square, [Bvf[i]], [Bj, Bst[i]], accum_out=st[i][:, 0:1])
        p.act(st[i][:, 1:2], st[i][:, 0:1], AF.Sqrt, [Bst[i]], [Bst[i]], scale=1.0 / 3072, bias=EPS)
        p.recip(st[i][:, 2:3], st[i][:, 1:2], [Bst[i]], [Bst[i]])
        p.ts("pool", vn[i], vf[i], st[i][:, 2:3], None, ALU.mult, None, [Bvf[i], Bst[i]], [Bvn[i]])
        for gi in range(8):
            ps, pb = bank(g)
            p.mm(ps[:, 0:384], wsT[:, gi, :], vn[i][:, gi * 384:(gi + 1) * 384], True, True, [Bc, Bvn[i]], [pb])
            tg, btg = tmpg[gi % 2], Btg[gi % 2]
            p.tt("dve", tg, ps[:, 0:384], gvb[:, gi * 384:(gi + 1) * 384], ALU.mult, [pb, Bc], [btg])
            p.ts("pool", mix[i][:, gi * 384:(gi + 1) * 384], tg, bsT[:, gi:gi + 1], None, ALU.add, None, [btg, Bc], [Bmix[i]])
        p.dma("pool", S["MIX"][t * 128:(t + 1) * 128, :], mix[i], R=[Bmix[i]], W=[g.DB["MIX"]])
    new_phase(g)
    Wu = alloc(g, [8, 3072], BF16)
    Wo = alloc(g, [24, 1024], BF16)
    BWu, BWo = Buf(), Buf()
    load_w_bf16(g, Wu, w_in[:, :, 0:3072], BWu, max_cols=1024)
    load_w_bf16(g, Wo, I["w_out_odd"][o].rearrange("(k p) n -> p k n", p=128), BWo)
    mods = [load_mod(g, l, s_, 0) for s_ in range(2)]
    gates = [load_gate(g, l, s_, 0) for s_ in range(2)]
    nb = NormBufs(g)
    mix = [alloc(g, [3072], F32) for _ in range(2)]
    Bmix = [Buf() for _ in range(2)]
    uf = [alloc(g, [512], F32) for _ in range(2)]
    Buf_ = [Buf() for _ in range(2)]
    sb = [alloc(g, [3072], BF16) for _ in range(2)]
    Bsb = [Buf() for _ in range(2)]
    sT = [alloc(g, [24, 128], BF16) for _ in range(2)]
    BsT = [Buf() for _ in range(2)]
    tmp = [alloc(g, [512], F32) for _ in range(2)]
    Btmp = [Buf() for _ in range(2)]
    ui = 0
    for it, t in enumerate(tiles):
        i = it % 2
        s_ = 1 if t < 2 else 0
        p.dma("sp", mix[i], S["MIX"][t * 128:(t + 1) * 128, :], R=[g.DB["MIX"]], W=[Bmix[i]])
        hT, BhT = norm_T(g, nb, i, t, mods[s_])
        for n in range(6):
            ps, pb = bank(g)
            for k in range(8):
                p.mm(ps[:, :], hT[:, k, :], Wu[:, k, n * 512:(n + 1) * 512], k == 0, k == 7, [BhT, BWu], [pb])
            u_, bu_ = uf[ui % 2], Buf_[ui % 2]
            ui += 1
            p.act(u_, ps[:, :], AF.Gelu, [pb], [bu_])
            p.tt("pool", sb[i][:, n * 512:(n + 1) * 512], u_, mix[i][:, n * 512:(n + 1) * 512], ALU.mult, [bu_, Bmix[i]], [Bsb[i]])
        for q in range(3):
            ps, pb = bank(g)
            psb = ps[:, :].bitcast(BF16)
            for kk in range(8):
                k = q * 8 + kk
                p.tr(psb[:, kk * 128:(kk + 1) * 128], sb[i][:, k * 128:(k + 1) * 128], g.identB, [Bsb[i], g.Bconst], [pb])
            dst = sT[i][:, q * 8:(q + 1) * 8, :]
            src = psb.rearrange("p (a b) -> p a b", a=8)
            if q % 2 == 0:
                p.copy("act", dst, src, [pb], [BsT[i]])
            else:
                p.copy("dve", dst, src, [pb], [BsT[i]])
        pss = []
        for n in range(2):
            ps, pb = bank(g)
            for k in range(24):
                p.mm(ps[:, :], sT[i][:, k, :], Wo[:, k, n * 512:(n + 1) * 512], k == 0, k == 23, [BsT[i], BWo], [pb])
            pss.append((ps, pb))
        gt, Bgt = gates[s_]
        resid_update(g, pss, nb.x[i], nb.Bx[i], gt, Bgt, tmp, Btmp)
        p.dma("pool", S["X"][t * 128:(t + 1) * 128, :], nb.x[i], R=[nb.Bx[i]], W=[g.DB["X"]])


def make_in_maps(inputs):
    f = lambda a: np.ascontiguousarray(np.asarray(a, dtype=np.float32))
    shared = {}
    for k in ["w_ada", "b_ada", "g_norm_mix", "g_norm_ffn", "w_in_even", "w_out_even", "dn_conv", "dn_g_out",
              "w_in_odd", "gm_g_v", "w_out_odd", "w_ff1", "w_ff2", "g_final"]:
        shared[k] = f(inputs[k])
    shared["rpb_tab"] = rpb_table(f(inputs["na_rpb"]))
    shared["dn_a_log"] = f(inputs["dn_a_log"]).reshape(2, 8)
    shared["dn_dt_bias"] = f(inputs["dn_dt_bias"]).reshape(2, 8)
    shared["gm_wsT"] = np.ascontiguousarray(f(inputs["gm_ws"]).transpose(0, 3, 1, 2))
    shared["gm_bsT"] = np.ascontiguousarray(f(inputs["gm_bs"]).transpose(0, 2, 1))
    shared.update(host_consts())
    x = f(inputs["x"])
    ctx = f(inputs["ctx"])
    c = f(inputs["c"])
    c_ctx = f(inputs["c_ctx"])
    maps = []
    for b in range(8):
        m = dict(shared)
        m["x"] = x[b]
        m["ctx"] = ctx[b]
        m["cvec"] = np.ascontiguousarray(np.stack([c[b], c_ctx]))
        maps.append(m)
    return maps


def kernel(**inputs):
    nc, g = build()
    maps = make_in_maps(inputs)
    res = run_bass_kernel_spmd(nc, maps, core_ids=list(range(8)))
    return np.stack([np.asarray(r["out"], dtype=np.float32) for r in res.results], axis=0)
```

```python
import numpy as np
from contextlib import ExitStack
import concourse.bass as bass
import concourse.mybir as mybir
from concourse.bass_utils import run_bass_kernel_spmd

F32 = mybir.dt.float32
BF16 = mybir.dt.bfloat16
AF = mybir.ActivationFunctionType
ALU = mybir.AluOpType

D = 1024
L = 4096
LC = 256
NT = (L + LC) // 128
NTOK = L + LC
EPS = 1e-6
DEPTH = 4
EVEN_IN = 3600
NPAD = NTOK + 4
NEG = -30000.0


class Buf:
    __slots__ = ("name", "w", "r", "excl")

    def __init__(self, name="", excl=False):
        self.name = name
        self.w = None
        self.r = []
        self.excl = excl


class Prog:
    ENGS = ("pe", "act", "dve", "pool", "sp")

    def __init__(self, nc):
        self.nc = nc
        self.ops = []
        self.stream_cnt = {}
        self.stream_last = {}
        self.stream_R = {"ld": 8, "st": 8, "wl": 12}
        self.last_real = {}

    def sem_names(self):
        names = ["e_" + e for e in self.ENGS]
        for st, R in self.stream_R.items():
            names += ["s_%s%d" % (st, j) for j in range(R)]
        return names

    def add(self, eng, fn, reads=(), writes=(), stream=None, extra=()):
        i = len(self.ops)
        deps = set(extra)
        xr = [b for b in reads if b.excl and eng != "pe"]
        if xr:
            reads = [b for b in reads if not (b.excl and eng != "pe")]
            writes = list(writes) + xr
        for b in reads:
            if b.w is not None:
                deps.add(b.w)
        for b in writes:
            if b.w is not None:
                deps.add(b.w)
            deps.update(b.r)
        for b in reads:
            b.r.append(i)
        for b in writes:
            b.w = i
            b.r = []
        val = None
        if stream is not None:
            n = self.stream_cnt.get(stream, 0)
            self.stream_cnt[stream] = n + 1
            R = self.stream_R[stream]
            stream = "%s%d" % (stream, n % R)
            val = 16 * (n // R + 1)
            prev = self.stream_last.get(stream)
            if prev is not None:
                deps.add(prev)
            self.stream_last[stream] = i
        elif fn is not None:
            self.last_real[eng] = i
        self.ops.append([eng, fn, deps, stream, False, val])
        return i

    def barrier(self):
        ex = set(self.last_real.values()) | set(self.stream_last.values())
        for e in self.ENGS:
            self.add(e, None, extra=ex)

    def dma(self, q, out, in_, R=(), W=(), stream=None, **kw):
        if stream is None:
            stream = "st" if q == "pool" else "ld"
        return self.add(q, lambda e: e.dma_start(out=out, in_=in_, **kw), R, W, stream=stream)

    def mm(self, out, lhsT, rhs, start, stop, R, W):
        return self.add("pe", lambda e: e.matmul(out, lhsT=lhsT, rhs=rhs, start=start, stop=stop), R, W)

    def tr(self, out, in_, ident, R, W):
        return self.add("pe", lambda e: e.transpose(out=out, in_=in_, identity=ident), R, W)

    def act(self, out, in_, func, R, W, **kw):
        return self.add("act", lambda e: e.activation(out=out, in_=in_, func=func, **kw), R, W)

    def ts(self, eng, out, in0, s1, s2, op0, op1, R, W):
        if s2 is None:
            return self.add(eng, lambda e: e.tensor_scalar(out=out, in0=in0, scalar1=s1, scalar2=None, op0=op0), R, W)
        return self.add(eng, lambda e: e.tensor_scalar(out=out, in0=in0, scalar1=s1, scalar2=s2, op0=op0, op1=op1), R, W)

    def tt(self, eng, out, in0, in1, op, R, W):
        return self.add(eng, lambda e: e.tensor_tensor(out=out, in0=in0, in1=in1, op=op), R, W)

    def stt(self, out, in0, scalar, in1, op0, op1, R, W):
        return self.add("dve", lambda e: e.scalar_tensor_tensor(out=out, in0=in0, scalar=scalar, in1=in1, op0=op0, op1=op1), R, W)

    def copy(self, eng, out, in_, R, W):
        if eng == "act":
            return self.add("act", lambda e: e.activation(out=out, in_=in_, func=AF.Copy), R, W)
        return self.add(eng, lambda e: e.tensor_copy(out=out, in_=in_), R, W)

    def memset(self, eng, out, val, W):
        return self.add(eng, lambda e: e.memset(out, val), (), W)

    def recip(self, out, in_, R, W):
        return self.add("dve", lambda e: e.reciprocal(out=out, in_=in_), R, W)

    def emit(self, sems):
        ops = self.ops
        for (eng, fn, deps, stream, sig, val) in ops:
            for d in deps:
                po = ops[d]
                if po[3] is None:
                    if po[0] == "pe" and eng == "pe":
                        continue
                    po[4] = True
        cnt = {e: 0 for e in self.ENGS}
        for o in ops:
            if o[3] is None and o[4]:
                cnt[o[0]] += 1
                o[5] = cnt[o[0]]
        per_eng = {e: [] for e in self.ENGS}
        waited = {e: {} for e in self.ENGS}
        for (eng, fn, deps, stream, sig, val) in ops:
            need = {}
            for d in deps:
                po = ops[d]
                if po[3] is None:
                    if po[0] == "pe" and eng == "pe":
                        continue
                    if po[1] is None:
                        continue
                    key = "e_" + po[0]
                else:
                    key = "s_" + po[3]
                need[key] = max(need.get(key, 0), po[5])
            waits = []
            for key, v in need.items():
                if waited[eng].get(key, 0) >= v:
                    continue
                waited[eng][key] = v
                waits.append((key, v))
            per_eng[eng].append((waits, fn, stream, sig))
        self.n_inst = {e: len(v) for e, v in per_eng.items()}

        def run(engobj, lst, ename):
            for (waits, fn, stream, sig) in lst:
                for (key, v) in waits:
                    engobj.wait_ge(sems[key], v)
                if fn is None:
                    continue
                ins = fn(engobj)
                if stream is not None:
                    ins.then_inc(sems["s_" + stream], 16)
                elif sig:
                    ins.then_inc(sems["e_" + ename], 1)

        with self.nc.Block() as block:
            @block.tensor
            def _(e):
                run(e, per_eng["pe"], "pe")

            @block.scalar
            def _(e):
                run(e, per_eng["act"], "act")

            @block.vector
            def _(e):
                run(e, per_eng["dve"], "dve")

            @block.gpsimd
            def _(e):
                run(e, per_eng["pool"], "pool")

            @block.sync
            def _(e):
                run(e, per_eng["sp"], "sp")


def host_consts():
    c = {}
    c["ident"] = np.eye(128, dtype=np.float32)
    c["ones"] = np.ones((128, 128), np.float32)
    idx = np.arange(128)
    c["tri"] = np.stack([(idx[:, None] <= idx[None, :]), (idx[:, None] >= idx[None, :]),
                         (idx[:, None] > idx[None, :]), (idx[:, None] < idx[None, :])]).astype(np.float32)
    rm = np.zeros((128, 128), np.float32)
    for j in range(32):
        rm[32 + j, j] = -1.0
        rm[j, 32 + j] = 1.0
        rm[96 + j, 64 + j] = -1.0
        rm[64 + j, 96 + j] = 1.0
    c["rotm"] = rm
    t = np.arange(L)
    row = (t // 64).astype(np.float32)
    col = (t % 64).astype(np.float32)
    inv = (10000.0 ** (-np.arange(32, dtype=np.float32) / 32)).astype(np.float32)
    ar = row[:, None] * inv
    ac = col[:, None] * inv
    ang = np.concatenate([ar, ar, ac, ac], axis=-1)
    c["cosT"] = np.ascontiguousarray(np.cos(ang).T.astype(np.float32))
    c["sinT"] = np.ascontiguousarray(np.sin(ang).T.astype(np.float32))
    cc = np.arange(64)
    cstart = np.clip(cc - 8, 0, 48)
    ok = (cc[:, None] >= cstart[None, :]) & (cc[:, None] < cstart[None, :] + 16)
    c["namask"] = np.where(ok, 0.0, NEG).astype(np.float32)
    return c


def rpb_table(na_rpb):
    cc = np.arange(64)
    dc = np.clip(cc[:, None] - cc[None, :], -15, 15) + 15
    return np.ascontiguousarray(na_rpb[:, :, :, dc]).astype(np.float32)


class Ctx:
    pass


def build(cfg=None):
    cfg = cfg or {}
    layers = cfg.get("layers", list(range(DEPTH)))
    dbg = cfg.get("debug", ())
    nc = bass.Bass("TRN2", target_bir_lowering=False)
    g = Ctx()
    g.nc = nc
    g.cfg = cfg

    def din(name, shape, dt=F32):
        return nc.dram_tensor(name, list(shape), dt, kind="ExternalInput").ap()

    def dscr(name, shape, dt=F32):
        kind = "ExternalOutput" if name in dbg else "Internal"
        return nc.dram_tensor(name, list(shape), dt, kind=kind).ap()

    hc = host_consts()
    shapes = {"x": [L, D], "ctx": [LC, D], "cvec": [2, D], "w_ada": [4, D, 6 * D], "b_ada": [4, 6 * D],
              "g_norm_mix": [4, D], "g_norm_ffn": [4, D], "w_in_even": [2, D, EVEN_IN], "w_out_even": [2, D, D],
              "rpb_tab": [2, 8, 15, 64, 64], "dn_conv": [2, 3, 1536], "dn_a_log": [2, 8], "dn_dt_bias": [2, 8],
              "dn_g_out": [2, 128], "w_in_odd": [2, D, 6144], "gm_g_v": [2, 3072], "gm_wsT": [2, 128, 8, 128],
              "gm_bsT": [2, 128, 8], "w_out_odd": [2, 3072, D], "w_ff1": [4, D, 4096], "w_ff2": [4, 4096, D],
              "g_final": [D]}
    for k, v in hc.items():
        shapes[k] = list(v.shape)

    class LazyIn(dict):
        def __missing__(self, k):
            self[k] = din(k, shapes[k])
            return self[k]
    I = g.I = LazyIn()
    if not cfg.get("lazy"):
        for k in shapes:
            I[k]
    g.out = nc.dram_tensor("out", [L, D], F32, kind="ExternalOutput").ap()

    S = g.S = {}
    S["X"] = dscr("X", [NTOK, D])
    S["MODV"] = dscr("MODV", [4, 2, 6, D])
    S["QK"] = dscr("QK", [8, 128, NTOK], BF16)
    S["VA"] = dscr("VA", [NTOK, 520], BF16)
    S["DNRAW"] = dscr("DNRAW", [12, 128, NPAD])
    S["GATE"] = dscr("GATE", [NTOK, 512])
    S["BA"] = dscr("BA", [NTOK, 16])
    S["DNP"] = dscr("DNP", [NT, 8, 128, 648])
    S["OF"] = dscr("OF", [NTOK, 512])
    S["ZT"] = dscr("ZT", [8, 128, NTOK], BF16)
    S["MIX"] = dscr("MIX", [NTOK, 3072])

    p = g.p = Prog(nc)
    with ExitStack() as es:
        sems = {n: es.enter_context(nc.semaphore(n)) for n in p.sem_names()}
        NW = 53000
        g.big = es.enter_context(nc.sbuf_tensor("big", [128, NW], F32))
        g.NW = NW
        g.ps = [es.enter_context(nc.psum_tensor("ps%d" % i, [128, 512], F32)) for i in range(8)]
        g.persist = 0
        g.off = 0
        g.DB = {k: Buf(k) for k in list(S.keys()) + ["out"]}
        phase_setup(g)
        for l in layers:
            if "nomix" in cfg:
                pass
            elif l % 2 == 0:
                phase_even(g, l)
            else:
                phase_odd(g, l)
            if "noffn" not in cfg:
                phase_ffn(g, l)
        if "nofinal" not in cfg:
            phase_final(g)
        p.add("sp", None, extra=set(p.stream_last.values()) | set(p.last_real.values()))
        p.emit(sems)
    g.n_inst = p.n_inst
    return nc, g


def alloc(g, free_shape, dt=F32, persist=False):
    n = int(np.prod(free_shape))
    words = n if dt == F32 else (n + 1) // 2
    words = (words + 7) // 8 * 8
    a = g.big[:, g.off:g.off + words]
    g.off += words
    assert g.off <= g.NW, "SBUF overflow %d" % g.off
    if persist:
        g.persist = g.off
    if dt != F32:
        a = a.bitcast(dt)
    a = a[:, 0:n]
    if len(free_shape) == 2:
        a = a.rearrange("p (a b) -> p a b", a=free_shape[0])
    elif len(free_shape) == 3:
        a = a.rearrange("p (a b c) -> p a b c", a=free_shape[0], b=free_shape[1])
    return a


def new_phase(g):
    g.p.barrier()
    g.off = g.persist
    g.PB = [Buf("ps%d" % i, excl=True) for i in range(8)]
    g.bank_i = 0


def bank(g):
    i = g.bank_i % 8
    g.bank_i += 1
    return g.ps[i], g.PB[i]


def phase_setup(g):
    p, I, S = g.p, g.I, g.S
    g.PB = [Buf("ps%d" % i, excl=True) for i in range(8)]
    g.bank_i = 0
    g.identF = alloc(g, [128], F32, persist=True)
    g.identB = alloc(g, [128], BF16, persist=True)
    g.onesF = alloc(g, [128], F32, persist=True)
    g.Bconst = Buf("const")
    p.dma("sp", g.identF, I["ident"][:, :], W=[g.Bconst])
    p.dma("sp", g.onesF, I["ones"][:, :], W=[g.Bconst])
    p.copy("dve", g.identB, g.identF, [g.Bconst], [g.Bconst])
    p.dma("sp", S["X"][0:LC, :], I["ctx"][:, :], W=[g.DB["X"]])
    p.dma("sp", S["X"][LC:NTOK, :], I["x"][:, :], W=[g.DB["X"]])
    z = alloc(g, [12, 4], F32)
    Bz = Buf()
    p.memset("dve", z, 0.0, [Bz])
    for col in (0, 257, 258, NPAD - 1):
        p.dma("sp", S["DNRAW"][:, :, col:col + 1].rearrange("c p t -> p c t"), z[:, :, 0:1], R=[Bz], W=[g.DB["DNRAW"]],
              allow_slow_non_contiguous=True)
    if g.cfg.get("ntiles"):
        zz = alloc(g, [12, 512], F32)
        Bzz = Buf()
        p.memset("dve", zz, 0.0, [Bzz])
        for c0 in range(0, NPAD, 512):
            c1 = min(NPAD, c0 + 512)
            p.dma("sp", S["DNRAW"][:, :, c0:c1].rearrange("c p t -> p c t"), zz[:, :, 0:c1 - c0], R=[Bzz], W=[g.DB["DNRAW"]])
        zb = alloc(g, [8, 520], BF16)
        p.memset("dve", zb, 0.0, [Bzz])
        for c0 in range(0, NTOK, 512):
            c1 = min(NTOK, c0 + 512)
            p.dma("sp", S["QK"][:, :, c0:c1].rearrange("c p t -> p c t"), zb[:, :, 0:c1 - c0], R=[Bzz], W=[g.DB["QK"]])
        for t_ in range(NT):
            p.dma("sp", S["VA"][t_ * 128:(t_ + 1) * 128, :], zb[:, 0, :], R=[Bzz], W=[g.DB["VA"]])
    cf = alloc(g, [8, 2], F32)
    Bcf = Buf()
    for s_ in range(2):
        p.dma("sp", cf[:, :, s_], I["cvec"][s_, :].rearrange("(k p) -> p k", p=128), W=[Bcf], allow_slow_non_contiguous=True)
    p.act(cf, cf, AF.Silu, [Bcf], [Bcf])
    wb = [alloc(g, [8, 512], F32) for _ in range(3)]
    Bw = [Buf() for _ in range(3)]
    mrow = alloc(g, [6 * D], F32)
    brow = alloc(g, [6 * D], F32)
    grow = alloc(g, [2, D], F32)
    tmp = alloc(g, [D], F32)
    Bm, Bb, Bg, Bt = Buf(), Buf(), Buf(), Buf()
    it = 0
    for l in ([] if g.cfg.get("nomod") else g.cfg.get("layers", list(range(DEPTH)))):
        p.dma("sp", brow[0:2, :], I["b_ada"][l, :].partition_broadcast(2), W=[Bb])
        p.dma("sp", grow[0:2, 0, :], I["g_norm_mix"][l, :].partition_broadcast(2), W=[Bg])
        p.dma("sp", grow[0:2, 1, :], I["g_norm_ffn"][l, :].partition_broadcast(2), W=[Bg])
        for n in range(12):
            w_, bw_ = wb[it % 3], Bw[it % 3]
            it += 1
            p.dma("sp", w_, I["w_ada"][l, :, n * 512:(n + 1) * 512].rearrange("(k p) n -> p k n", p=128), W=[bw_])
            ps, pb = bank(g)
            for k in range(8):
                p.mm(ps[0:2, :], cf[:, k, :], w_[:, k, :], k == 0, k == 7, [Bcf, bw_], [pb])
            p.tt("dve", mrow[0:2, n * 512:(n + 1) * 512], ps[0:2, :], brow[0:2, n * 512:(n + 1) * 512], ALU.add, [pb, Bb], [Bm])
        for j, (sc_i, sh_i, gt_i) in enumerate([(1, 0, 2), (4, 3, 5)]):
            p.stt(tmp[0:2, :], mrow[0:2, sc_i * D:(sc_i + 1) * D], 1.0, grow[0:2, j, :], ALU.add, ALU.mult, [Bm, Bg], [Bt])
            p.dma("sp", S["MODV"][l, :, 3 * j + 0, :], tmp[0:2, :], R=[Bt], W=[g.DB["MODV"]])
            p.dma("sp", S["MODV"][l, :, 3 * j + 1, :], mrow[0:2, sh_i * D:(sh_i + 1) * D], R=[Bm], W=[g.DB["MODV"]])
            p.dma("sp", S["MODV"][l, :, 3 * j + 2, :], mrow[0:2, gt_i * D:(gt_i + 1) * D], R=[Bm], W=[g.DB["MODV"]])


def load_mod(g, l, stream, which):
    p, S = g.p, g.S
    A = alloc(g, [D], F32)
    sh = alloc(g, [D], F32)
    B = Buf()
    p.dma("sp", A, S["MODV"][l, stream, 3 * which + 0, :].partition_broadcast(128), R=[g.DB["MODV"]], W=[B])
    p.dma("sp", sh, S["MODV"][l, stream, 3 * which + 1, :].partition_broadcast(128), R=[g.DB["MODV"]], W=[B])
    return A, sh, B


def load_gate(g, l, stream, which):
    p, S = g.p, g.S
    gt = alloc(g, [D], F32)
    B = Buf()
    p.dma("sp", gt, S["MODV"][l, stream, 3 * which + 2, :].partition_broadcast(128), R=[g.DB["MODV"]], W=[B])
    return gt, B


class NormBufs:
    def __init__(self, g, nbuf=2):
        self.n = nbuf
        self.x = [alloc(g, [D], F32) for _ in range(nbuf)]
        self.Bx = [Buf() for _ in range(nbuf)]
        self.tm = [alloc(g, [D], F32) for _ in range(nbuf)]
        self.Btm = [Buf() for _ in range(nbuf)]
        self.xn = [alloc(g, [D], BF16) for _ in range(nbuf)]
        self.Bxn = [Buf() for _ in range(nbuf)]
        self.hT = [alloc(g, [8, 128], BF16) for _ in range(nbuf)]
        self.BhT = [Buf() for _ in range(nbuf)]
        self.st = [alloc(g, [4], F32) for _ in range(nbuf)]
        self.Bst = [Buf() for _ in range(nbuf)]


def norm_T(g, nb, i, t, mods):
    p, S = g.p, g.S
    x, Bx, xn, Bxn, hT, BhT, st, Bst, tm, Btm = (nb.x[i], nb.Bx[i], nb.xn[i], nb.Bxn[i], nb.hT[i], nb.BhT[i],
                                                 nb.st[i], nb.Bst[i], nb.tm[i], nb.Btm[i])
    A, sh, Bmod = mods
    p.dma("sp", x, S["X"][t * 128:(t + 1) * 128, :], R=[g.DB["X"]], W=[Bx])
    p.act(tm, x, AF.Square, [Bx], [Btm, Bst], accum_out=st[:, 0:1])
    p.act(st[:, 1:2], st[:, 0:1], AF.Sqrt, [Bst], [Bst], scale=1.0 / D, bias=EPS)
    p.recip(st[:, 2:3], st[:, 1:2], [Bst], [Bst])
    p.stt(tm, x, st[:, 2:3], A, ALU.mult, ALU.mult, [Bx, Bst, Bmod, Btm], [Btm])
    p.tt("pool", xn, tm, sh, ALU.add, [Btm, Bmod], [Bxn])
    ps, pb = bank(g)
    psb = ps[:, :].bitcast(BF16)
    for k in range(8):
        p.tr(psb[:, k * 128:(k + 1) * 128], xn[:, k * 128:(k + 1) * 128], g.identB, [Bxn, g.Bconst], [pb])
    p.copy("act", hT, psb.rearrange("p (a b) -> p a b", a=8), [pb], [BhT])
    return hT, BhT


def pipelined(tiles, stage0, stage1):
    if not tiles:
        return
    nxt = stage0(0, tiles[0])
    for it, t in enumerate(tiles):
        cur = nxt
        if it + 1 < len(tiles):
            nxt = stage0(it + 1, tiles[it + 1])
        stage1(it, t, cur)


def load_w_bf16(g, dst, src_ap, Bw, max_cols=2048):
    p = g.p
    K, N = dst.shape[1], dst.shape[2]
    step = max_cols
    for k in range(K):
        for n0 in range(0, N, step):
            n1 = min(N, n0 + step)
            p.dma("pool", dst[:, k, n0:n1], src_ap[:, k, n0:n1], W=[Bw], stream="wl")


def phase_ffn(g, l):
    p, I, S = g.p, g.I, g.S
    new_phase(g)
    tiles = tile_list(g, l, "ffn")
    W1 = alloc(g, [8, 4096], BF16)
    W2 = alloc(g, [32, 1024], BF16)
    BW1, BW2 = Buf(), Buf()
    load_w_bf16(g, W1, I["w_ff1"][l].rearrange("(k p) n -> p k n", p=128), BW1)
    load_w_bf16(g, W2, I["w_ff2"][l].rearrange("(k p) n -> p k n", p=128), BW2)
    mods = [load_mod(g, l, s_, 1) for s_ in range(2)]
    gates = [load_gate(g, l, s_, 1) for s_ in range(2)]
    nb = NormBufs(g)
    r_ = [alloc(g, [512], F32) for _ in range(2)]
    Br = [Buf() for _ in range(2)]
    aT = [alloc(g, [32, 128], BF16) for _ in range(2)]
    BaT = [Buf() for _ in range(2)]
    tmp = [alloc(g, [512], F32) for _ in range(2)]
    Btmp = [Buf() for _ in range(2)]
    ri = [0]

    def stage0(it, t):
        return norm_T(g, nb, it % 2, t, mods[1 if t < 2 else 0])

    def stage1(it, t, cur):
        i = it % 2
        s_ = 1 if t < 2 else 0
        hT, BhT = cur
        for mb in range(8):
            ps, pb = bank(g)
            for mm_ in range(4):
                m = mb * 4 + mm_
                for k in range(8):
                    p.mm(ps[:, mm_ * 128:(mm_ + 1) * 128], W1[:, k, m * 128:(m + 1) * 128], hT[:, k, :], k == 0, k == 7, [BW1, BhT], [pb])
            rr, brr = r_[ri[0] % 2], Br[ri[0] % 2]
            ri[0] += 1
            p.act(rr, ps[:, :], AF.Relu, [pb], [brr])
            eng = "dve"
            p.tt(eng, aT[i][:, mb * 4:(mb + 1) * 4, :], rr.rearrange("p (a b) -> p a b", a=4), rr.rearrange("p (a b) -> p a b", a=4), ALU.mult, [brr], [BaT[i]])
        gt, Bgt = gates[s_]
        for n in range(2):
            ps, pb = bank(g)
            for k in range(32):
                p.mm(ps[:, :], aT[i][:, k, :], W2[:, k, n * 512:(n + 1) * 512], k == 0, k == 31, [BaT[i], BW2], [pb])
            p.tt("dve", tmp[n], ps[:, :], gt[:, n * 512:(n + 1) * 512], ALU.mult, [pb, Bgt], [Btmp[n]])
            p.tt("pool", nb.x[i][:, n * 512:(n + 1) * 512], tmp[n], nb.x[i][:, n * 512:(n + 1) * 512], ALU.add, [Btmp[n], nb.Bx[i]], [nb.Bx[i]])
        p.dma("pool", S["X"][t * 128:(t + 1) * 128, :], nb.x[i], R=[nb.Bx[i]], W=[g.DB["X"]])

    pipelined(tiles, stage0, stage1)


def tile_list(g, l, kind):
    lim = g.cfg.get("ntiles")
    lat = list(range(2, NT))
    if lim:
        lat = lat[:lim]
    if kind in ("ffn", "mixout"):
        ctx = [0, 1] if l < 2 else []
    elif kind == "proj":
        ctx = [0, 1] if l < 3 else []
    else:
        ctx = []
    return ctx + lat


def phase_final(g):
    p, I, S = g.p, g.I, g.S
    new_phase(g)
    gf = alloc(g, [D], F32)
    Bg = Buf()
    p.dma("sp", gf, I["g_final"].partition_broadcast(128), W=[Bg])
    x = [alloc(g, [D], F32) for _ in range(2)]
    Bx = [Buf() for _ in range(2)]
    y = [alloc(g, [D], F32) for _ in range(2)]
    By = [Buf() for _ in range(2)]
    st = [alloc(g, [4], F32) for _ in range(2)]
    Bst = [Buf() for _ in range(2)]
    junk = alloc(g, [D], BF16)
    Bj = Buf()
    for it, t in enumerate(tile_list(g, 3, "lat")):
        i = it % 2
        p.dma("sp", x[i], S["X"][t * 128:(t + 1) * 128, :], R=[g.DB["X"]], W=[Bx[i]])
        p.act(junk, x[i], AF.Square, [Bx[i]], [Bj, Bst[i]], accum_out=st[i][:, 0:1])
        p.act(st[i][:, 1:2], st[i][:, 0:1], AF.Sqrt, [Bst[i]], [Bst[i]], scale=1.0 / D, bias=EPS)
        p.recip(st[i][:, 2:3], st[i][:, 1:2], [Bst[i]], [Bst[i]])
        p.stt(y[i], x[i], st[i][:, 2:3], gf, ALU.mult, ALU.mult, [Bx[i], Bst[i], Bg], [By[i]])
        p.dma("pool", g.out[(t - 2) * 128:(t - 1) * 128, :], y[i], R=[By[i]], W=[g.DB["out"]])


def phase_even(g, l):
    e = l // 2
    ctx_out = (l == 0)
    st = g.cfg.get("even_stages", "ABCND")
    if "A" in st:
        even_proj(g, l, e)
    if "B" in st:
        even_dnprep(g, l, e)
    if "C" in st:
        even_dnscan(g, l, e, ctx_out)
    if "N" in st:
        even_na(g, l, e, ctx_out)
    if "D" in st:
        even_out(g, l, e)


def dn_col0(t):
    return 1 + t * 128 if t < 2 else 259 + (t - 2) * 128


def even_proj(g, l, e):
    p, I, S = g.p, g.I, g.S
    new_phase(g)
    tiles = tile_list(g, l, "proj")
    W = alloc(g, [8, EVEN_IN], BF16)
    BW = Buf()
    load_w_bf16(g, W, I["w_in_even"][e].rearrange("(k p) n -> p k n", p=128), BW, max_cols=1800)
    mods = [load_mod(g, l, s_, 0) for s_ in range(2)]
    nb = NormBufs(g)
    qk_sb = [alloc(g, [8, 128], BF16) for _ in range(2)]
    va_sb = [alloc(g, [8, 65], BF16) for _ in range(2)]
    dn_sb = [alloc(g, [12, 128], F32) for _ in range(2)]
    gt_sb = [alloc(g, [512], F32) for _ in range(2)]
    ba_sb = [alloc(g, [16], F32) for _ in range(2)]
    Bqk, Bva, Bdn, Bgt, Bba = [[Buf() for _ in range(2)] for _ in range(5)]
    for i in range(2):
        p.memset("pool", va_sb[i][:, :, 64:65], 1.0, [Bva[i]])
    def stage0(it, t):
        return norm_T(g, nb, it % 2, t, mods[1 if t < 2 else 0])

    def stage1(it, t, cur):
        i = it % 2
        s_ = 1 if t < 2 else 0
        hT, BhT = cur
        def fm_bank(col0, nch):
            ps, pb = bank(g)
            for cc in range(nch):
                for k in range(8):
                    p.mm(ps[:, cc * 128:(cc + 1) * 128], W[:, k, col0 + cc * 128: col0 + (cc + 1) * 128], hT[:, k, :], k == 0, k == 7, [BW, BhT], [pb])
            return ps, pb
        ps, pb = fm_bank(0, 4)
        p.act(qk_sb[i][:, 0:4, :], ps[:, :].rearrange("p (a b) -> p a b", a=4), AF.Copy, [pb], [Bqk[i]], scale=0.125)
        ps, pb = fm_bank(512, 4)
        p.copy("dve", qk_sb[i][:, 4:8, :], ps[:, :].rearrange("p (a b) -> p a b", a=4), [pb], [Bqk[i]])
        p.dma("pool", S["QK"][:, :, t * 128:(t + 1) * 128].rearrange("c p t -> p c t"), qk_sb[i], R=[Bqk[i]], W=[g.DB["QK"]])
        ps, pb = bank(g)
        for k in range(8):
            p.mm(ps[:, :], hT[:, k, :], W[:, k, 1024:1536], k == 0, k == 7, [BhT, BW], [pb])
        p.copy("dve", va_sb[i][:, :, 0:64], ps[:, :].rearrange("p (a b) -> p a b", a=8), [pb], [Bva[i]])
        p.dma("pool", S["VA"][t * 128:(t + 1) * 128, :], va_sb[i].rearrange("p a b -> p (a b)"), R=[Bva[i]], W=[g.DB["VA"]])
        for q in range(3):
            ps, pb = fm_bank(1536 + q * 512, 4)
            dst = dn_sb[i][:, q * 4:(q + 1) * 4, :]
            if q == 1:
                p.copy("dve", dst, ps[:, :].rearrange("p (a b) -> p a b", a=4), [pb], [Bdn[i]])
            else:
                p.copy("act", dst, ps[:, :].rearrange("p (a b) -> p a b", a=4), [pb], [Bdn[i]])
        c0 = dn_col0(t)
        p.dma("pool", S["DNRAW"][:, :, c0:c0 + 128].rearrange("c p t -> p c t"), dn_sb[i], R=[Bdn[i]], W=[g.DB["DNRAW"]])
        ps, pb = bank(g)
        for k in range(8):
            p.mm(ps[:, :], hT[:, k, :], W[:, k, 3072:3584], k == 0, k == 7, [BhT, BW], [pb])
        p.copy("act", gt_sb[i], ps[:, :], [pb], [Bgt[i]])
        p.dma("pool", S["GATE"][t * 128:(t + 1) * 128, :], gt_sb[i], R=[Bgt[i]], W=[g.DB["GATE"]])
        ps, pb = bank(g)
        for k in range(8):
            p.mm(ps[:, 0:16], hT[:, k, :], W[:, k, 3584:3600], k == 0, k == 7, [BhT, BW], [pb])
        p.copy("dve", ba_sb[i], ps[:, 0:16], [pb], [Bba[i]])
        p.dma("pool", S["BA"][t * 128:(t + 1) * 128, :], ba_sb[i], R=[Bba[i]], W=[g.DB["BA"]])

    pipelined(tiles, stage0, stage1)


def even_dnprep(g, l, e):
    p, I, S = g.p, g.I, g.S
    new_phase(g)
    tiles = tile_list(g, l, "proj")
    Bc = Buf()
    cw = alloc(g, [3, 12], F32)
    cwr = alloc(g, [128], F32)
    Bcw = Buf()
    p.dma("sp", cwr[0:36, :], I["dn_conv"][e].rearrange("j (c p) -> (j c) p", p=128), W=[Bcw])
    ps, pb = bank(g)
    p.tr(ps[:, 0:36], cwr[0:36, :], g.identF[0:36, 0:36], [Bcw, g.Bconst], [pb])
    p.copy("dve", cw.rearrange("p a b -> p (a b)"), ps[:, 0:36], [pb], [Bc])
    tri = alloc(g, [4, 128], F32)
    p.dma("sp", tri, I["tri"].rearrange("m a b -> a m b"), W=[Bc])
    LE, GE, GT, LT = [tri[:, m_, :] for m_ in range(4)]
    rotm = alloc(g, [128], F32)
    p.dma("sp", rotm, I["rotm"][:, :], W=[Bc])
    dtb = alloc(g, [8], F32)
    nexpA = alloc(g, [8], F32)
    p.dma("sp", dtb, I["dn_dt_bias"][e, :].partition_broadcast(128), W=[Bc])
    p.dma("sp", nexpA, I["dn_a_log"][e, :].partition_broadcast(128), W=[Bc])
    p.act(nexpA, nexpA, AF.Exp, [Bc], [Bc])
    p.ts("dve", nexpA, nexpA, -1.0, None, ALU.mult, None, [Bc], [Bc])
    raw = [alloc(g, [12, 130], F32) for _ in range(2)]
    ba = [alloc(g, [16], F32) for _ in range(2)]
    cs = [alloc(g, [2, 128], F32) for _ in range(2)]
    Braw, Bba, Bcs = [[Buf() for _ in range(2)] for _ in range(3)]
    pk = [alloc(g, [8, 648], F32) for _ in range(2)]
    Bpk = [Buf() for _ in range(2)]
    cv = alloc(g, [12, 128], F32)
    tmpc = alloc(g, [128], F32)
    sqb = alloc(g, [1024], F32)
    rn = alloc(g, [1024], F32)
    qkr = alloc(g, [8, 128], F32)
    t1 = alloc(g, [8, 128], F32)
    ktok = alloc(g, [4, 128], F32)
    vtok = alloc(g, [4, 128], F32)
    sm = alloc(g, [12, 8], F32)
    lrep = alloc(g, [8, 128], F32)
    egB = alloc(g, [8, 128], F32)
    mmx = alloc(g, [8, 128], F32)
    dec = alloc(g, [8, 128], F32)
    decI = alloc(g, [8, 128], F32)
    decS = alloc(g, [8, 128], F32)
    aqk = alloc(g, [8, 128], F32)
    bv = alloc(g, [8, 128], F32)
    bek = alloc(g, [8, 128], F32)
    Pb = [alloc(g, [8, 128], F32) for _ in range(2)]
    Qb = [alloc(g, [8, 128], F32) for _ in range(2)]
    Nb = [alloc(g, [8, 128], F32) for _ in range(2)]
    Bcv, Btc, Bsq, Brn, Bqkr, Bt1, Bkt, Bvt, Bsm, Blr, BeB, Bmx, Bdec, BdI, BdS, Baqk, Bbv, Bbek = [Buf() for _ in range(18)]
    BP = [Buf() for _ in range(2)]
    BQ = [Buf() for _ in range(2)]
    BN = [Buf() for _ in range(2)]
    v4 = lambda ps: ps[:, :].rearrange("p (a b) -> p a b", a=4)
    for it, t in enumerate(tiles):
        i = it % 2
        lat = t >= 2
        c0 = dn_col0(t) - 1
        p.dma("sp", raw[i], S["DNRAW"][:, :, c0:c0 + 130].rearrange("c p t -> p c t"), R=[g.DB["DNRAW"]], W=[Braw[i]])
        p.dma("sp", ba[i], S["BA"][t * 128:(t + 1) * 128, :], R=[g.DB["BA"]], W=[Bba[i]])
        if lat:
            p.dma("sp", cs[i][:, 0, :], I["cosT"][:, (t - 2) * 128:(t - 1) * 128], W=[Bcs[i]])
            p.dma("sp", cs[i][:, 1, :], I["sinT"][:, (t - 2) * 128:(t - 1) * 128], W=[Bcs[i]])
        for c in range(12):
            p.ts("dve", cv[:, c, :], raw[i][:, c, 0:128], cw[:, 0, c:c + 1], None, ALU.mult, None, [Braw[i], Bc], [Bcv])
            p.stt(cv[:, c, :], raw[i][:, c, 1:129], cw[:, 1, c:c + 1], cv[:, c, :], ALU.mult, ALU.add, [Braw[i], Bc, Bcv], [Bcv])
            p.stt(cv[:, c, :], raw[i][:, c, 2:130], cw[:, 2, c:c + 1], cv[:, c, :], ALU.mult, ALU.add, [Braw[i], Bc, Bcv], [Bcv])
        p.act(cv, cv, AF.Silu, [Bcv], [Bcv])
        if g.cfg.get("cutB", 99) <= 1:
            continue
        cvf = cv.rearrange("p a b -> p (a b)")
        p.act(sqb, cvf[:, 0:1024], AF.Square, [Bcv], [Bsq])
        for n in range(2):
            ps, pb = bank(g)
            p.mm(ps[:, :], g.onesF, sqb[:, n * 512:(n + 1) * 512], True, True, [g.Bconst, Bsq], [pb])
            if n == 0:
                p.act(rn[:, 0:512], ps[:, :], AF.Sqrt, [pb], [Brn], scale=128.0, bias=EPS * 128.0)
            else:
                p.act(rn[:, 512:1024], ps[:, :], AF.Sqrt, [pb], [Brn], scale=1.0, bias=EPS)
        p.recip(rn, rn, [Brn], [Brn])
        qk0 = cv[:, 0:8, :]
        p.tt("pool", qk0, qk0, rn.rearrange("p (a b) -> p a b", a=8), ALU.mult, [Bcv, Brn], [Bcv])
        if g.cfg.get("cutB", 99) <= 2:
            continue
        if lat:
            for n in range(2):
                ps, pb = bank(g)
                p.mm(ps[:, :], rotm, cvf[:, n * 512:(n + 1) * 512], True, True, [Bc, Bcv], [pb])
                p.tt("dve", t1[:, n * 4:(n + 1) * 4, :], v4(ps), cs[i][:, 1:2, :].broadcast_to([128, 4, 128]), ALU.mult, [pb, Bcs[i]], [Bt1])
            p.tt("pool", qkr, qk0, cs[i][:, 0:1, :].broadcast_to([128, 8, 128]), ALU.mult, [Bcv, Bcs[i]], [Bqkr])
            p.tt("pool", qkr, qkr, t1, ALU.add, [Bqkr, Bt1], [Bqkr])
            QK_, BQK_ = qkr, Bqkr
        else:
            QK_, BQK_ = qk0, Bcv
        qT = lambda h: QK_[:, h, :]
        kT = lambda h: QK_[:, 4 + h, :]
        if g.cfg.get("cutB", 99) <= 3:
            continue
        ps, pb = bank(g)
        for h in range(4):
            p.tr(ps[:, h * 128:(h + 1) * 128], kT(h), g.identF, [BQK_, g.Bconst], [pb])
        p.copy("act", ktok, v4(ps), [pb], [Bkt])
        ps, pb = bank(g)
        for h in range(4):
            p.tr(ps[:, h * 128:(h + 1) * 128], cv[:, 8 + h, :], g.identF, [Bcv, g.Bconst], [pb])
        p.copy("dve", vtok, v4(ps), [pb], [Bvt])
        if g.cfg.get("cutB", 99) <= 4:
            continue
        beta, nbeta, z, logg, gam, eg, be, glg, kds, glv = [sm[:, r_, :] for r_ in range(10)]
        p.act(beta, ba[i][:, 0:8], AF.Sigmoid, [Bba[i]], [Bsm])
        p.ts("dve", nbeta, beta, -1.0, None, ALU.mult, None, [Bsm], [Bsm])
        p.tt("pool", z, ba[i][:, 8:16], dtb, ALU.add, [Bba[i], Bc], [Bsm])
        p.act(z, z, AF.Exp, [Bsm], [Bsm])
        p.act(z, z, AF.Ln, [Bsm], [Bsm], bias=1.0)
        p.tt("pool", logg, z, nexpA, ALU.mult, [Bsm, Bc], [Bsm])
        ps, pb = bank(g)
        p.mm(ps[:, 0:4], LE, logg[:, 0:4], True, True, [Bc, Bsm], [pb])
        p.mm(ps[:, 4:8], GE, logg[:, 4:8], True, True, [Bc, Bsm], [pb])
        p.copy("dve", gam, ps[:, 0:8], [pb], [Bsm])
        if g.cfg.get("cutB", 99) <= 4.1:
            continue
        p.copy("pool", lrep, logg.unsqueeze(2).broadcast_to([128, 8, 128]), [Bsm], [Blr])
        gps = []
        for d_ in range(2):
            ps, pb = bank(g)
            for h in range(4):
                p.mm(ps[:, h * 128:(h + 1) * 128], lrep[:, d_ * 4 + h, :], LE if d_ == 0 else GE, True, True, [Blr, Bc], [pb])
            gps.append((ps, pb))
        if g.cfg.get("cutB", 99) <= 4.2:
            continue
        p.act(eg, gam, AF.Exp, [Bsm], [Bsm])
        p.tt("pool", be, beta, eg, ALU.mult, [Bsm], [Bsm])
        for d_ in range(2):
            ps, pb = gps[d_]
            last = 127 if d_ == 0 else 0
            p.act(egB[:, d_ * 4:(d_ + 1) * 4, :], v4(ps), AF.Exp, [pb], [BeB])
            p.copy("dve", glg[:, d_ * 4:(d_ + 1) * 4], v4(ps)[:, :, last], [pb], [Bsm])
            for h in range(4):
                dh = d_ * 4 + h
                p.ts("dve", mmx[:, dh, :], ps[:, h * 128:(h + 1) * 128], gam[:, dh:dh + 1], 0.0, ALU.subtract, ALU.max, [pb, Bsm], [Bmx])
        if g.cfg.get("cutB", 99) <= 4.3:
            continue
        p.tt("pool", kds, glg, gam, ALU.subtract, [Bsm], [Bsm])
        p.act(kds, kds, AF.Exp, [Bsm], [Bsm])
        p.act(glv, glg, AF.Exp, [Bsm], [Bsm])
        if g.cfg.get("cutB", 99) <= 4.4:
            continue
        p.act(dec, mmx, AF.Exp, [Bmx], [Bdec], scale=-1.0)
        for d_ in range(2):
            sl_ = slice(d_ * 4, (d_ + 1) * 4)
            p.tt("pool", decI[:, sl_, :], dec[:, sl_, :], (GE if d_ == 0 else LE).unsqueeze(1).broadcast_to([128, 4, 128]), ALU.mult, [Bdec, Bc], [BdI])
            p.tt("pool", decS[:, sl_, :], dec[:, sl_, :], (GT if d_ == 0 else LT).unsqueeze(1).broadcast_to([128, 4, 128]), ALU.mult, [Bdec, Bc], [BdS])
        if g.cfg.get("cutB", 99) <= 5:
            continue
        ps_kk, pb_kk = bank(g)
        for h in range(4):
            p.mm(ps_kk[:, h * 128:(h + 1) * 128], kT(h), kT(h), True, True, [BQK_], [pb_kk])
        ps_qk, pb_qk = bank(g)
        for h in range(4):
            p.mm(ps_qk[:, h * 128:(h + 1) * 128], qT(h), kT(h), True, True, [BQK_], [pb_qk])
        Q, P_, N_ = Qb[0], Pb[0], Nb[0]
        for d_ in range(2):
            for h in range(4):
                dh = d_ * 4 + h
                p.stt(Q[:, dh, :], ps_kk[:, h * 128:(h + 1) * 128], nbeta[:, dh:dh + 1], decS[:, dh, :], ALU.mult, ALU.mult, [pb_kk, Bsm, BdS], [BQ[0]])
            p.tt("dve", aqk[:, d_ * 4:(d_ + 1) * 4, :], v4(ps_qk), decI[:, d_ * 4:(d_ + 1) * 4, :], ALU.mult, [pb_qk, BdI], [Baqk])
        if g.cfg.get("cutB", 99) <= 6:
            continue
        for d_ in range(2):
            ps, pb = bank(g)
            for h in range(4):
                p.tr(ps[:, h * 128:(h + 1) * 128], Q[:, d_ * 4 + h, :], g.identF, [BQ[0], g.Bconst], [pb])
            p.copy("act", P_[:, d_ * 4:(d_ + 1) * 4, :], v4(ps), [pb], [BP[0]])
            ps, pb = bank(g)
            for h in range(4):
                p.tr(ps[:, h * 128:(h + 1) * 128], aqk[:, d_ * 4 + h, :], g.identF, [Baqk, g.Bconst], [pb])
            p.copy("dve", pk[i][:, d_ * 4:(d_ + 1) * 4, 256:384], v4(ps), [pb], [Bpk[i]])
        if g.cfg.get("cutB", 99) <= 7:
            continue
        p.tt("pool", N_, P_, g.identF.unsqueeze(1).broadcast_to([128, 8, 128]), ALU.add, [BP[0], g.Bconst], [BN[0]])
        cur = 0
        for lev in range(6):
            nxt = 1 - cur
            lastlev = (lev == 5)
            for d_ in range(2):
                sl_ = slice(d_ * 4, (d_ + 1) * 4)
                if not lastlev:
                    ps, pb = bank(g)
                    for h in range(4):
                        dh = d_ * 4 + h
                        p.mm(ps[:, h * 128:(h + 1) * 128], Qb[cur][:, dh, :], Pb[cur][:, dh, :], True, True, [BQ[cur], BP[cur]], [pb])
                    p.copy("act", Pb[nxt][:, sl_, :], v4(ps), [pb], [BP[nxt]])
                ps, pb = bank(g)
                for h in range(4):
                    dh = d_ * 4 + h
                    p.mm(ps[:, h * 128:(h + 1) * 128], Pb[cur][:, dh, :], Qb[cur][:, dh, :], True, True, [BQ[cur], BP[cur]], [pb])
                p.copy("act" if lastlev else "dve", Qb[nxt][:, sl_, :], v4(ps), [pb], [BQ[nxt]])
            for d_ in range(2):
                sl_ = slice(d_ * 4, (d_ + 1) * 4)
                ps, pb = bank(g)
                for h in range(4):
                    dh = d_ * 4 + h
                    p.mm(ps[:, h * 128:(h + 1) * 128], Qb[nxt][:, dh, :], Nb[cur][:, dh, :], True, True, [BQ[nxt], BN[cur]], [pb])
                p.tt("dve", Nb[nxt][:, sl_, :], v4(ps), Nb[cur][:, sl_, :], ALU.add, [pb, BN[cur]], [BN[nxt]])
            cur = nxt
        TT, BTT = Nb[cur], BN[cur]
        if g.cfg.get("cutB", 99) <= 8:
            continue
        for d_ in range(2):
            for h in range(4):
                dh = d_ * 4 + h
                p.act(bv[:, dh, :], vtok[:, h, :], AF.Copy, [Bvt, Bsm], [Bbv], scale=beta[:, dh:dh + 1])
                p.act(bek[:, dh, :], ktok[:, h, :], AF.Copy, [Bkt, Bsm], [Bbek], scale=be[:, dh:dh + 1])
                p.ts("dve", pk[i][:, dh, 512:640], ktok[:, h, :], kds[:, dh:dh + 1], None, ALU.mult, None, [Bkt, Bsm], [Bpk[i]])
            p.tt("pool", pk[i][:, d_ * 4:(d_ + 1) * 4, 384:512], QK_[:, 0:4, :], egB[:, d_ * 4:(d_ + 1) * 4, :], ALU.mult, [BQK_, BeB], [Bpk[i]])
        p.copy("pool", pk[i][:, :, 640:641], glv.unsqueeze(2), [Bsm], [Bpk[i]])
        for d_ in range(2):
            ps, pb = bank(g)
            for h in range(4):
                dh = d_ * 4 + h
                p.mm(ps[:, h * 128:(h + 1) * 128], TT[:, dh, :], bv[:, dh, :], True, True, [BTT, Bbv], [pb])
            p.copy("act", pk[i][:, d_ * 4:(d_ + 1) * 4, 0:128], v4(ps), [pb], [Bpk[i]])
            ps, pb = bank(g)
            for h in range(4):
                dh = d_ * 4 + h
                p.mm(ps[:, h * 128:(h + 1) * 128], bek[:, dh, :], TT[:, dh, :], True, True, [BTT, Bbek], [pb])
            p.copy("dve", pk[i][:, d_ * 4:(d_ + 1) * 4, 128:256], v4(ps), [pb], [Bpk[i]])
        if g.cfg.get("cutB", 99) <= 9:
            continue
        p.dma("pool", S["DNP"][t].rearrange("d p c -> p d c"), pk[i], R=[Bpk[i]], W=[g.DB["DNP"]])


def even_dnscan(g, l, e, ctx_out):
    p, I, S = g.p, g.I, g.S
    new_phase(g)
    lat_tiles = [t for t in tile_list(g, l, "proj") if t >= 2]
    Bc = Buf()
    goutb = alloc(g, [128], F32)
    p.dma("sp", goutb, I["dn_g_out"][e, :].partition_broadcast(128), W=[Bc])
    Sst = alloc(g, [8, 128], F32)
    BS = [Buf() for _ in range(8)]
    for dh in range(8):
        p.memset("pool", Sst[:, dh, :], 0.0, [BS[dh]])
    NPK = 6
    pk = [alloc(g, [648], F32) for _ in range(NPK)]
    Bpk = [Buf() for _ in range(NPK)]
    u_sb = [alloc(g, [128], F32) for _ in range(4)]
    Bu = [Buf() for _ in range(4)]
    o_sb = [alloc(g, [4, 128], F32) for _ in range(2)]
    Bo = [Buf() for _ in range(2)]
    of_sb = [alloc(g, [4, 128], F32) for _ in range(2)]
    Bof = [Buf() for _ in range(2)]
    gate = [alloc(g, [512], F32) for _ in range(2)]
    Bgate = [Buf() for _ in range(2)]
    y1 = alloc(g, [4, 128], F32)
    ydn = alloc(g, [512], BF16)
    zt = [alloc(g, [4, 128], BF16) for _ in range(2)]
    Bzt = [Buf() for _ in range(2)]
    st = alloc(g, [3, 4], F32)
    junk = alloc(g, [128], BF16)
    By1, Bydn, Bst, Bj = Buf(), Buf(), Buf(), Buf()
    ipk = 0
    iu = 0
    for d_ in range(2):
        order = [0, 1] + lat_tiles if d_ == 0 else [1, 0] + lat_tiles[::-1]
        for it, t in enumerate(order):
            i = it % 2
            want_o = (t >= 2) or ctx_out
            if d_ == 1 and want_o:
                p.dma("sp", of_sb[i].rearrange("p a b -> p (a b)"), S["OF"][t * 128:(t + 1) * 128, :], R=[g.DB["OF"]], W=[Bof[i]])
                p.dma("sp", gate[i], S["GATE"][t * 128:(t + 1) * 128, :], R=[g.DB["GATE"]], W=[Bgate[i]])
            for h in range(4):
                dh = d_ * 4 + h
                pk_, bpk_ = pk[ipk % NPK], Bpk[ipk % NPK]
                ipk += 1
                p.dma("sp", pk_, S["DNP"][t, dh], R=[g.DB["DNP"]], W=[bpk_])
                u_, bu_ = u_sb[iu % 4], Bu[iu % 4]
                iu += 1
                ps1, pb1 = bank(g)
                p.mm(ps1[:, 0:128], pk_[:, 128:256], Sst[:, dh, :], True, True, [bpk_, BS[dh]], [pb1])
                p.tt("dve", u_, pk_[:, 0:128], ps1[:, 0:128], ALU.subtract, [bpk_, pb1], [bu_])
                if want_o:
                    ps2, pb2 = bank(g)
                    p.mm(ps2[:, 0:128], pk_[:, 384:512], Sst[:, dh, :], True, False, [bpk_, BS[dh]], [pb2])
                    p.mm(ps2[:, 0:128], pk_[:, 256:384], u_, False, True, [bpk_, bu_], [pb2])
                ps3, pb3 = bank(g)
                p.mm(ps3[:, 0:128], pk_[:, 512:640], u_, True, True, [bpk_, bu_], [pb3])
                p.stt(Sst[:, dh, :], Sst[:, dh, :], pk_[:, 640:641], ps3[:, 0:128], ALU.mult, ALU.add, [BS[dh], bpk_, pb3], [BS[dh]])
                if want_o:
                    if d_ == 0:
                        p.copy("act", o_sb[i][:, h, :], ps2[:, 0:128], [pb2], [Bo[i]])
                    else:
                        p.tt("dve", o_sb[i][:, h, :], ps2[:, 0:128], of_sb[i][:, h, :], ALU.add, [pb2, Bof[i]], [Bo[i]])
            if not want_o:
                continue
            if d_ == 0:
                p.dma("pool", S["OF"][t * 128:(t + 1) * 128, :], o_sb[i].rearrange("p a b -> p (a b)"), R=[Bo[i]], W=[g.DB["OF"]])
                continue
            for h in range(4):
                p.act(junk, o_sb[i][:, h, :], AF.Square, [Bo[i]], [Bj, Bst], accum_out=st[:, 0, h:h + 1])
            p.act(st[:, 1, :], st[:, 0, :], AF.Sqrt, [Bst], [Bst], scale=1.0 / 128, bias=EPS)
            p.recip(st[:, 2, :], st[:, 1, :], [Bst], [Bst])
            p.act(gate[i], gate[i], AF.Silu, [Bgate[i]], [Bgate[i]])
            for h in range(4):
                p.stt(y1[:, h, :], o_sb[i][:, h, :], st[:, 2, h:h + 1], goutb, ALU.mult, ALU.mult, [Bo[i], Bst, Bc], [By1])
            p.tt("pool", ydn, y1.rearrange("p a b -> p (a b)"), gate[i], ALU.mult, [By1, Bgate[i]], [Bydn])
            ps, pb = bank(g)
            psb = ps[:, :].bitcast(BF16)
            for h in range(4):
                p.tr(psb[:, h * 128:(h + 1) * 128], ydn[:, h * 128:(h + 1) * 128], g.identB, [Bydn, g.Bconst], [pb])
            p.copy("act", zt[i], psb[:, 0:512].rearrange("p (a b) -> p a b", a=4), [pb], [Bzt[i]])
            p.dma("pool", S["ZT"][4:8, :, t * 128:(t + 1) * 128].rearrange("c p t -> p c t"), zt[i], R=[Bzt[i]], W=[g.DB["ZT"]])


def even_na(g, l, e, ctx_out):
    p, I, S = g.p, g.I, g.S
    new_phase(g)
    nrows = 64
    lim = g.cfg.get("ntiles")
    if lim:
        nrows = g.cfg.get("narows") or 2 * lim
    Bc = Buf()
    KT = alloc(g, [4, NTOK], BF16)
    p.dma("sp", KT, S["QK"][4:8, :, :].rearrange("c p t -> p c t"), R=[g.DB["QK"]], W=[Bc])
    VAs = alloc(g, [NT, 520], BF16)
    p.dma("sp", VAs, S["VA"].rearrange("(t p) f -> p t f", p=128), R=[g.DB["VA"]], W=[Bc])
    tabf = alloc(g, [8, 16, 64], F32)
    maskf = alloc(g, [64], F32)
    TT = alloc(g, [8, 16, 64], BF16)
    p.memset("pool", tabf, 0.0, [Bc])
    for h in range(8):
        p.dma("sp", tabf[0:64, h, 0:15, :], I["rpb_tab"][e, h].rearrange("b k c -> k b c"), W=[Bc])
        p.dma("sp", tabf[64:128, h, 0:14, :], I["rpb_tab"][e, h, 1:15].rearrange("b k c -> k b c"), W=[Bc])
    p.dma("sp", maskf[0:64, :], I["namask"][:, :], W=[Bc])
    p.dma("sp", maskf[64:128, :], I["namask"][:, :], W=[Bc])
    for h in range(8):
        p.tt("pool", TT[:, h, :, :], tabf[:, h, :, :], maskf.unsqueeze(1).broadcast_to([128, 16, 64]), ALU.add, [Bc], [Bc])
    qT = [alloc(g, [4, 128], BF16) for _ in range(2)]
    BqT = [Buf() for _ in range(2)]
    PT = [alloc(g, [512], BF16) for _ in range(3)]
    BPT = [Buf() for _ in range(3)]
    att = [alloc(g, [8, 64], BF16) for _ in range(2)]
    Batt = [Buf() for _ in range(2)]
    rinv = [alloc(g, [8], F32) for _ in range(2)]
    Brinv = [Buf() for _ in range(2)]
    zt = [alloc(g, [4, 128], BF16) for _ in range(2)]
    Bzt = [Buf() for _ in range(2)]
    ipt = 0
    stc = [0]

    def st_bank():
        i_ = 4 + stc[0] % 3
        stc[0] += 1
        return g.ps[i_], g.PB[i_]

    def oa_banks(k):
        b0 = (k % 2) * 2
        return [(g.ps[b0], g.PB[b0]), (g.ps[b0 + 1], g.PB[b0 + 1])]

    def finish(oa, i, tq):
        for bnk in range(2):
            ps, pb = oa[bnk]
            v = ps[:, 0:260].rearrange("p (a b) -> p a b", a=4)
            p.recip(rinv[i][:, bnk * 4:(bnk + 1) * 4], v[:, :, 64], [pb], [Brinv[i]])
            p.tt("dve", att[i][:, bnk * 4:(bnk + 1) * 4, :], v[:, :, 0:64],
                 rinv[i][:, bnk * 4:(bnk + 1) * 4].unsqueeze(2).broadcast_to([128, 4, 64]), ALU.mult, [pb, Brinv[i]], [Batt[i]])
        ps, pb = g.ps[7], g.PB[7]
        psb = ps[:, :].bitcast(BF16)
        af = att[i].rearrange("p a b -> p (a b)")
        for c in range(4):
            p.tr(psb[:, c * 128:(c + 1) * 128], af[:, c * 128:(c + 1) * 128], g.identB, [Batt[i], g.Bconst], [pb])
        p.copy("act", zt[i], psb[:, 0:512].rearrange("p (a b) -> p a b", a=4), [pb], [Bzt[i]])
        p.dma("pool", S["ZT"][0:4, :, tq * 128:(tq + 1) * 128].rearrange("c p t -> p c t"), zt[i], R=[Bzt[i]], W=[g.DB["ZT"]])

    if ctx_out:
        qc = alloc(g, [4, 256], BF16)
        Bqc = Buf()
        p.dma("sp", qc, S["QK"][0:4, :, 0:256].rearrange("c p t -> p c t"), R=[g.DB["QK"]], W=[Bqc])
        PTc = [alloc(g, [512], BF16) for _ in range(2)]
        BPTc = [Buf() for _ in range(2)]
        oa = [oa_banks(0), oa_banks(1)]
        for h in range(8):
            pr, pb_ = h // 2, (h % 2) * 64
            ps, pb = st_bank()
            for j in range(2):
                p.mm(ps[:, j * 256:(j + 1) * 256], KT[pb_:pb_ + 64, pr, j * 128:(j + 1) * 128], qc[pb_:pb_ + 64, pr, :], True, True, [Bc, Bqc], [pb])
            P_, BP_ = PTc[h % 2], BPTc[h % 2]
            p.act(P_, ps[:, :], AF.Exp, [pb], [BP_])
            for qt in range(2):
                ops_, opb = oa[qt][h // 4]
                hh = h % 4
                for j in range(2):
                    p.mm(ops_[:, hh * 65:(hh + 1) * 65], P_[:, j * 256 + qt * 128: j * 256 + (qt + 1) * 128], VAs[:, j, h * 65:(h + 1) * 65],
                         j == 0, j == 1, [BP_, Bc], [opb])
        for qt in range(2):
            finish(oa[qt], qt, qt)
    for r in range(nrows):
        tq = 2 + r // 2
        iq = (r // 2) % 2
        hq = r % 2
        if hq == 0:
            p.dma("sp", qT[iq], S["QK"][0:4, :, tq * 128:(tq + 1) * 128].rearrange("c p t -> p c t"), R=[g.DB["QK"]], W=[BqT[iq]])
            oa = oa_banks(r // 2)
        rs = min(max(r - 4, 0), 56)
        units = []
        wr = rs
        while wr < rs + 8:
            if wr % 2 == 0 and wr + 1 < rs + 8:
                units.append((wr, 2))
                wr += 2
            else:
                units.append((wr, 1))
                wr += 1
        nloc = (rs + 7) // 2 - rs // 2 + 1
        for h in range(8):
            pr, pb_ = h // 2, (h % 2) * 64
            ps, pb = st_bank()
            q_ap = qT[iq][pb_:pb_ + 64, pr, hq * 64:(hq + 1) * 64]
            geo = []
            for (wr, n) in units:
                slot = wr // 2 - rs // 2
                col = 256 + wr * 64
                if n == 2:
                    lo, hi, b = 0, 128, wr - r + 7
                elif wr % 2 == 1:
                    lo, hi, b = 64, 128, wr - r + 6
                else:
                    lo, hi, b = 0, 64, wr - r + 7
                geo.append((wr, slot, lo, hi))
                out = ps[lo:hi, slot * 64:(slot + 1) * 64]
                p.mm(out, KT[pb_:pb_ + 64, pr, col:col + (hi - lo)], q_ap, True, False, [Bc, BqT[iq]], [pb])
                p.mm(out, g.identB[:, lo:hi], TT[:, h, b, :], False, True, [g.Bconst, Bc], [pb])
            for j in range(2):
                slot = nloc + j
                p.mm(ps[:, slot * 64:(slot + 1) * 64], KT[pb_:pb_ + 64, pr, j * 128:(j + 1) * 128], q_ap, True, True, [Bc, BqT[iq]], [pb])
            ncol = (nloc + 2) * 64
            P_, BP_ = PT[ipt % 3], BPT[ipt % 3]
            ipt += 1
            p.act(P_[:, 0:ncol], ps[:, 0:ncol], AF.Exp, [pb], [BP_])
            ops_, opb = oa[h // 4]
            hh = h % 4
            out = ops_[hq * 64:(hq + 1) * 64, hh * 65:(hh + 1) * 65]
            nmm = len(geo) + 2
            for ii, (wr, slot, lo, hi) in enumerate(geo):
                p.mm(out, P_[lo:hi, slot * 64:(slot + 1) * 64], VAs[lo:hi, 2 + wr // 2, h * 65:(h + 1) * 65], ii == 0, False, [BP_, Bc], [opb])
            for j in range(2):
                slot = nloc + j
                p.mm(out, P_[:, slot * 64:(slot + 1) * 64], VAs[:, j, h * 65:(h + 1) * 65], False, j == 1, [BP_, Bc], [opb])
        if hq == 1:
            finish(oa, iq, tq)


def even_out(g, l, e):
    p, I, S = g.p, g.I, g.S
    new_phase(g)
    tiles = tile_list(g, l, "mixout")
    Wo = alloc(g, [8, 1024], BF16)
    BWo = Buf()
    load_w_bf16(g, Wo, I["w_out_even"][e].rearrange("(k p) n -> p k n", p=128), BWo)
    gates = [load_gate(g, l, s_, 0) for s_ in range(2)]
    x = [alloc(g, [D], F32) for _ in range(2)]
    Bx = [Buf() for _ in range(2)]
    zt = [alloc(g, [8, 128], BF16) for _ in range(2)]
    Bzt = [Buf() for _ in range(2)]
    tmp = [alloc(g, [512], F32) for _ in range(2)]
    Btmp = [Buf() for _ in range(2)]
    for it, t in enumerate(tiles):
        i = it % 2
        s_ = 1 if t < 2 else 0
        p.dma("sp", x[i], S["X"][t * 128:(t + 1) * 128, :], R=[g.DB["X"]], W=[Bx[i]])
        p.dma("sp", zt[i], S["ZT"][:, :, t * 128:(t + 1) * 128].rearrange("c p t -> p c t"), R=[g.DB["ZT"]], W=[Bzt[i]])
        pss = []
        for n in range(2):
            ps, pb = bank(g)
            for k in range(8):
                p.mm(ps[:, :], zt[i][:, k, :], Wo[:, k, n * 512:(n + 1) * 512], k == 0, k == 7, [Bzt[i], BWo], [pb])
            pss.append((ps, pb))
        gt, Bgt = gates[s_]
        resid_update(g, pss, x[i], Bx[i], gt, Bgt, tmp, Btmp)
        p.dma("pool", S["X"][t * 128:(t + 1) * 128, :], x[i], R=[Bx[i]], W=[g.DB["X"]])


def resid_update(g, ps_list, x, Bx, gt, Bgt, tmp, Btmp):
    p = g.p
    for n, (ps, pb) in enumerate(ps_list):
        p.tt("dve", tmp[n], ps[:, :], gt[:, n * 512:(n + 1) * 512], ALU.mult, [pb, Bgt], [Btmp[n]])
        p.tt("pool", x[:, n * 512:(n + 1) * 512], tmp[n], x[:, n * 512:(n + 1) * 512], ALU.add, [Btmp[n], Bx], [Bx])


def phase_odd(g, l):
    p, I, S = g.p, g.I, g.S
    o = l // 2
    tiles = tile_list(g, l, "mixout")
    w_in = I["w_in_odd"][o].rearrange("(k p) n -> p k n", p=128)
    new_phase(g)
    Wv = alloc(g, [8, 3072], BF16)
    BWv = Buf()
    load_w_bf16(g, Wv, w_in[:, :, 3072:6144], BWv, max_cols=1024)
    wsT = alloc(g, [8, 128], BF16)
    bsT = alloc(g, [8], F32)
    gvb = alloc(g, [3072], F32)
    Bc = Buf()
    p.dma("pool", wsT, I["gm_wsT"][o], W=[Bc], stream="wl")
    p.dma("sp", bsT, I["gm_bsT"][o], W=[Bc])
    p.dma("sp", gvb, I["gm_g_v"][o, :].partition_broadcast(128), W=[Bc])
    mods = [load_mod(g, l, s_, 0) for s_ in range(2)]
    nb = NormBufs(g)
    vf = [alloc(g, [3072], F32) for _ in range(2)]
    Bvf = [Buf() for _ in range(2)]
    vn = [alloc(g, [3072], BF16) for _ in range(2)]
    Bvn = [Buf() for _ in range(2)]
    junk = alloc(g, [3072], BF16)
    Bj = Buf()
    st = [alloc(g, [4], F32) for _ in range(2)]
    Bst = [Buf() for _ in range(2)]
    mix = [alloc(g, [3072], F32) for _ in range(2)]
    Bmix = [Buf() for _ in range(2)]
    tmpg = [alloc(g, [384], F32) for _ in range(2)]
    Btg = [Buf() for _ in range(2)]
    def stage0(it, t):
        return norm_T(g, nb, it % 2, t, mods[1 if t < 2 else 0])

    def stage1(it, t, cur):
        i = it % 2
        s_ = 1 if t < 2 else 0
        hT, BhT = cur
        for n in range(6):
            ps, pb = bank(g)
            for k in range(8):
                p.mm(ps[:, :], hT[:, k, :], Wv[:, k, n * 512:(n + 1) * 512], k == 0, k == 7, [BhT, BWv], [pb])
            p.act(vf[i][:, n * 512:(n + 1) * 512], ps[:, :], AF.Gelu, [pb], [Bvf[i]])
        p.act(junk, vf[i], AF.Square, [Bvf[i]], [Bj, Bst[i]], accum_out=st[i][:, 0:1])
        p.act(st[i][:, 1:2], st[i][:, 0:1], AF.Sqrt, [Bst[i]], [Bst[i]], scale=1.0 / 3072, bias=EPS)
        p.recip(st[i][:, 2:3], st[i][:, 1:2], [Bst[i]], [Bst[i]])
        p.ts("dve", vn[i], vf[i], st[i][:, 2:3], None, ALU.mult, None, [Bvf[i], Bst[i]], [Bvn[i]])
        for gi in range(8):
            ps, pb = bank(g)
            p.mm(ps[:, 0:384], wsT[:, gi, :], vn[i][:, gi * 384:(gi + 1) * 384], True, True, [Bc, Bvn[i]], [pb])
            tg, btg = tmpg[gi % 2], Btg[gi % 2]
            p.tt("dve", tg, ps[:, 0:384], gvb[:, gi * 384:(gi + 1) * 384], ALU.mult, [pb, Bc], [btg])
            p.ts("dve", mix[i][:, gi * 384:(gi + 1) * 384], tg, bsT[:, gi:gi + 1], None, ALU.add, None, [btg, Bc], [Bmix[i]])
        p.dma("pool", S["MIX"][t * 128:(t + 1) * 128, :], mix[i], R=[Bmix[i]], W=[g.DB["MIX"]])
    pipelined(tiles, stage0, stage1)
    new_phase(g)
    Wu = alloc(g, [8, 3072], BF16)
    Wo = alloc(g, [24, 1024], BF16)
    BWu, BWo = Buf(), Buf()
    load_w_bf16(g, Wu, w_in[:, :, 0:3072], BWu, max_cols=1024)
    load_w_bf16(g, Wo, I["w_out_odd"][o].rearrange("(k p) n -> p k n", p=128), BWo)
    mods = [load_mod(g, l, s_, 0) for s_ in range(2)]
    gates = [load_gate(g, l, s_, 0) for s_ in range(2)]
    nb = NormBufs(g)
    mix = [alloc(g, [3072], F32) for _ in range(2)]
    Bmix = [Buf() for _ in range(2)]
    uf = [alloc(g, [512], F32) for _ in range(2)]
    Buf_ = [Buf() for _ in range(2)]
    sb = [alloc(g, [3072], BF16) for _ in range(2)]
    Bsb = [Buf() for _ in range(2)]
    sT = [alloc(g, [24, 128], BF16) for _ in range(2)]
    BsT = [Buf() for _ in range(2)]
    tmp = [alloc(g, [512], F32) for _ in range(2)]
    Btmp = [Buf() for _ in range(2)]
    uic = [0]

    def stage0(it, t):
        p.dma("sp", mix[it % 2], S["MIX"][t * 128:(t + 1) * 128, :], R=[g.DB["MIX"]], W=[Bmix[it % 2]])
        return norm_T(g, nb, it % 2, t, mods[1 if t < 2 else 0])

    def stage1(it, t, cur):
        i = it % 2
        s_ = 1 if t < 2 else 0
        hT, BhT = cur
        for n in range(6):
            ps, pb = bank(g)
            for k in range(8):
                p.mm(ps[:, :], hT[:, k, :], Wu[:, k, n * 512:(n + 1) * 512], k == 0, k == 7, [BhT, BWu], [pb])
            u_, bu_ = uf[uic[0] % 2], Buf_[uic[0] % 2]
            uic[0] += 1
            p.act(u_, ps[:, :], AF.Gelu, [pb], [bu_])
            p.tt("pool", sb[i][:, n * 512:(n + 1) * 512], u_, mix[i][:, n * 512:(n + 1) * 512], ALU.mult, [bu_, Bmix[i]], [Bsb[i]])
        for q in range(3):
            ps, pb = bank(g)
            psb = ps[:, :].bitcast(BF16)
            for kk in range(8):
                k = q * 8 + kk
                p.tr(psb[:, kk * 128:(kk + 1) * 128], sb[i][:, k * 128:(k + 1) * 128], g.identB, [Bsb[i], g.Bconst], [pb])
            dst = sT[i][:, q * 8:(q + 1) * 8, :]
            src = psb.rearrange("p (a b) -> p a b", a=8)
            if q % 2 == 0:
                p.copy("act", dst, src, [pb], [BsT[i]])
            else:
                p.copy("dve", dst, src, [pb], [BsT[i]])
        pss = []
        for n in range(2):
            ps, pb = bank(g)
            for k in range(24):
                p.mm(ps[:, :], sT[i][:, k, :], Wo[:, k, n * 512:(n + 1) * 512], k == 0, k == 23, [BsT[i], BWo], [pb])
            pss.append((ps, pb))
        gt, Bgt = gates[s_]
        resid_update(g, pss, nb.x[i], nb.Bx[i], gt, Bgt, tmp, Btmp)
        p.dma("pool", S["X"][t * 128:(t + 1) * 128, :], nb.x[i], R=[nb.Bx[i]], W=[g.DB["X"]])


    pipelined(tiles, stage0, stage1)


def make_in_maps(inputs):
    f = lambda a: np.ascontiguousarray(np.asarray(a, dtype=np.float32))
    shared = {}
    for k in ["w_ada", "b_ada", "g_norm_mix", "g_norm_ffn", "w_in_even", "w_out_even", "dn_conv", "dn_g_out",
              "w_in_odd", "gm_g_v", "w_out_odd", "w_ff1", "w_ff2", "g_final"]:
        shared[k] = f(inputs[k])
    shared["rpb_tab"] = rpb_table(f(inputs["na_rpb"]))
    shared["dn_a_log"] = f(inputs["dn_a_log"]).reshape(2, 8)
    shared["dn_dt_bias"] = f(inputs["dn_dt_bias"]).reshape(2, 8)
    shared["gm_wsT"] = np.ascontiguousarray(f(inputs["gm_ws"]).transpose(0, 3, 1, 2))
    shared["gm_bsT"] = np.ascontiguousarray(f(inputs["gm_bs"]).transpose(0, 2, 1))
    shared.update(host_consts())
    x = f(inputs["x"])
    ctx = f(inputs["ctx"])
    c = f(inputs["c"])
    c_ctx = f(inputs["c_ctx"])
    maps = []
    for b in range(8):
        m = dict(shared)
        m["x"] = x[b]
        m["ctx"] = ctx[b]
        m["cvec"] = np.ascontiguousarray(np.stack([c[b], c_ctx]))
        maps.append(m)
    return maps


def kernel(**inputs):
    nc, g = build()
    maps = make_in_maps(inputs)
    res = run_bass_kernel_spmd(nc, maps, core_ids=list(range(8)))
    return np.stack([np.asarray(r["out"], dtype=np.float32) for r in res.results], axis=0)
```

```python
import numpy as np
from contextlib import ExitStack
import concourse.bass as bass
import concourse.mybir as mybir
from concourse.bass_utils import run_bass_kernel_spmd

F32 = mybir.dt.float32
BF16 = mybir.dt.bfloat16
AF = mybir.ActivationFunctionType
ALU = mybir.AluOpType

D = 1024
L = 4096
LC = 256
NT = (L + LC) // 128
NTOK = L + LC
EPS = 1e-6
DEPTH = 4
EVEN_IN = 3600
NPAD = NTOK + 4
NEG = -30000.0


class Buf:
    __slots__ = ("name", "w", "r", "excl")

    def __init__(self, name="", excl=False):
        self.name = name
        self.w = None
        self.r = []
        self.excl = excl


class Prog:
    ENGS = ("pe", "act", "dve", "pool", "sp")

    def __init__(self, nc):
        self.nc = nc
        self.ops = []
        self.stream_cnt = {}
        self.stream_last = {}
        self.stream_R = {"ld": 8, "st": 8, "wl": 12}
        self.last_real = {}

    def sem_names(self):
        names = ["e_" + e for e in self.ENGS]
        for st, R in self.stream_R.items():
            names += ["s_%s%d" % (st, j) for j in range(R)]
        return names

    def add(self, eng, fn, reads=(), writes=(), stream=None, extra=()):
        i = len(self.ops)
        deps = set(extra)
        xr = [b for b in reads if b.excl and eng != "pe"]
        if xr:
            reads = [b for b in reads if not (b.excl and eng != "pe")]
            writes = list(writes) + xr
        for b in reads:
            if b.w is not None:
                deps.add(b.w)
        for b in writes:
            if b.w is not None:
                deps.add(b.w)
            deps.update(b.r)
        for b in reads:
            b.r.append(i)
        for b in writes:
            b.w = i
            b.r = []
        val = None
        if stream is not None:
            n = self.stream_cnt.get(stream, 0)
            self.stream_cnt[stream] = n + 1
            R = self.stream_R[stream]
            stream = "%s%d" % (stream, n % R)
            val = 16 * (n // R + 1)
            prev = self.stream_last.get(stream)
            if prev is not None:
                deps.add(prev)
            self.stream_last[stream] = i
        elif fn is not None:
            self.last_real[eng] = i
        self.ops.append([eng, fn, deps, stream, False, val])
        return i

    def barrier(self):
        ex = set(self.last_real.values()) | set(self.stream_last.values())
        for e in self.ENGS:
            self.add(e, None, extra=ex)

    def dma(self, q, out, in_, R=(), W=(), stream=None, **kw):
        if stream is None:
            stream = "st" if q == "pool" else "ld"
        return self.add(q, lambda e: e.dma_start(out=out, in_=in_, **kw), R, W, stream=stream)

    def mm(self, out, lhsT, rhs, start, stop, R, W):
        return self.add("pe", lambda e: e.matmul(out, lhsT=lhsT, rhs=rhs, start=start, stop=stop), R, W)

    def tr(self, out, in_, ident, R, W):
        return self.add("pe", lambda e: e.transpose(out=out, in_=in_, identity=ident), R, W)

    def act(self, out, in_, func, R, W, **kw):
        return self.add("act", lambda e: e.activation(out=out, in_=in_, func=func, **kw), R, W)

    def ts(self, eng, out, in0, s1, s2, op0, op1, R, W):
        if s2 is None:
            return self.add(eng, lambda e: e.tensor_scalar(out=out, in0=in0, scalar1=s1, scalar2=None, op0=op0), R, W)
        return self.add(eng, lambda e: e.tensor_scalar(out=out, in0=in0, scalar1=s1, scalar2=s2, op0=op0, op1=op1), R, W)

    def tt(self, eng, out, in0, in1, op, R, W):
        return self.add(eng, lambda e: e.tensor_tensor(out=out, in0=in0, in1=in1, op=op), R, W)

    def stt(self, out, in0, scalar, in1, op0, op1, R, W):
        return self.add("dve", lambda e: e.scalar_tensor_tensor(out=out, in0=in0, scalar=scalar, in1=in1, op0=op0, op1=op1), R, W)

    def copy(self, eng, out, in_, R, W):
        if eng == "act":
            return self.add("act", lambda e: e.activation(out=out, in_=in_, func=AF.Copy), R, W)
        return self.add(eng, lambda e: e.tensor_copy(out=out, in_=in_), R, W)

    def memset(self, eng, out, val, W):
        return self.add(eng, lambda e: e.memset(out, val), (), W)

    def recip(self, out, in_, R, W):
        return self.add("dve", lambda e: e.reciprocal(out=out, in_=in_), R, W)

    def emit(self, sems):
        ops = self.ops
        for (eng, fn, deps, stream, sig, val) in ops:
            for d in deps:
                po = ops[d]
                if po[3] is None:
                    if po[0] == "pe" and eng == "pe":
                        continue
                    po[4] = True
        cnt = {e: 0 for e in self.ENGS}
        for o in ops:
            if o[3] is None and o[4]:
                cnt[o[0]] += 1
                o[5] = cnt[o[0]]
        per_eng = {e: [] for e in self.ENGS}
        waited = {e: {} for e in self.ENGS}
        for (eng, fn, deps, stream, sig, val) in ops:
            need = {}
            for d in deps:
                po = ops[d]
                if po[3] is None:
                    if po[0] == "pe" and eng == "pe":
                        continue
                    if po[1] is None:
                        continue
                    key = "e_" + po[0]
                else:
                    key = "s_" + po[3]
                need[key] = max(need.get(key, 0), po[5])
            waits = []
            for key, v in need.items():
                if waited[eng].get(key, 0) >= v:
                    continue
                waited[eng][key] = v
                waits.append((key, v))
            per_eng[eng].append((waits, fn, stream, sig))
        self.n_inst = {e: len(v) for e, v in per_eng.items()}

        def run(engobj, lst, ename):
            for (waits, fn, stream, sig) in lst:
                for (key, v) in waits:
                    engobj.wait_ge(sems[key], v)
                if fn is None:
                    continue
                ins = fn(engobj)
                if stream is not None:
                    ins.then_inc(sems["s_" + stream], 16)
                elif sig:
                    ins.then_inc(sems["e_" + ename], 1)

        with self.nc.Block() as block:
            @block.tensor
            def _(e):
                run(e, per_eng["pe"], "pe")

            @block.scalar
            def _(e):
                run(e, per_eng["act"], "act")

            @block.vector
            def _(e):
                run(e, per_eng["dve"], "dve")

            @block.gpsimd
            def _(e):
                run(e, per_eng["pool"], "pool")

            @block.sync
            def _(e):
                run(e, per_eng["sp"], "sp")


def host_consts():
    c = {}
    c["ident"] = np.eye(128, dtype=np.float32)
    c["ones"] = np.ones((128, 128), np.float32)
    idx = np.arange(128)
    c["tri"] = np.stack([(idx[:, None] <= idx[None, :]), (idx[:, None] >= idx[None, :]),
                         (idx[:, None] > idx[None, :]), (idx[:, None] < idx[None, :])]).astype(np.float32)
    rm = np.zeros((128, 128), np.float32)
    for j in range(32):
        rm[32 + j, j] = -1.0
        rm[j, 32 + j] = 1.0
        rm[96 + j, 64 + j] = -1.0
        rm[64 + j, 96 + j] = 1.0
    c["rotm"] = rm
    t = np.arange(L)
    row = (t // 64).astype(np.float32)
    col = (t % 64).astype(np.float32)
    inv = (10000.0 ** (-np.arange(32, dtype=np.float32) / 32)).astype(np.float32)
    ar = row[:, None] * inv
    ac = col[:, None] * inv
    ang = np.concatenate([ar, ar, ac, ac], axis=-1)
    c["cosT"] = np.ascontiguousarray(np.cos(ang).T.astype(np.float32))
    c["sinT"] = np.ascontiguousarray(np.sin(ang).T.astype(np.float32))
    cc = np.arange(64)
    cstart = np.clip(cc - 8, 0, 48)
    ok = (cc[:, None] >= cstart[None, :]) & (cc[:, None] < cstart[None, :] + 16)
    c["namask"] = np.where(ok, 0.0, NEG).astype(np.float32)
    return c


def rpb_table(na_rpb):
    cc = np.arange(64)
    dc = np.clip(cc[:, None] - cc[None, :], -15, 15) + 15
    return np.ascontiguousarray(na_rpb[:, :, :, dc]).astype(np.float32)


class Ctx:
    pass


def build(cfg=None):
    cfg = cfg or {}
    layers = cfg.get("layers", list(range(DEPTH)))
    dbg = cfg.get("debug", ())
    nc = bass.Bass("TRN2", target_bir_lowering=False)
    g = Ctx()
    g.nc = nc
    g.cfg = cfg

    def din(name, shape, dt=F32):
        return nc.dram_tensor(name, list(shape), dt, kind="ExternalInput").ap()

    def dscr(name, shape, dt=F32):
        kind = "ExternalOutput" if name in dbg else "Internal"
        return nc.dram_tensor(name, list(shape), dt, kind=kind).ap()

    hc = host_consts()
    shapes = {"x": [L, D], "ctx": [LC, D], "cvec": [2, D], "w_ada": [4, D, 6 * D], "b_ada": [4, 6 * D],
              "g_norm_mix": [4, D], "g_norm_ffn": [4, D], "w_in_even": [2, D, EVEN_IN], "w_out_even": [2, D, D],
              "rpb_tab": [2, 8, 15, 64, 64], "dn_conv": [2, 3, 1536], "dn_a_log": [2, 8], "dn_dt_bias": [2, 8],
              "dn_g_out": [2, 128], "w_in_odd": [2, D, 6144], "gm_g_v": [2, 3072], "gm_wsT": [2, 128, 8, 128],
              "gm_bsT": [2, 128, 8], "w_out_odd": [2, 3072, D], "w_ff1": [4, D, 4096], "w_ff2": [4, 4096, D],
              "g_final": [D]}
    for k, v in hc.items():
        shapes[k] = list(v.shape)

    class LazyIn(dict):
        def __missing__(self, k):
            self[k] = din(k, shapes[k])
            return self[k]
    I = g.I = LazyIn()
    if not cfg.get("lazy"):
        for k in shapes:
            I[k]
    g.out = nc.dram_tensor("out", [L, D], F32, kind="ExternalOutput").ap()

    S = g.S = {}
    S["X"] = dscr("X", [NTOK, D])
    S["MODV"] = dscr("MODV", [4, 2, 6, D])
    S["QK"] = dscr("QK", [8, 128, NTOK], BF16)
    S["VA"] = dscr("VA", [NTOK, 520], BF16)
    S["DNRAW"] = dscr("DNRAW", [12, 128, NPAD])
    S["GATE"] = dscr("GATE", [NTOK, 512])
    S["BA"] = dscr("BA", [NTOK, 16])
    S["DNP"] = dscr("DNP", [NT, 8, 128, 648])
    S["OF"] = dscr("OF", [NTOK, 512])
    S["ZT"] = dscr("ZT", [8, 128, NTOK], BF16)
    S["MIX"] = dscr("MIX", [NTOK, 3072])

    p = g.p = Prog(nc)
    with ExitStack() as es:
        sems = {n: es.enter_context(nc.semaphore(n)) for n in p.sem_names()}
        NW = 53000
        g.big = es.enter_context(nc.sbuf_tensor("big", [128, NW], F32))
        g.NW = NW
        g.ps = [es.enter_context(nc.psum_tensor("ps%d" % i, [128, 512], F32)) for i in range(8)]
        g.persist = 0
        g.off = 0
        g.DB = {k: Buf(k) for k in list(S.keys()) + ["out"]}
        g._dbt = {}
        g.DBT = lambda name, t: g._dbt.setdefault((name, t), Buf())
        phase_setup(g)
        for l in layers:
            if "nomix" in cfg:
                pass
            elif l % 2 == 0:
                phase_even(g, l)
            else:
                phase_odd(g, l)
            if "noffn" not in cfg:
                phase_ffn(g, l)
        if "nofinal" not in cfg:
            phase_final(g)
        p.add("sp", None, extra=set(p.stream_last.values()) | set(p.last_real.values()))
        p.emit(sems)
    g.n_inst = p.n_inst
    return nc, g


def alloc(g, free_shape, dt=F32, persist=False):
    n = int(np.prod(free_shape))
    words = n if dt == F32 else (n + 1) // 2
    words = (words + 7) // 8 * 8
    a = g.big[:, g.off:g.off + words]
    g.off += words
    assert g.off <= g.NW, "SBUF overflow %d" % g.off
    if persist:
        g.persist = g.off
    if dt != F32:
        a = a.bitcast(dt)
    a = a[:, 0:n]
    if len(free_shape) == 2:
        a = a.rearrange("p (a b) -> p a b", a=free_shape[0])
    elif len(free_shape) == 3:
        a = a.rearrange("p (a b c) -> p a b c", a=free_shape[0], b=free_shape[1])
    return a


def new_phase(g):
    g.p.barrier()
    g.off = g.persist
    g.PB = [Buf("ps%d" % i, excl=True) for i in range(8)]
    g.bank_i = 0


def bank(g):
    i = g.bank_i % 8
    g.bank_i += 1
    return g.ps[i], g.PB[i]


def phase_setup(g):
    p, I, S = g.p, g.I, g.S
    g.PB = [Buf("ps%d" % i, excl=True) for i in range(8)]
    g.bank_i = 0
    g.identF = alloc(g, [128], F32, persist=True)
    g.identB = alloc(g, [128], BF16, persist=True)
    g.onesF = alloc(g, [128], F32, persist=True)
    g.Bconst = Buf("const")
    p.dma("sp", g.identF, I["ident"][:, :], W=[g.Bconst])
    p.dma("sp", g.onesF, I["ones"][:, :], W=[g.Bconst])
    p.copy("dve", g.identB, g.identF, [g.Bconst], [g.Bconst])
    p.dma("sp", S["X"][0:LC, :], I["ctx"][:, :], W=[g.DB["X"]])
    p.dma("sp", S["X"][LC:NTOK, :], I["x"][:, :], W=[g.DB["X"]])
    z = alloc(g, [12, 4], F32)
    Bz = Buf()
    p.memset("dve", z, 0.0, [Bz])
    for col in (0, 257, 258, NPAD - 1):
        p.dma("sp", S["DNRAW"][:, :, col:col + 1].rearrange("c p t -> p c t"), z[:, :, 0:1], R=[Bz], W=[],
              allow_slow_non_contiguous=True)
    if g.cfg.get("ntiles"):
        zz = alloc(g, [12, 512], F32)
        Bzz = Buf()
        p.memset("dve", zz, 0.0, [Bzz])
        for c0 in range(0, NPAD, 512):
            c1 = min(NPAD, c0 + 512)
            p.dma("sp", S["DNRAW"][:, :, c0:c1].rearrange("c p t -> p c t"), zz[:, :, 0:c1 - c0], R=[Bzz], W=[])
        zb = alloc(g, [8, 520], BF16)
        p.memset("dve", zb, 0.0, [Bzz])
        for c0 in range(0, NTOK, 512):
            c1 = min(NTOK, c0 + 512)
            p.dma("sp", S["QK"][:, :, c0:c1].rearrange("c p t -> p c t"), zb[:, :, 0:c1 - c0], R=[Bzz], W=[])
        for t_ in range(NT):
            p.dma("sp", S["VA"][t_ * 128:(t_ + 1) * 128, :], zb[:, 0, :], R=[Bzz], W=[])
    cf = alloc(g, [8, 2], F32)
    Bcf = Buf()
    for s_ in range(2):
        p.dma("sp", cf[:, :, s_], I["cvec"][s_, :].rearrange("(k p) -> p k", p=128), W=[Bcf], allow_slow_non_contiguous=True)
    p.act(cf, cf, AF.Silu, [Bcf], [Bcf])
    wb = [alloc(g, [8, 512], F32) for _ in range(3)]
    Bw = [Buf() for _ in range(3)]
    mrow = alloc(g, [6 * D], F32)
    brow = alloc(g, [6 * D], F32)
    grow = alloc(g, [2, D], F32)
    tmp = alloc(g, [D], F32)
    Bm, Bb, Bg, Bt = Buf(), Buf(), Buf(), Buf()
    it = 0
    for l in ([] if g.cfg.get("nomod") else g.cfg.get("layers", list(range(DEPTH)))):
        p.dma("sp", brow[0:2, :], I["b_ada"][l, :].partition_broadcast(2), W=[Bb])
        p.dma("sp", grow[0:2, 0, :], I["g_norm_mix"][l, :].partition_broadcast(2), W=[Bg])
        p.dma("sp", grow[0:2, 1, :], I["g_norm_ffn"][l, :].partition_broadcast(2), W=[Bg])
        for n in range(12):
            w_, bw_ = wb[it % 3], Bw[it % 3]
            it += 1
            p.dma("sp", w_, I["w_ada"][l, :, n * 512:(n + 1) * 512].rearrange("(k p) n -> p k n", p=128), W=[bw_])
            ps, pb = bank(g)
            for k in range(8):
                p.mm(ps[0:2, :], cf[:, k, :], w_[:, k, :], k == 0, k == 7, [Bcf, bw_], [pb])
            p.tt("dve", mrow[0:2, n * 512:(n + 1) * 512], ps[0:2, :], brow[0:2, n * 512:(n + 1) * 512], ALU.add, [pb, Bb], [Bm])
        for j, (sc_i, sh_i, gt_i) in enumerate([(1, 0, 2), (4, 3, 5)]):
            p.stt(tmp[0:2, :], mrow[0:2, sc_i * D:(sc_i + 1) * D], 1.0, grow[0:2, j, :], ALU.add, ALU.mult, [Bm, Bg], [Bt])
            p.dma("sp", S["MODV"][l, :, 3 * j + 0, :], tmp[0:2, :], R=[Bt], W=[])
            p.dma("sp", S["MODV"][l, :, 3 * j + 1, :], mrow[0:2, sh_i * D:(sh_i + 1) * D], R=[Bm], W=[])
            p.dma("sp", S["MODV"][l, :, 3 * j + 2, :], mrow[0:2, gt_i * D:(gt_i + 1) * D], R=[Bm], W=[])


def load_mod(g, l, stream, which):
    p, S = g.p, g.S
    A = alloc(g, [D], F32)
    sh = alloc(g, [D], F32)
    B = Buf()
    p.dma("sp", A, S["MODV"][l, stream, 3 * which + 0, :].partition_broadcast(128), R=[], W=[B])
    p.dma("sp", sh, S["MODV"][l, stream, 3 * which + 1, :].partition_broadcast(128), R=[], W=[B])
    return A, sh, B


def load_gate(g, l, stream, which):
    p, S = g.p, g.S
    gt = alloc(g, [D], F32)
    B = Buf()
    p.dma("sp", gt, S["MODV"][l, stream, 3 * which + 2, :].partition_broadcast(128), R=[], W=[B])
    return gt, B


class NormBufs:
    def __init__(self, g, nbuf=2, nx=2):
        self.n = nbuf
        self.nx = nx
        self.x = [alloc(g, [D], F32) for _ in range(nx)]
        self.Bx = [Buf() for _ in range(nx)]
        self.tm = [alloc(g, [D], F32) for _ in range(nbuf)]
        self.Btm = [Buf() for _ in range(nbuf)]
        self.xn = [alloc(g, [D], BF16) for _ in range(nbuf)]
        self.Bxn = [Buf() for _ in range(nbuf)]
        self.hT = [alloc(g, [8, 128], BF16) for _ in range(nbuf)]
        self.BhT = [Buf() for _ in range(nbuf)]
        self.st = [alloc(g, [4], F32) for _ in range(nbuf)]
        self.Bst = [Buf() for _ in range(nbuf)]


def norm_T(g, nb, i, t, mods, ix=None):
    p, S = g.p, g.S
    ix = i if ix is None else ix
    x, Bx, xn, Bxn, hT, BhT, st, Bst, tm, Btm = (nb.x[ix], nb.Bx[ix], nb.xn[i], nb.Bxn[i], nb.hT[i], nb.BhT[i],
                                                 nb.st[i], nb.Bst[i], nb.tm[i], nb.Btm[i])
    A, sh, Bmod = mods
    p.dma("sp", x, S["X"][t * 128:(t + 1) * 128, :], R=[g.DBT("X", t)], W=[Bx])
    p.act(tm, x, AF.Square, [Bx], [Btm, Bst], accum_out=st[:, 0:1])
    p.act(st[:, 1:2], st[:, 0:1], AF.Sqrt, [Bst], [Bst], scale=1.0 / D, bias=EPS)
    p.recip(st[:, 2:3], st[:, 1:2], [Bst], [Bst])
    p.stt(tm, x, st[:, 2:3], A, ALU.mult, ALU.mult, [Bx, Bst, Bmod, Btm], [Btm])
    p.tt("pool", xn, tm, sh, ALU.add, [Btm, Bmod], [Bxn])
    ps, pb = bank(g)
    psb = ps[:, :].bitcast(BF16)
    for k in range(8):
        p.tr(psb[:, k * 128:(k + 1) * 128], xn[:, k * 128:(k + 1) * 128], g.identB, [Bxn, g.Bconst], [pb])
    p.copy("act", hT, psb.rearrange("p (a b) -> p a b", a=8), [pb], [BhT])
    return hT, BhT


def pipelined(tiles, stage0, stage1):
    if not tiles:
        return
    nxt = stage0(0, tiles[0])
    for it, t in enumerate(tiles):
        cur = nxt
        if it + 1 < len(tiles):
            nxt = stage0(it + 1, tiles[it + 1])
        stage1(it, t, cur)


def load_w_bf16(g, dst, src_ap, Bw, max_cols=2048):
    p = g.p
    K, N = dst.shape[1], dst.shape[2]
    step = max_cols
    for k in range(K):
        for n0 in range(0, N, step):
            n1 = min(N, n0 + step)
            p.dma("pool", dst[:, k, n0:n1], src_ap[:, k, n0:n1], W=[Bw], stream="wl")


def phase_ffn(g, l):
    p, I, S = g.p, g.I, g.S
    new_phase(g)
    tiles = tile_list(g, l, "ffn")
    W1 = alloc(g, [8, 4096], BF16)
    W2 = alloc(g, [32, 1024], BF16)
    BW1, BW2 = Buf(), Buf()
    load_w_bf16(g, W1, I["w_ff1"][l].rearrange("(k p) n -> p k n", p=128), BW1)
    load_w_bf16(g, W2, I["w_ff2"][l].rearrange("(k p) n -> p k n", p=128), BW2)
    mods = [load_mod(g, l, s_, 1) for s_ in range(2)]
    gates = [load_gate(g, l, s_, 1) for s_ in range(2)]
    nb = NormBufs(g, nx=3)
    r_ = [alloc(g, [512], F32) for _ in range(2)]
    Br = [Buf() for _ in range(2)]
    aT = [alloc(g, [32, 128], BF16) for _ in range(2)]
    BaT = [Buf() for _ in range(2)]
    tmp = [alloc(g, [512], F32) for _ in range(2)]
    Btmp = [Buf() for _ in range(2)]
    ri = [0]

    def stage0(it, t):
        return norm_T(g, nb, it % 2, t, mods[1 if t < 2 else 0], ix=it % 3)

    def stage1(it, t, cur):
        i = it % 2
        s_ = 1 if t < 2 else 0
        hT, BhT = cur
        for mb in range(8):
            ps, pb = bank(g)
            for mm_ in range(4):
                m = mb * 4 + mm_
                for k in range(8):
                    p.mm(ps[:, mm_ * 128:(mm_ + 1) * 128], W1[:, k, m * 128:(m + 1) * 128], hT[:, k, :], k == 0, k == 7, [BW1, BhT], [pb])
            rr, brr = r_[ri[0] % 2], Br[ri[0] % 2]
            ri[0] += 1
            p.act(rr, ps[:, :], AF.Relu, [pb], [brr])
            eng = "dve"
            p.tt(eng, aT[i][:, mb * 4:(mb + 1) * 4, :], rr.rearrange("p (a b) -> p a b", a=4), rr.rearrange("p (a b) -> p a b", a=4), ALU.mult, [brr], [BaT[i]])
        gt, Bgt = gates[s_]
        for n in range(2):
            ps, pb = bank(g)
            for k in range(32):
                p.mm(ps[:, :], aT[i][:, k, :], W2[:, k, n * 512:(n + 1) * 512], k == 0, k == 31, [BaT[i], BW2], [pb])
            p.tt("dve", tmp[n], ps[:, :], gt[:, n * 512:(n + 1) * 512], ALU.mult, [pb, Bgt], [Btmp[n]])
            p.tt("pool", nb.x[it % 3][:, n * 512:(n + 1) * 512], tmp[n], nb.x[it % 3][:, n * 512:(n + 1) * 512], ALU.add, [Btmp[n], nb.Bx[it % 3]], [nb.Bx[it % 3]])
        p.dma("pool", S["X"][t * 128:(t + 1) * 128, :], nb.x[it % 3], R=[nb.Bx[it % 3]], W=[g.DBT("X", t)])

    pipelined(tiles, stage0, stage1)


def tile_list(g, l, kind):
    lim = g.cfg.get("ntiles")
    lat = list(range(2, NT))
    if lim:
        lat = lat[:lim]
    if kind in ("ffn", "mixout"):
        ctx = [0, 1] if l < 2 else []
    elif kind == "proj":
        ctx = [0, 1] if l < 3 else []
    else:
        ctx = []
    return ctx + lat


def phase_final(g):
    p, I, S = g.p, g.I, g.S
    new_phase(g)
    gf = alloc(g, [D], F32)
    Bg = Buf()
    p.dma("sp", gf, I["g_final"].partition_broadcast(128), W=[Bg])
    x = [alloc(g, [D], F32) for _ in range(2)]
    Bx = [Buf() for _ in range(2)]
    y = [alloc(g, [D], F32) for _ in range(2)]
    By = [Buf() for _ in range(2)]
    st = [alloc(g, [4], F32) for _ in range(2)]
    Bst = [Buf() for _ in range(2)]
    junk = alloc(g, [D], BF16)
    Bj = Buf()
    for it, t in enumerate(tile_list(g, 3, "lat")):
        i = it % 2
        p.dma("sp", x[i], S["X"][t * 128:(t + 1) * 128, :], R=[g.DBT("X", t)], W=[Bx[i]])
        p.act(junk, x[i], AF.Square, [Bx[i]], [Bj, Bst[i]], accum_out=st[i][:, 0:1])
        p.act(st[i][:, 1:2], st[i][:, 0:1], AF.Sqrt, [Bst[i]], [Bst[i]], scale=1.0 / D, bias=EPS)
        p.recip(st[i][:, 2:3], st[i][:, 1:2], [Bst[i]], [Bst[i]])
        p.stt(y[i], x[i], st[i][:, 2:3], gf, ALU.mult, ALU.mult, [Bx[i], Bst[i], Bg], [By[i]])
        p.dma("pool", g.out[(t - 2) * 128:(t - 1) * 128, :], y[i], R=[By[i]], W=[])


def phase_even(g, l):
    e = l // 2
    ctx_out = (l == 0)
    st = g.cfg.get("even_stages", "ABCND")
    if "A" in st:
        even_proj(g, l, e)
    if "B" in st:
        even_dnprep(g, l, e)
    if "C" in st:
        even_dnscan(g, l, e, ctx_out)
    if "N" in st:
        even_na(g, l, e, ctx_out)
    if "D" in st:
        even_out(g, l, e)


def dn_col0(t):
    return 1 + t * 128 if t < 2 else 259 + (t - 2) * 128


def even_proj(g, l, e):
    p, I, S = g.p, g.I, g.S
    new_phase(g)
    tiles = tile_list(g, l, "proj")
    W = alloc(g, [8, EVEN_IN], BF16)
    BW = Buf()
    load_w_bf16(g, W, I["w_in_even"][e].rearrange("(k p) n -> p k n", p=128), BW, max_cols=1800)
    mods = [load_mod(g, l, s_, 0) for s_ in range(2)]
    nb = NormBufs(g)
    qk_sb = [alloc(g, [8, 128], BF16) for _ in range(2)]
    va_sb = [alloc(g, [8, 65], BF16) for _ in range(2)]
    dn_sb = [alloc(g, [12, 128], F32) for _ in range(2)]
    gt_sb = [alloc(g, [512], F32) for _ in range(2)]
    ba_sb = [alloc(g, [16], F32) for _ in range(2)]
    Bqk, Bva, Bdn, Bgt, Bba = [[Buf() for _ in range(2)] for _ in range(5)]
    for i in range(2):
        p.memset("pool", va_sb[i][:, :, 64:65], 1.0, [Bva[i]])
    def stage0(it, t):
        return norm_T(g, nb, it % 2, t, mods[1 if t < 2 else 0])

    def stage1(it, t, cur):
        i = it % 2
        s_ = 1 if t < 2 else 0
        hT, BhT = cur
        def fm_bank(col0, nch):
            ps, pb = bank(g)
            for cc in range(nch):
                for k in range(8):
                    p.mm(ps[:, cc * 128:(cc + 1) * 128], W[:, k, col0 + cc * 128: col0 + (cc + 1) * 128], hT[:, k, :], k == 0, k == 7, [BW, BhT], [pb])
            return ps, pb
        ps, pb = fm_bank(0, 4)
        p.act(qk_sb[i][:, 0:4, :], ps[:, :].rearrange("p (a b) -> p a b", a=4), AF.Copy, [pb], [Bqk[i]], scale=0.125)
        ps, pb = fm_bank(512, 4)
        p.copy("dve", qk_sb[i][:, 4:8, :], ps[:, :].rearrange("p (a b) -> p a b", a=4), [pb], [Bqk[i]])
        p.dma("pool", S["QK"][:, :, t * 128:(t + 1) * 128].rearrange("c p t -> p c t"), qk_sb[i], R=[Bqk[i]], W=[])
        ps, pb = bank(g)
        for k in range(8):
            p.mm(ps[:, :], hT[:, k, :], W[:, k, 1024:1536], k == 0, k == 7, [BhT, BW], [pb])
        p.copy("dve", va_sb[i][:, :, 0:64], ps[:, :].rearrange("p (a b) -> p a b", a=8), [pb], [Bva[i]])
        p.dma("pool", S["VA"][t * 128:(t + 1) * 128, :], va_sb[i].rearrange("p a b -> p (a b)"), R=[Bva[i]], W=[])
        for q in range(3):
            ps, pb = fm_bank(1536 + q * 512, 4)
            dst = dn_sb[i][:, q * 4:(q + 1) * 4, :]
            if q == 1:
                p.copy("dve", dst, ps[:, :].rearrange("p (a b) -> p a b", a=4), [pb], [Bdn[i]])
            else:
                p.copy("act", dst, ps[:, :].rearrange("p (a b) -> p a b", a=4), [pb], [Bdn[i]])
        c0 = dn_col0(t)
        p.dma("pool", S["DNRAW"][:, :, c0:c0 + 128].rearrange("c p t -> p c t"), dn_sb[i], R=[Bdn[i]], W=[])
        ps, pb = bank(g)
        for k in range(8):
            p.mm(ps[:, :], hT[:, k, :], W[:, k, 3072:3584], k == 0, k == 7, [BhT, BW], [pb])
        p.copy("act", gt_sb[i], ps[:, :], [pb], [Bgt[i]])
        p.dma("pool", S["GATE"][t * 128:(t + 1) * 128, :], gt_sb[i], R=[Bgt[i]], W=[])
        ps, pb = bank(g)
        for k in range(8):
            p.mm(ps[:, 0:16], hT[:, k, :], W[:, k, 3584:3600], k == 0, k == 7, [BhT, BW], [pb])
        p.copy("dve", ba_sb[i], ps[:, 0:16], [pb], [Bba[i]])
        p.dma("pool", S["BA"][t * 128:(t + 1) * 128, :], ba_sb[i], R=[Bba[i]], W=[])

    pipelined(tiles, stage0, stage1)


def even_dnprep(g, l, e):
    p, I, S = g.p, g.I, g.S
    new_phase(g)
    tiles = tile_list(g, l, "proj")
    Bc = Buf()
    cw = alloc(g, [3, 12], F32)
    cwr = alloc(g, [128], F32)
    Bcw = Buf()
    p.dma("sp", cwr[0:36, :], I["dn_conv"][e].rearrange("j (c p) -> (j c) p", p=128), W=[Bcw])
    ps, pb = bank(g)
    p.tr(ps[:, 0:36], cwr[0:36, :], g.identF[0:36, 0:36], [Bcw, g.Bconst], [pb])
    p.copy("dve", cw.rearrange("p a b -> p (a b)"), ps[:, 0:36], [pb], [Bc])
    tri = alloc(g, [4, 128], F32)
    p.dma("sp", tri, I["tri"].rearrange("m a b -> a m b"), W=[Bc])
    LE, GE, GT, LT = [tri[:, m_, :] for m_ in range(4)]
    rotm = alloc(g, [128], F32)
    p.dma("sp", rotm, I["rotm"][:, :], W=[Bc])
    dtb = alloc(g, [8], F32)
    nexpA = alloc(g, [8], F32)
    p.dma("sp", dtb, I["dn_dt_bias"][e, :].partition_broadcast(128), W=[Bc])
    p.dma("sp", nexpA, I["dn_a_log"][e, :].partition_broadcast(128), W=[Bc])
    p.act(nexpA, nexpA, AF.Exp, [Bc], [Bc])
    p.ts("dve", nexpA, nexpA, -1.0, None, ALU.mult, None, [Bc], [Bc])
    raw = [alloc(g, [12, 130], F32) for _ in range(2)]
    ba = [alloc(g, [16], F32) for _ in range(2)]
    cs = [alloc(g, [2, 128], F32) for _ in range(2)]
    Braw, Bba, Bcs = [[Buf() for _ in range(2)] for _ in range(3)]
    pk = [alloc(g, [8, 648], F32) for _ in range(2)]
    Bpk = [Buf() for _ in range(2)]
    cv = alloc(g, [12, 128], F32)
    tmpc = alloc(g, [128], F32)
    sqb = alloc(g, [1024], F32)
    rn = alloc(g, [1024], F32)
    qkr = alloc(g, [8, 128], F32)
    t1 = alloc(g, [8, 128], F32)
    ktok = alloc(g, [4, 128], F32)
    vtok = alloc(g, [4, 128], F32)
    sm = alloc(g, [12, 8], F32)
    lrep = alloc(g, [8, 128], F32)
    egB = alloc(g, [8, 128], F32)
    mmx = alloc(g, [8, 128], F32)
    dec = alloc(g, [8, 128], F32)
    decI = alloc(g, [8, 128], F32)
    decS = alloc(g, [8, 128], F32)
    aqk = alloc(g, [8, 128], F32)
    bv = alloc(g, [8, 128], F32)
    bek = alloc(g, [8, 128], F32)
    Pb = [alloc(g, [8, 128], F32) for _ in range(2)]
    Qb = [alloc(g, [8, 128], F32) for _ in range(2)]
    Nb = [alloc(g, [8, 128], F32) for _ in range(2)]
    Bcv, Btc, Bsq, Brn, Bqkr, Bt1, Bkt, Bvt, Bsm, Blr, BeB, Bmx, Bdec, BdI, BdS, Baqk, Bbv, Bbek = [Buf() for _ in range(18)]
    BP = [Buf() for _ in range(2)]
    BQ = [Buf() for _ in range(2)]
    BN = [Buf() for _ in range(2)]
    v4 = lambda ps: ps[:, :].rearrange("p (a b) -> p a b", a=4)
    for it, t in enumerate(tiles):
        i = it % 2
        lat = t >= 2
        c0 = dn_col0(t) - 1
        p.dma("sp", raw[i], S["DNRAW"][:, :, c0:c0 + 130].rearrange("c p t -> p c t"), R=[], W=[Braw[i]])
        p.dma("sp", ba[i], S["BA"][t * 128:(t + 1) * 128, :], R=[], W=[Bba[i]])
        if lat:
            p.dma("sp", cs[i][:, 0, :], I["cosT"][:, (t - 2) * 128:(t - 1) * 128], W=[Bcs[i]])
            p.dma("sp", cs[i][:, 1, :], I["sinT"][:, (t - 2) * 128:(t - 1) * 128], W=[Bcs[i]])
        for c in range(12):
            p.ts("dve", cv[:, c, :], raw[i][:, c, 0:128], cw[:, 0, c:c + 1], None, ALU.mult, None, [Braw[i], Bc], [Bcv])
            p.stt(cv[:, c, :], raw[i][:, c, 1:129], cw[:, 1, c:c + 1], cv[:, c, :], ALU.mult, ALU.add, [Braw[i], Bc, Bcv], [Bcv])
            p.stt(cv[:, c, :], raw[i][:, c, 2:130], cw[:, 2, c:c + 1], cv[:, c, :], ALU.mult, ALU.add, [Braw[i], Bc, Bcv], [Bcv])
        p.act(cv, cv, AF.Silu, [Bcv], [Bcv])
        if g.cfg.get("cutB", 99) <= 1:
            continue
        cvf = cv.rearrange("p a b -> p (a b)")
        p.act(sqb, cvf[:, 0:1024], AF.Square, [Bcv], [Bsq])
        for n in range(2):
            ps, pb = bank(g)
            p.mm(ps[:, :], g.onesF, sqb[:, n * 512:(n + 1) * 512], True, True, [g.Bconst, Bsq], [pb])
            if n == 0:
                p.act(rn[:, 0:512], ps[:, :], AF.Sqrt, [pb], [Brn], scale=128.0, bias=EPS * 128.0)
            else:
                p.act(rn[:, 512:1024], ps[:, :], AF.Sqrt, [pb], [Brn], scale=1.0, bias=EPS)
        p.recip(rn, rn, [Brn], [Brn])
        qk0 = cv[:, 0:8, :]
        p.tt("pool", qk0, qk0, rn.rearrange("p (a b) -> p a b", a=8), ALU.mult, [Bcv, Brn], [Bcv])
        if g.cfg.get("cutB", 99) <= 2:
            continue
        if lat:
            for n in range(2):
                ps, pb = bank(g)
                p.mm(ps[:, :], rotm, cvf[:, n * 512:(n + 1) * 512], True, True, [Bc, Bcv], [pb])
                p.tt("dve", t1[:, n * 4:(n + 1) * 4, :], v4(ps), cs[i][:, 1:2, :].broadcast_to([128, 4, 128]), ALU.mult, [pb, Bcs[i]], [Bt1])
            p.tt("pool", qkr, qk0, cs[i][:, 0:1, :].broadcast_to([128, 8, 128]), ALU.mult, [Bcv, Bcs[i]], [Bqkr])
            p.tt("pool", qkr, qkr, t1, ALU.add, [Bqkr, Bt1], [Bqkr])
            QK_, BQK_ = qkr, Bqkr
        else:
            QK_, BQK_ = qk0, Bcv
        qT = lambda h: QK_[:, h, :]
        kT = lambda h: QK_[:, 4 + h, :]
        if g.cfg.get("cutB", 99) <= 3:
            continue
        ps, pb = bank(g)
        for h in range(4):
            p.tr(ps[:, h * 128:(h + 1) * 128], kT(h), g.identF, [BQK_, g.Bconst], [pb])
        p.copy("act", ktok, v4(ps), [pb], [Bkt])
        ps, pb = bank(g)
        for h in range(4):
            p.tr(ps[:, h * 128:(h + 1) * 128], cv[:, 8 + h, :], g.identF, [Bcv, g.Bconst], [pb])
        p.copy("dve", vtok, v4(ps), [pb], [Bvt])
        if g.cfg.get("cutB", 99) <= 4:
            continue
        beta, nbeta, z, logg, gam, eg, be, glg, kds, glv = [sm[:, r_, :] for r_ in range(10)]
        p.act(beta, ba[i][:, 0:8], AF.Sigmoid, [Bba[i]], [Bsm])
        p.ts("dve", nbeta, beta, -1.0, None, ALU.mult, None, [Bsm], [Bsm])
        p.tt("pool", z, ba[i][:, 8:16], dtb, ALU.add, [Bba[i], Bc], [Bsm])
        p.act(z, z, AF.Exp, [Bsm], [Bsm])
        p.act(z, z, AF.Ln, [Bsm], [Bsm], bias=1.0)
        p.tt("pool", logg, z, nexpA, ALU.mult, [Bsm, Bc], [Bsm])
        ps, pb = bank(g)
        p.mm(ps[:, 0:4], LE, logg[:, 0:4], True, True, [Bc, Bsm], [pb])
        p.mm(ps[:, 4:8], GE, logg[:, 4:8], True, True, [Bc, Bsm], [pb])
        p.copy("dve", gam, ps[:, 0:8], [pb], [Bsm])
        if g.cfg.get("cutB", 99) <= 4.1:
            continue
        p.copy("pool", lrep, logg.unsqueeze(2).broadcast_to([128, 8, 128]), [Bsm], [Blr])
        gps = []
        for d_ in range(2):
            ps, pb = bank(g)
            for h in range(4):
                p.mm(ps[:, h * 128:(h + 1) * 128], lrep[:, d_ * 4 + h, :], LE if d_ == 0 else GE, True, True, [Blr, Bc], [pb])
            gps.append((ps, pb))
        if g.cfg.get("cutB", 99) <= 4.2:
            continue
        p.act(eg, gam, AF.Exp, [Bsm], [Bsm])
        p.tt("pool", be, beta, eg, ALU.mult, [Bsm], [Bsm])
        for d_ in range(2):
            ps, pb = gps[d_]
            last = 127 if d_ == 0 else 0
            p.act(egB[:, d_ * 4:(d_ + 1) * 4, :], v4(ps), AF.Exp, [pb], [BeB])
            p.copy("dve", glg[:, d_ * 4:(d_ + 1) * 4], v4(ps)[:, :, last], [pb], [Bsm])
            for h in range(4):
                dh = d_ * 4 + h
                p.ts("dve", mmx[:, dh, :], ps[:, h * 128:(h + 1) * 128], gam[:, dh:dh + 1], 0.0, ALU.subtract, ALU.max, [pb, Bsm], [Bmx])
        if g.cfg.get("cutB", 99) <= 4.3:
            continue
        p.tt("pool", kds, glg, gam, ALU.subtract, [Bsm], [Bsm])
        p.act(kds, kds, AF.Exp, [Bsm], [Bsm])
        p.act(glv, glg, AF.Exp, [Bsm], [Bsm])
        if g.cfg.get("cutB", 99) <= 4.4:
            continue
        p.act(dec, mmx, AF.Exp, [Bmx], [Bdec], scale=-1.0)
        for d_ in range(2):
            sl_ = slice(d_ * 4, (d_ + 1) * 4)
            p.tt("pool", decI[:, sl_, :], dec[:, sl_, :], (GE if d_ == 0 else LE).unsqueeze(1).broadcast_to([128, 4, 128]), ALU.mult, [Bdec, Bc], [BdI])
            p.tt("pool", decS[:, sl_, :], dec[:, sl_, :], (GT if d_ == 0 else LT).unsqueeze(1).broadcast_to([128, 4, 128]), ALU.mult, [Bdec, Bc], [BdS])
        if g.cfg.get("cutB", 99) <= 5:
            continue
        ps_kk, pb_kk = bank(g)
        for h in range(4):
            p.mm(ps_kk[:, h * 128:(h + 1) * 128], kT(h), kT(h), True, True, [BQK_], [pb_kk])
        ps_qk, pb_qk = bank(g)
        for h in range(4):
            p.mm(ps_qk[:, h * 128:(h + 1) * 128], qT(h), kT(h), True, True, [BQK_], [pb_qk])
        Q, P_, N_ = Qb[0], Pb[0], Nb[0]
        for d_ in range(2):
            for h in range(4):
                dh = d_ * 4 + h
                p.stt(Q[:, dh, :], ps_kk[:, h * 128:(h + 1) * 128], nbeta[:, dh:dh + 1], decS[:, dh, :], ALU.mult, ALU.mult, [pb_kk, Bsm, BdS], [BQ[0]])
            p.tt("dve", aqk[:, d_ * 4:(d_ + 1) * 4, :], v4(ps_qk), decI[:, d_ * 4:(d_ + 1) * 4, :], ALU.mult, [pb_qk, BdI], [Baqk])
        if g.cfg.get("cutB", 99) <= 6:
            continue
        for d_ in range(2):
            ps, pb = bank(g)
            for h in range(4):
                p.tr(ps[:, h * 128:(h + 1) * 128], Q[:, d_ * 4 + h, :], g.identF, [BQ[0], g.Bconst], [pb])
            p.copy("act", P_[:, d_ * 4:(d_ + 1) * 4, :], v4(ps), [pb], [BP[0]])
            ps, pb = bank(g)
            for h in range(4):
                p.tr(ps[:, h * 128:(h + 1) * 128], aqk[:, d_ * 4 + h, :], g.identF, [Baqk, g.Bconst], [pb])
            p.copy("dve", pk[i][:, d_ * 4:(d_ + 1) * 4, 256:384], v4(ps), [pb], [Bpk[i]])
        if g.cfg.get("cutB", 99) <= 7:
            continue
        p.tt("pool", N_, P_, g.identF.unsqueeze(1).broadcast_to([128, 8, 128]), ALU.add, [BP[0], g.Bconst], [BN[0]])
        cur = 0
        for lev in range(6):
            nxt = 1 - cur
            lastlev = (lev == 5)
            for d_ in range(2):
                sl_ = slice(d_ * 4, (d_ + 1) * 4)
                if not lastlev:
                    ps, pb = bank(g)
                    for h in range(4):
                        dh = d_ * 4 + h
                        p.mm(ps[:, h * 128:(h + 1) * 128], Qb[cur][:, dh, :], Pb[cur][:, dh, :], True, True, [BQ[cur], BP[cur]], [pb])
                    p.copy("act", Pb[nxt][:, sl_, :], v4(ps), [pb], [BP[nxt]])
                ps, pb = bank(g)
                for h in range(4):
                    dh = d_ * 4 + h
                    p.mm(ps[:, h * 128:(h + 1) * 128], Pb[cur][:, dh, :], Qb[cur][:, dh, :], True, True, [BQ[cur], BP[cur]], [pb])
                p.copy("act" if lastlev else "dve", Qb[nxt][:, sl_, :], v4(ps), [pb], [BQ[nxt]])
            for d_ in range(2):
                sl_ = slice(d_ * 4, (d_ + 1) * 4)
                ps, pb = bank(g)
                for h in range(4):
                    dh = d_ * 4 + h
                    p.mm(ps[:, h * 128:(h + 1) * 128], Qb[nxt][:, dh, :], Nb[cur][:, dh, :], True, True, [BQ[nxt], BN[cur]], [pb])
                p.tt("dve", Nb[nxt][:, sl_, :], v4(ps), Nb[cur][:, sl_, :], ALU.add, [pb, BN[cur]], [BN[nxt]])
            cur = nxt
        TT, BTT = Nb[cur], BN[cur]
        if g.cfg.get("cutB", 99) <= 8:
            continue
        for d_ in range(2):
            for h in range(4):
                dh = d_ * 4 + h
                p.act(bv[:, dh, :], vtok[:, h, :], AF.Copy, [Bvt, Bsm], [Bbv], scale=beta[:, dh:dh + 1])
                p.act(bek[:, dh, :], ktok[:, h, :], AF.Copy, [Bkt, Bsm], [Bbek], scale=be[:, dh:dh + 1])
                p.ts("dve", pk[i][:, dh, 512:640], ktok[:, h, :], kds[:, dh:dh + 1], None, ALU.mult, None, [Bkt, Bsm], [Bpk[i]])
            p.tt("pool", pk[i][:, d_ * 4:(d_ + 1) * 4, 384:512], QK_[:, 0:4, :], egB[:, d_ * 4:(d_ + 1) * 4, :], ALU.mult, [BQK_, BeB], [Bpk[i]])
        p.copy("pool", pk[i][:, :, 640:641], glv.unsqueeze(2), [Bsm], [Bpk[i]])
        for d_ in range(2):
            ps, pb = bank(g)
            for h in range(4):
                dh = d_ * 4 + h
                p.mm(ps[:, h * 128:(h + 1) * 128], TT[:, dh, :], bv[:, dh, :], True, True, [BTT, Bbv], [pb])
            p.copy("act", pk[i][:, d_ * 4:(d_ + 1) * 4, 0:128], v4(ps), [pb], [Bpk[i]])
            ps, pb = bank(g)
            for h in range(4):
                dh = d_ * 4 + h
                p.mm(ps[:, h * 128:(h + 1) * 128], bek[:, dh, :], TT[:, dh, :], True, True, [BTT, Bbek], [pb])
            p.copy("dve", pk[i][:, d_ * 4:(d_ + 1) * 4, 128:256], v4(ps), [pb], [Bpk[i]])
        if g.cfg.get("cutB", 99) <= 9:
            continue
        p.dma("pool", S["DNP"][t].rearrange("d p c -> p d c"), pk[i], R=[Bpk[i]], W=[])


def even_dnscan(g, l, e, ctx_out):
    p, I, S = g.p, g.I, g.S
    new_phase(g)
    lat_tiles = [t for t in tile_list(g, l, "proj") if t >= 2]
    Bc = Buf()
    goutb = alloc(g, [128], F32)
    p.dma("sp", goutb, I["dn_g_out"][e, :].partition_broadcast(128), W=[Bc])
    Sst = alloc(g, [8, 128], F32)
    BS = [Buf() for _ in range(8)]
    for dh in range(8):
        p.memset("pool", Sst[:, dh, :], 0.0, [BS[dh]])
    NPK = 6
    pk = [alloc(g, [648], F32) for _ in range(NPK)]
    Bpk = [Buf() for _ in range(NPK)]
    u_sb = [alloc(g, [128], F32) for _ in range(4)]
    Bu = [Buf() for _ in range(4)]
    o_sb = [alloc(g, [4, 128], F32) for _ in range(2)]
    Bo = [Buf() for _ in range(2)]
    of_sb = [alloc(g, [4, 128], F32) for _ in range(2)]
    Bof = [Buf() for _ in range(2)]
    gate = [alloc(g, [512], F32) for _ in range(2)]
    Bgate = [Buf() for _ in range(2)]
    y1 = alloc(g, [4, 128], F32)
    ydn = alloc(g, [512], BF16)
    zt = [alloc(g, [4, 128], BF16) for _ in range(2)]
    Bzt = [Buf() for _ in range(2)]
    st = alloc(g, [3, 4], F32)
    junk = alloc(g, [128], BF16)
    By1, Bydn, Bst, Bj = Buf(), Buf(), Buf(), Buf()
    ipk = 0
    iu = 0
    for d_ in range(2):
        order = [0, 1] + lat_tiles if d_ == 0 else [1, 0] + lat_tiles[::-1]
        for it, t in enumerate(order):
            i = it % 2
            want_o = (t >= 2) or ctx_out
            if d_ == 1 and want_o:
                p.dma("sp", of_sb[i].rearrange("p a b -> p (a b)"), S["OF"][t * 128:(t + 1) * 128, :], R=[g.DBT("OF", t)], W=[Bof[i]])
                p.dma("sp", gate[i], S["GATE"][t * 128:(t + 1) * 128, :], R=[], W=[Bgate[i]])
            for h in range(4):
                dh = d_ * 4 + h
                pk_, bpk_ = pk[ipk % NPK], Bpk[ipk % NPK]
                ipk += 1
                p.dma("sp", pk_, S["DNP"][t, dh], R=[], W=[bpk_])
                u_, bu_ = u_sb[iu % 4], Bu[iu % 4]
                iu += 1
                ps1, pb1 = bank(g)
                p.mm(ps1[:, 0:128], pk_[:, 128:256], Sst[:, dh, :], True, True, [bpk_, BS[dh]], [pb1])
                p.tt("dve", u_, pk_[:, 0:128], ps1[:, 0:128], ALU.subtract, [bpk_, pb1], [bu_])
                if want_o:
                    ps2, pb2 = bank(g)
                    p.mm(ps2[:, 0:128], pk_[:, 384:512], Sst[:, dh, :], True, False, [bpk_, BS[dh]], [pb2])
                    p.mm(ps2[:, 0:128], pk_[:, 256:384], u_, False, True, [bpk_, bu_], [pb2])
                ps3, pb3 = bank(g)
                p.mm(ps3[:, 0:128], pk_[:, 512:640], u_, True, True, [bpk_, bu_], [pb3])
                p.stt(Sst[:, dh, :], Sst[:, dh, :], pk_[:, 640:641], ps3[:, 0:128], ALU.mult, ALU.add, [BS[dh], bpk_, pb3], [BS[dh]])
                if want_o:
                    if d_ == 0:
                        p.copy("act", o_sb[i][:, h, :], ps2[:, 0:128], [pb2], [Bo[i]])
                    else:
                        p.tt("dve", o_sb[i][:, h, :], ps2[:, 0:128], of_sb[i][:, h, :], ALU.add, [pb2, Bof[i]], [Bo[i]])
            if not want_o:
                continue
            if d_ == 0:
                p.dma("pool", S["OF"][t * 128:(t + 1) * 128, :], o_sb[i].rearrange("p a b -> p (a b)"), R=[Bo[i]], W=[g.DBT("OF", t)])
                continue
            for h in range(4):
                p.act(junk, o_sb[i][:, h, :], AF.Square, [Bo[i]], [Bj, Bst], accum_out=st[:, 0, h:h + 1])
            p.act(st[:, 1, :], st[:, 0, :], AF.Sqrt, [Bst], [Bst], scale=1.0 / 128, bias=EPS)
            p.recip(st[:, 2, :], st[:, 1, :], [Bst], [Bst])
            p.act(gate[i], gate[i], AF.Silu, [Bgate[i]], [Bgate[i]])
            for h in range(4):
                p.stt(y1[:, h, :], o_sb[i][:, h, :], st[:, 2, h:h + 1], goutb, ALU.mult, ALU.mult, [Bo[i], Bst, Bc], [By1])
            p.tt("pool", ydn, y1.rearrange("p a b -> p (a b)"), gate[i], ALU.mult, [By1, Bgate[i]], [Bydn])
            ps, pb = bank(g)
            psb = ps[:, :].bitcast(BF16)
            for h in range(4):
                p.tr(psb[:, h * 128:(h + 1) * 128], ydn[:, h * 128:(h + 1) * 128], g.identB, [Bydn, g.Bconst], [pb])
            p.copy("act", zt[i], psb[:, 0:512].rearrange("p (a b) -> p a b", a=4), [pb], [Bzt[i]])
            p.dma("pool", S["ZT"][4:8, :, t * 128:(t + 1) * 128].rearrange("c p t -> p c t"), zt[i], R=[Bzt[i]], W=[])


def even_na(g, l, e, ctx_out):
    p, I, S = g.p, g.I, g.S
    new_phase(g)
    nrows = 64
    lim = g.cfg.get("ntiles")
    if lim:
        nrows = g.cfg.get("narows") or 2 * lim
    Bc = Buf()
    KT = alloc(g, [4, NTOK], BF16)
    p.dma("sp", KT, S["QK"][4:8, :, :].rearrange("c p t -> p c t"), R=[], W=[Bc])
    VAs = alloc(g, [NT, 520], BF16)
    p.dma("sp", VAs, S["VA"].rearrange("(t p) f -> p t f", p=128), R=[], W=[Bc])
    tabf = alloc(g, [8, 16, 64], F32)
    maskf = alloc(g, [64], F32)
    TT = alloc(g, [8, 16, 64], BF16)
    p.memset("pool", tabf, 0.0, [Bc])
    for h in range(8):
        p.dma("sp", tabf[0:64, h, 0:15, :], I["rpb_tab"][e, h].rearrange("b k c -> k b c"), W=[Bc])
        p.dma("sp", tabf[64:128, h, 0:14, :], I["rpb_tab"][e, h, 1:15].rearrange("b k c -> k b c"), W=[Bc])
    p.dma("sp", maskf[0:64, :], I["namask"][:, :], W=[Bc])
    p.dma("sp", maskf[64:128, :], I["namask"][:, :], W=[Bc])
    for h in range(8):
        p.tt("pool", TT[:, h, :, :], tabf[:, h, :, :], maskf.unsqueeze(1).broadcast_to([128, 16, 64]), ALU.add, [Bc], [Bc])
    qT = [alloc(g, [4, 128], BF16) for _ in range(2)]
    BqT = [Buf() for _ in range(2)]
    PT = [alloc(g, [512], BF16) for _ in range(3)]
    BPT = [Buf() for _ in range(3)]
    att = [alloc(g, [8, 64], BF16) for _ in range(2)]
    Batt = [Buf() for _ in range(2)]
    rinv = [alloc(g, [8], F32) for _ in range(2)]
    Brinv = [Buf() for _ in range(2)]
    zt = [alloc(g, [4, 128], BF16) for _ in range(2)]
    Bzt = [Buf() for _ in range(2)]
    ipt = 0
    stc = [0]

    def st_bank():
        i_ = 4 + stc[0] % 3
        stc[0] += 1
        return g.ps[i_], g.PB[i_]

    def oa_banks(k):
        b0 = (k % 2) * 2
        return [(g.ps[b0], g.PB[b0]), (g.ps[b0 + 1], g.PB[b0 + 1])]

    def finish(oa, i, tq):
        for bnk in range(2):
            ps, pb = oa[bnk]
            v = ps[:, 0:260].rearrange("p (a b) -> p a b", a=4)
            p.recip(rinv[i][:, bnk * 4:(bnk + 1) * 4], v[:, :, 64], [pb], [Brinv[i]])
            p.tt("dve", att[i][:, bnk * 4:(bnk + 1) * 4, :], v[:, :, 0:64],
                 rinv[i][:, bnk * 4:(bnk + 1) * 4].unsqueeze(2).broadcast_to([128, 4, 64]), ALU.mult, [pb, Brinv[i]], [Batt[i]])
        ps, pb = g.ps[7], g.PB[7]
        psb = ps[:, :].bitcast(BF16)
        af = att[i].rearrange("p a b -> p (a b)")
        for c in range(4):
            p.tr(psb[:, c * 128:(c + 1) * 128], af[:, c * 128:(c + 1) * 128], g.identB, [Batt[i], g.Bconst], [pb])
        p.copy("act", zt[i], psb[:, 0:512].rearrange("p (a b) -> p a b", a=4), [pb], [Bzt[i]])
        p.dma("pool", S["ZT"][0:4, :, tq * 128:(tq + 1) * 128].rearrange("c p t -> p c t"), zt[i], R=[Bzt[i]], W=[])

    if ctx_out:
        qc = alloc(g, [4, 256], BF16)
        Bqc = Buf()
        p.dma("sp", qc, S["QK"][0:4, :, 0:256].rearrange("c p t -> p c t"), R=[], W=[Bqc])
        PTc = [alloc(g, [512], BF16) for _ in range(2)]
        BPTc = [Buf() for _ in range(2)]
        oa = [oa_banks(0), oa_banks(1)]
        for h in range(8):
            pr, pb_ = h // 2, (h % 2) * 64
            ps, pb = st_bank()
            for j in range(2):
                p.mm(ps[:, j * 256:(j + 1) * 256], KT[pb_:pb_ + 64, pr, j * 128:(j + 1) * 128], qc[pb_:pb_ + 64, pr, :], True, True, [Bc, Bqc], [pb])
            P_, BP_ = PTc[h % 2], BPTc[h % 2]
            p.act(P_, ps[:, :], AF.Exp, [pb], [BP_])
            for qt in range(2):
                ops_, opb = oa[qt][h // 4]
                hh = h % 4
                for j in range(2):
                    p.mm(ops_[:, hh * 65:(hh + 1) * 65], P_[:, j * 256 + qt * 128: j * 256 + (qt + 1) * 128], VAs[:, j, h * 65:(h + 1) * 65],
                         j == 0, j == 1, [BP_, Bc], [opb])
        for qt in range(2):
            finish(oa[qt], qt, qt)
    for r in range(nrows):
        tq = 2 + r // 2
        iq = (r // 2) % 2
        hq = r % 2
        if hq == 0:
            p.dma("sp", qT[iq], S["QK"][0:4, :, tq * 128:(tq + 1) * 128].rearrange("c p t -> p c t"), R=[], W=[BqT[iq]])
            oa = oa_banks(r // 2)
        rs = min(max(r - 4, 0), 56)
        units = []
        wr = rs
        while wr < rs + 8:
            if wr % 2 == 0 and wr + 1 < rs + 8:
                units.append((wr, 2))
                wr += 2
            else:
                units.append((wr, 1))
                wr += 1
        nloc = (rs + 7) // 2 - rs // 2 + 1
        for h in range(8):
            pr, pb_ = h // 2, (h % 2) * 64
            ps, pb = st_bank()
            q_ap = qT[iq][pb_:pb_ + 64, pr, hq * 64:(hq + 1) * 64]
            geo = []
            for (wr, n) in units:
                slot = wr // 2 - rs // 2
                col = 256 + wr * 64
                if n == 2:
                    lo, hi, b = 0, 128, wr - r + 7
                elif wr % 2 == 1:
                    lo, hi, b = 64, 128, wr - r + 6
                else:
                    lo, hi, b = 0, 64, wr - r + 7
                geo.append((wr, slot, lo, hi))
                out = ps[lo:hi, slot * 64:(slot + 1) * 64]
                p.mm(out, KT[pb_:pb_ + 64, pr, col:col + (hi - lo)], q_ap, True, False, [Bc, BqT[iq]], [pb])
                p.mm(out, g.identB[:, lo:hi], TT[:, h, b, :], False, True, [g.Bconst, Bc], [pb])
            for j in range(2):
                slot = nloc + j
                p.mm(ps[:, slot * 64:(slot + 1) * 64], KT[pb_:pb_ + 64, pr, j * 128:(j + 1) * 128], q_ap, True, True, [Bc, BqT[iq]], [pb])
            ncol = (nloc + 2) * 64
            P_, BP_ = PT[ipt % 3], BPT[ipt % 3]
            ipt += 1
            p.act(P_[:, 0:ncol], ps[:, 0:ncol], AF.Exp, [pb], [BP_])
            ops_, opb = oa[h // 4]
            hh = h % 4
            out = ops_[hq * 64:(hq + 1) * 64, hh * 65:(hh + 1) * 65]
            nmm = len(geo) + 2
            for ii, (wr, slot, lo, hi) in enumerate(geo):
                p.mm(out, P_[lo:hi, slot * 64:(slot + 1) * 64], VAs[lo:hi, 2 + wr // 2, h * 65:(h + 1) * 65], ii == 0, False, [BP_, Bc], [opb])
            for j in range(2):
                slot = nloc + j
                p.mm(out, P_[:, slot * 64:(slot + 1) * 64], VAs[:, j, h * 65:(h + 1) * 65], False, j == 1, [BP_, Bc], [opb])
        if hq == 1:
            finish(oa, iq, tq)


def even_out(g, l, e):
    p, I, S = g.p, g.I, g.S
    new_phase(g)
    tiles = tile_list(g, l, "mixout")
    Wo = alloc(g, [8, 1024], BF16)
    BWo = Buf()
    load_w_bf16(g, Wo, I["w_out_even"][e].rearrange("(k p) n -> p k n", p=128), BWo)
    gates = [load_gate(g, l, s_, 0) for s_ in range(2)]
    x = [alloc(g, [D], F32) for _ in range(2)]
    Bx = [Buf() for _ in range(2)]
    zt = [alloc(g, [8, 128], BF16) for _ in range(2)]
    Bzt = [Buf() for _ in range(2)]
    tmp = [alloc(g, [512], F32) for _ in range(2)]
    Btmp = [Buf() for _ in range(2)]
    for it, t in enumerate(tiles):
        i = it % 2
        s_ = 1 if t < 2 else 0
        p.dma("sp", x[i], S["X"][t * 128:(t + 1) * 128, :], R=[g.DBT("X", t)], W=[Bx[i]])
        p.dma("sp", zt[i], S["ZT"][:, :, t * 128:(t + 1) * 128].rearrange("c p t -> p c t"), R=[], W=[Bzt[i]])
        pss = []
        for n in range(2):
            ps, pb = bank(g)
            for k in range(8):
                p.mm(ps[:, :], zt[i][:, k, :], Wo[:, k, n * 512:(n + 1) * 512], k == 0, k == 7, [Bzt[i], BWo], [pb])
            pss.append((ps, pb))
        gt, Bgt = gates[s_]
        resid_update(g, pss, x[i], Bx[i], gt, Bgt, tmp, Btmp)
        p.dma("pool", S["X"][t * 128:(t + 1) * 128, :], x[i], R=[Bx[i]], W=[g.DBT("X", t)])


def resid_update(g, ps_list, x, Bx, gt, Bgt, tmp, Btmp):
    p = g.p
    for n, (ps, pb) in enumerate(ps_list):
        p.tt("dve", tmp[n], ps[:, :], gt[:, n * 512:(n + 1) * 512], ALU.mult, [pb, Bgt], [Btmp[n]])
        p.tt("pool", x[:, n * 512:(n + 1) * 512], tmp[n], x[:, n * 512:(n + 1) * 512], ALU.add, [Btmp[n], Bx], [Bx])


def phase_odd(g, l):
    p, I, S = g.p, g.I, g.S
    o = l // 2
    tiles = tile_list(g, l, "mixout")
    w_in = I["w_in_odd"][o].rearrange("(k p) n -> p k n", p=128)
    new_phase(g)
    Wv = alloc(g, [8, 3072], BF16)
    BWv = Buf()
    load_w_bf16(g, Wv, w_in[:, :, 3072:6144], BWv, max_cols=1024)
    wsT = alloc(g, [8, 128], BF16)
    bsT = alloc(g, [8], F32)
    gvb = alloc(g, [3072], F32)
    Bc = Buf()
    p.dma("pool", wsT, I["gm_wsT"][o], W=[Bc], stream="wl")
    p.dma("sp", bsT, I["gm_bsT"][o], W=[Bc])
    p.dma("sp", gvb, I["gm_g_v"][o, :].partition_broadcast(128), W=[Bc])
    mods = [load_mod(g, l, s_, 0) for s_ in range(2)]
    nb = NormBufs(g)
    vf = [alloc(g, [3072], F32) for _ in range(2)]
    Bvf = [Buf() for _ in range(2)]
    vn = [alloc(g, [3072], BF16) for _ in range(2)]
    Bvn = [Buf() for _ in range(2)]
    junk = alloc(g, [3072], BF16)
    Bj = Buf()
    st = [alloc(g, [4], F32) for _ in range(2)]
    Bst = [Buf() for _ in range(2)]
    mix = [alloc(g, [3072], F32) for _ in range(2)]
    Bmix = [Buf() for _ in range(2)]
    tmpg = [alloc(g, [384], F32) for _ in range(2)]
    Btg = [Buf() for _ in range(2)]
    def stage0(it, t):
        return norm_T(g, nb, it % 2, t, mods[1 if t < 2 else 0])

    def stage1(it, t, cur):
        i = it % 2
        s_ = 1 if t < 2 else 0
        hT, BhT = cur
        for n in range(6):
            ps, pb = bank(g)
            for k in range(8):
                p.mm(ps[:, :], hT[:, k, :], Wv[:, k, n * 512:(n + 1) * 512], k == 0, k == 7, [BhT, BWv], [pb])
            p.act(vf[i][:, n * 512:(n + 1) * 512], ps[:, :], AF.Gelu, [pb], [Bvf[i]])
        p.act(junk, vf[i], AF.Square, [Bvf[i]], [Bj, Bst[i]], accum_out=st[i][:, 0:1])
        p.act(st[i][:, 1:2], st[i][:, 0:1], AF.Sqrt, [Bst[i]], [Bst[i]], scale=1.0 / 3072, bias=EPS)
        p.recip(st[i][:, 2:3], st[i][:, 1:2], [Bst[i]], [Bst[i]])
        p.ts("dve", vn[i], vf[i], st[i][:, 2:3], None, ALU.mult, None, [Bvf[i], Bst[i]], [Bvn[i]])
        for gi in range(8):
            ps, pb = bank(g)
            p.mm(ps[:, 0:384], wsT[:, gi, :], vn[i][:, gi * 384:(gi + 1) * 384], True, True, [Bc, Bvn[i]], [pb])
            tg, btg = tmpg[gi % 2], Btg[gi % 2]
            p.tt("dve", tg, ps[:, 0:384], gvb[:, gi * 384:(gi + 1) * 384], ALU.mult, [pb, Bc], [btg])
            p.ts("dve", mix[i][:, gi * 384:(gi + 1) * 384], tg, bsT[:, gi:gi + 1], None, ALU.add, None, [btg, Bc], [Bmix[i]])
        p.dma("pool", S["MIX"][t * 128:(t + 1) * 128, :], mix[i], R=[Bmix[i]], W=[])
    pipelined(tiles, stage0, stage1)
    new_phase(g)
    Wu = alloc(g, [8, 3072], BF16)
    Wo = alloc(g, [24, 1024], BF16)
    BWu, BWo = Buf(), Buf()
    load_w_bf16(g, Wu, w_in[:, :, 0:3072], BWu, max_cols=1024)
    load_w_bf16(g, Wo, I["w_out_odd"][o].rearrange("(k p) n -> p k n", p=128), BWo)
    mods = [load_mod(g, l, s_, 0) for s_ in range(2)]
    gates = [load_gate(g, l, s_, 0) for s_ in range(2)]
    nb = NormBufs(g, nx=3)
    mix = [alloc(g, [3072], F32) for _ in range(2)]
    Bmix = [Buf() for _ in range(2)]
    uf = [alloc(g, [512], F32) for _ in range(2)]
    Buf_ = [Buf() for _ in range(2)]
    sb = [alloc(g, [3072], BF16) for _ in range(2)]
    Bsb = [Buf() for _ in range(2)]
    sT = [alloc(g, [24, 128], BF16) for _ in range(2)]
    BsT = [Buf() for _ in range(2)]
    tmp = [alloc(g, [512], F32) for _ in range(2)]
    Btmp = [Buf() for _ in range(2)]
    uic = [0]

    def stage0(it, t):
        p.dma("sp", mix[it % 2], S["MIX"][t * 128:(t + 1) * 128, :], R=[], W=[Bmix[it % 2]])
        return norm_T(g, nb, it % 2, t, mods[1 if t < 2 else 0], ix=it % 3)

    def stage1(it, t, cur):
        i = it % 2
        s_ = 1 if t < 2 else 0
        hT, BhT = cur
        for n in range(6):
            ps, pb = bank(g)
            for k in range(8):
                p.mm(ps[:, :], hT[:, k, :], Wu[:, k, n * 512:(n + 1) * 512], k == 0, k == 7, [BhT, BWu], [pb])
            u_, bu_ = uf[uic[0] % 2], Buf_[uic[0] % 2]
            uic[0] += 1
            p.act(u_, ps[:, :], AF.Gelu, [pb], [bu_])
            p.tt("pool", sb[i][:, n * 512:(n + 1) * 512], u_, mix[i][:, n * 512:(n + 1) * 512], ALU.mult, [bu_, Bmix[i]], [Bsb[i]])
        for q in range(3):
            ps, pb = bank(g)
            psb = ps[:, :].bitcast(BF16)
            for kk in range(8):
                k = q * 8 + kk
                p.tr(psb[:, kk * 128:(kk + 1) * 128], sb[i][:, k * 128:(k + 1) * 128], g.identB, [Bsb[i], g.Bconst], [pb])
            dst = sT[i][:, q * 8:(q + 1) * 8, :]
            src = psb.rearrange("p (a b) -> p a b", a=8)
            if q % 2 == 0:
                p.copy("act", dst, src, [pb], [BsT[i]])
            else:
                p.copy("dve", dst, src, [pb], [BsT[i]])
        pss = []
        for n in range(2):
            ps, pb = bank(g)
            for k in range(24):
                p.mm(ps[:, :], sT[i][:, k, :], Wo[:, k, n * 512:(n + 1) * 512], k == 0, k == 23, [BsT[i], BWo], [pb])
            pss.append((ps, pb))
        gt, Bgt = gates[s_]
        resid_update(g, pss, nb.x[it % 3], nb.Bx[it % 3], gt, Bgt, tmp, Btmp)
        p.dma("pool", S["X"][t * 128:(t + 1) * 128, :], nb.x[it % 3], R=[nb.Bx[it % 3]], W=[g.DBT("X", t)])


    pipelined(tiles, stage0, stage1)


def make_in_maps(inputs):
    f = lambda a: np.ascontiguousarray(np.asarray(a, dtype=np.float32))
    shared = {}
    for k in ["w_ada", "b_ada", "g_norm_mix", "g_norm_ffn", "w_in_even", "w_out_even", "dn_conv", "dn_g_out",
              "w_in_odd", "gm_g_v", "w_out_odd", "w_ff1", "w_ff2", "g_final"]:
        shared[k] = f(inputs[k])
    shared["rpb_tab"] = rpb_table(f(inputs["na_rpb"]))
    shared["dn_a_log"] = f(inputs["dn_a_log"]).reshape(2, 8)
    shared["dn_dt_bias"] = f(inputs["dn_dt_bias"]).reshape(2, 8)
    shared["gm_wsT"] = np.ascontiguousarray(f(inputs["gm_ws"]).transpose(0, 3, 1, 2))
    shared["gm_bsT"] = np.ascontiguousarray(f(inputs["gm_bs"]).transpose(0, 2, 1))
    shared.update(host_consts())
    x = f(inputs["x"])
    ctx = f(inputs["ctx"])
    c = f(inputs["c"])
    c_ctx = f(inputs["c_ctx"])
    maps = []
    for b in range(8):
        m = dict(shared)
        m["x"] = x[b]
        m["ctx"] = ctx[b]
        m["cvec"] = np.ascontiguousarray(np.stack([c[b], c_ctx]))
        maps.append(m)
    return maps


def kernel(**inputs):
    nc, g = build()
    maps = make_in_maps(inputs)
    res = run_bass_kernel_spmd(nc, maps, core_ids=list(range(8)))
    return np.stack([np.asarray(r["out"], dtype=np.float32) for r in res.results], axis=0)
```

```python
import numpy as np
from contextlib import ExitStack
import concourse.bass as bass
import concourse.mybir as mybir
from concourse.bass_utils import run_bass_kernel_spmd

F32 = mybir.dt.float32
BF16 = mybir.dt.bfloat16
AF = mybir.ActivationFunctionType
ALU = mybir.AluOpType

D = 1024
L = 4096
LC = 256
NT = (L + LC) // 128
NTOK = L + LC
EPS = 1e-6
DEPTH = 4
EVEN_IN = 3600
NPAD = NTOK + 4
NEG = -30000.0


class Buf:
    __slots__ = ("name", "w", "r", "excl")

    def __init__(self, name="", excl=False):
        self.name = name
        self.w = None
        self.r = []
        self.excl = excl


class Prog:
    ENGS = ("pe", "act", "dve", "pool", "sp")

    def __init__(self, nc):
        self.nc = nc
        self.ops = []
        self.stream_cnt = {}
        self.stream_last = {}
        self.stream_R = {"ld": 8, "st": 8, "wl": 12}
        self.last_real = {}

    def sem_names(self):
        names = ["e_" + e for e in self.ENGS]
        for st, R in self.stream_R.items():
            names += ["s_%s%d" % (st, j) for j in range(R)]
        return names

    def add(self, eng, fn, reads=(), writes=(), stream=None, extra=()):
        i = len(self.ops)
        deps = set(extra)
        xr = [b for b in reads if b.excl and eng != "pe"]
        if xr:
            reads = [b for b in reads if not (b.excl and eng != "pe")]
            writes = list(writes) + xr
        for b in reads:
            if b.w is not None:
                deps.add(b.w)
        for b in writes:
            if b.w is not None:
                deps.add(b.w)
            deps.update(b.r)
        for b in reads:
            b.r.append(i)
        for b in writes:
            b.w = i
            b.r = []
        val = None
        if stream is not None:
            n = self.stream_cnt.get(stream, 0)
            self.stream_cnt[stream] = n + 1
            R = self.stream_R[stream]
            stream = "%s%d" % (stream, n % R)
            val = 16 * (n // R + 1)
            prev = self.stream_last.get(stream)
            if prev is not None:
                deps.add(prev)
            self.stream_last[stream] = i
        elif fn is not None:
            self.last_real[eng] = i
        self.ops.append([eng, fn, deps, stream, False, val])
        return i

    def barrier(self):
        ex = set(self.last_real.values()) | set(self.stream_last.values())
        for e in self.ENGS:
            self.add(e, None, extra=ex)

    def dma(self, q, out, in_, R=(), W=(), stream=None, **kw):
        if stream is None:
            stream = "st" if q == "pool" else "ld"
        return self.add(q, lambda e: e.dma_start(out=out, in_=in_, **kw), R, W, stream=stream)

    def mm(self, out, lhsT, rhs, start, stop, R, W):
        return self.add("pe", lambda e: e.matmul(out, lhsT=lhsT, rhs=rhs, start=start, stop=stop), R, W)

    def tr(self, out, in_, ident, R, W):
        return self.add("pe", lambda e: e.transpose(out=out, in_=in_, identity=ident), R, W)

    def act(self, out, in_, func, R, W, **kw):
        return self.add("act", lambda e: e.activation(out=out, in_=in_, func=func, **kw), R, W)

    def ts(self, eng, out, in0, s1, s2, op0, op1, R, W):
        if s2 is None:
            return self.add(eng, lambda e: e.tensor_scalar(out=out, in0=in0, scalar1=s1, scalar2=None, op0=op0), R, W)
        return self.add(eng, lambda e: e.tensor_scalar(out=out, in0=in0, scalar1=s1, scalar2=s2, op0=op0, op1=op1), R, W)

    def tt(self, eng, out, in0, in1, op, R, W):
        return self.add(eng, lambda e: e.tensor_tensor(out=out, in0=in0, in1=in1, op=op), R, W)

    def stt(self, out, in0, scalar, in1, op0, op1, R, W):
        return self.add("dve", lambda e: e.scalar_tensor_tensor(out=out, in0=in0, scalar=scalar, in1=in1, op0=op0, op1=op1), R, W)

    def copy(self, eng, out, in_, R, W):
        if eng == "act":
            return self.add("act", lambda e: e.activation(out=out, in_=in_, func=AF.Copy), R, W)
        return self.add(eng, lambda e: e.tensor_copy(out=out, in_=in_), R, W)

    def memset(self, eng, out, val, W):
        return self.add(eng, lambda e: e.memset(out, val), (), W)

    def recip(self, out, in_, R, W):
        return self.add("dve", lambda e: e.reciprocal(out=out, in_=in_), R, W)

    def emit(self, sems):
        ops = self.ops
        for (eng, fn, deps, stream, sig, val) in ops:
            for d in deps:
                po = ops[d]
                if po[3] is None:
                    if po[0] == "pe" and eng == "pe":
                        continue
                    po[4] = True
        cnt = {e: 0 for e in self.ENGS}
        for o in ops:
            if o[3] is None and o[4]:
                cnt[o[0]] += 1
                o[5] = cnt[o[0]]
        per_eng = {e: [] for e in self.ENGS}
        waited = {e: {} for e in self.ENGS}
        for (eng, fn, deps, stream, sig, val) in ops:
            need = {}
            for d in deps:
                po = ops[d]
                if po[3] is None:
                    if po[0] == "pe" and eng == "pe":
                        continue
                    if po[1] is None:
                        continue
                    key = "e_" + po[0]
                else:
                    key = "s_" + po[3]
                need[key] = max(need.get(key, 0), po[5])
            waits = []
            for key, v in need.items():
                if waited[eng].get(key, 0) >= v:
                    continue
                waited[eng][key] = v
                waits.append((key, v))
            per_eng[eng].append((waits, fn, stream, sig))
        self.n_inst = {e: len(v) for e, v in per_eng.items()}

        def run(engobj, lst, ename):
            for (waits, fn, stream, sig) in lst:
                for (key, v) in waits:
                    engobj.wait_ge(sems[key], v)
                if fn is None:
                    continue
                ins = fn(engobj)
                if stream is not None:
                    ins.then_inc(sems["s_" + stream], 16)
                elif sig:
                    ins.then_inc(sems["e_" + ename], 1)

        with self.nc.Block() as block:
            @block.tensor
            def _(e):
                run(e, per_eng["pe"], "pe")

            @block.scalar
            def _(e):
                run(e, per_eng["act"], "act")

            @block.vector
            def _(e):
                run(e, per_eng["dve"], "dve")

            @block.gpsimd
            def _(e):
                run(e, per_eng["pool"], "pool")

            @block.sync
            def _(e):
                run(e, per_eng["sp"], "sp")


def host_consts():
    c = {}
    c["ident"] = np.eye(128, dtype=np.float32)
    c["ones"] = np.ones((128, 128), np.float32)
    idx = np.arange(128)
    c["tri"] = np.stack([(idx[:, None] <= idx[None, :]), (idx[:, None] >= idx[None, :]),
                         (idx[:, None] > idx[None, :]), (idx[:, None] < idx[None, :])]).astype(np.float32)
    rm = np.zeros((128, 128), np.float32)
    for j in range(32):
        rm[32 + j, j] = -1.0
        rm[j, 32 + j] = 1.0
        rm[96 + j, 64 + j] = -1.0
        rm[64 + j, 96 + j] = 1.0
    c["rotm"] = rm
    t = np.arange(L)
    row = (t // 64).astype(np.float32)
    col = (t % 64).astype(np.float32)
    inv = (10000.0 ** (-np.arange(32, dtype=np.float32) / 32)).astype(np.float32)
    ar = row[:, None] * inv
    ac = col[:, None] * inv
    ang = np.concatenate([ar, ar, ac, ac], axis=-1)
    c["cosT"] = np.ascontiguousarray(np.cos(ang).T.astype(np.float32))
    c["sinT"] = np.ascontiguousarray(np.sin(ang).T.astype(np.float32))
    cc = np.arange(64)
    cstart = np.clip(cc - 8, 0, 48)
    ok = (cc[:, None] >= cstart[None, :]) & (cc[:, None] < cstart[None, :] + 16)
    c["namask"] = np.where(ok, 0.0, NEG).astype(np.float32)
    return c


def rpb_table(na_rpb):
    cc = np.arange(64)
    dc = np.clip(cc[:, None] - cc[None, :], -15, 15) + 15
    return np.ascontiguousarray(na_rpb[:, :, :, dc]).astype(np.float32)


class Ctx:
    pass


def build(cfg=None):
    cfg = cfg or {}
    layers = cfg.get("layers", list(range(DEPTH)))
    dbg = cfg.get("debug", ())
    nc = bass.Bass("TRN2", target_bir_lowering=False)
    g = Ctx()
    g.nc = nc
    g.cfg = cfg

    def din(name, shape, dt=F32):
        return nc.dram_tensor(name, list(shape), dt, kind="ExternalInput").ap()

    def dscr(name, shape, dt=F32):
        kind = "ExternalOutput" if name in dbg else "Internal"
        return nc.dram_tensor(name, list(shape), dt, kind=kind).ap()

    hc = host_consts()
    shapes = {"x": [L, D], "ctx": [LC, D], "cvec": [2, D], "w_ada": [4, D, 6 * D], "b_ada": [4, 6 * D],
              "g_norm_mix": [4, D], "g_norm_ffn": [4, D], "w_in_even": [2, D, EVEN_IN], "w_out_even": [2, D, D],
              "rpb_tab": [2, 8, 15, 64, 64], "dn_conv": [2, 3, 1536], "dn_a_log": [2, 8], "dn_dt_bias": [2, 8],
              "dn_g_out": [2, 128], "w_in_odd": [2, D, 6144], "gm_g_v": [2, 3072], "gm_wsT": [2, 128, 8, 128],
              "gm_bsT": [2, 128, 8], "w_out_odd": [2, 3072, D], "w_ff1": [4, D, 4096], "w_ff2": [4, 4096, D],
              "g_final": [D]}
    for k, v in hc.items():
        shapes[k] = list(v.shape)

    class LazyIn(dict):
        def __missing__(self, k):
            self[k] = din(k, shapes[k])
            return self[k]
    I = g.I = LazyIn()
    if not cfg.get("lazy"):
        for k in shapes:
            I[k]
    g.out = nc.dram_tensor("out", [L, D], F32, kind="ExternalOutput").ap()

    S = g.S = {}
    S["X"] = dscr("X", [NTOK, D])
    S["MODV"] = dscr("MODV", [4, 2, 6, D])
    S["QK"] = dscr("QK", [8, 128, NTOK], BF16)
    S["VA"] = dscr("VA", [NTOK, 520], BF16)
    S["DNRAW"] = dscr("DNRAW", [12, 128, NPAD])
    S["GATE"] = dscr("GATE", [NTOK, 512])
    S["BA"] = dscr("BA", [NTOK, 16])
    S["DNP"] = dscr("DNP", [NT, 8, 128, 648])
    S["OF"] = dscr("OF", [NTOK, 512])
    S["ZT"] = dscr("ZT", [8, 128, NTOK], BF16)
    S["MIX"] = dscr("MIX", [NTOK, 3072])

    p = g.p = Prog(nc)
    with ExitStack() as es:
        sems = {n: es.enter_context(nc.semaphore(n)) for n in p.sem_names()}
        NW = 53000
        g.big = es.enter_context(nc.sbuf_tensor("big", [128, NW], F32))
        g.NW = NW
        g.ps = [es.enter_context(nc.psum_tensor("ps%d" % i, [128, 512], F32)) for i in range(8)]
        g.persist = 0
        g.off = 0
        g.DB = {k: Buf(k) for k in list(S.keys()) + ["out"]}
        g._dbt = {}
        g.DBT = lambda name, t: g._dbt.setdefault((name, t), Buf())
        phase_setup(g)
        for l in layers:
            if "nomix" in cfg:
                pass
            elif l % 2 == 0:
                phase_even(g, l)
            else:
                phase_odd(g, l)
            if "noffn" not in cfg:
                phase_ffn(g, l)
        if "nofinal" not in cfg:
            phase_final(g)
        p.add("sp", None, extra=set(p.stream_last.values()) | set(p.last_real.values()))
        p.emit(sems)
    g.n_inst = p.n_inst
    return nc, g


def alloc(g, free_shape, dt=F32, persist=False):
    n = int(np.prod(free_shape))
    words = n if dt == F32 else (n + 1) // 2
    words = (words + 7) // 8 * 8
    a = g.big[:, g.off:g.off + words]
    g.off += words
    assert g.off <= g.NW - getattr(g, "top_reserved", 0), "SBUF overflow %d" % g.off
    if persist:
        g.persist = g.off
    if dt != F32:
        a = a.bitcast(dt)
    a = a[:, 0:n]
    if len(free_shape) == 2:
        a = a.rearrange("p (a b) -> p a b", a=free_shape[0])
    elif len(free_shape) == 3:
        a = a.rearrange("p (a b c) -> p a b c", a=free_shape[0], b=free_shape[1])
    return a


def new_phase(g):
    g.p.barrier()
    g.off = g.persist
    g.top_reserved = 0
    g.PB = [Buf("ps%d" % i, excl=True) for i in range(8)]
    g.bank_i = 0


def bank(g):
    i = g.bank_i % 8
    g.bank_i += 1
    return g.ps[i], g.PB[i]


def phase_setup(g):
    p, I, S = g.p, g.I, g.S
    g.PB = [Buf("ps%d" % i, excl=True) for i in range(8)]
    g.bank_i = 0
    g.identF = alloc(g, [128], F32, persist=True)
    g.identB = alloc(g, [128], BF16, persist=True)
    g.onesF = alloc(g, [128], F32, persist=True)
    g.Bconst = Buf("const")
    p.dma("sp", g.identF, I["ident"][:, :], W=[g.Bconst])
    p.dma("sp", g.onesF, I["ones"][:, :], W=[g.Bconst])
    p.copy("dve", g.identB, g.identF, [g.Bconst], [g.Bconst])
    p.dma("sp", S["X"][0:LC, :], I["ctx"][:, :], W=[g.DB["X"]])
    p.dma("sp", S["X"][LC:NTOK, :], I["x"][:, :], W=[g.DB["X"]])
    z = alloc(g, [12, 4], F32)
    Bz = Buf()
    p.memset("dve", z, 0.0, [Bz])
    for col in (0, 257, 258, NPAD - 1):
        p.dma("sp", S["DNRAW"][:, :, col:col + 1].rearrange("c p t -> p c t"), z[:, :, 0:1], R=[Bz], W=[],
              allow_slow_non_contiguous=True)
    if g.cfg.get("ntiles"):
        zz = alloc(g, [12, 512], F32)
        Bzz = Buf()
        p.memset("dve", zz, 0.0, [Bzz])
        for c0 in range(0, NPAD, 512):
            c1 = min(NPAD, c0 + 512)
            p.dma("sp", S["DNRAW"][:, :, c0:c1].rearrange("c p t -> p c t"), zz[:, :, 0:c1 - c0], R=[Bzz], W=[])
        zb = alloc(g, [8, 520], BF16)
        p.memset("dve", zb, 0.0, [Bzz])
        for c0 in range(0, NTOK, 512):
            c1 = min(NTOK, c0 + 512)
            p.dma("sp", S["QK"][:, :, c0:c1].rearrange("c p t -> p c t"), zb[:, :, 0:c1 - c0], R=[Bzz], W=[])
        for t_ in range(NT):
            p.dma("sp", S["VA"][t_ * 128:(t_ + 1) * 128, :], zb[:, 0, :], R=[Bzz], W=[])
    cf = alloc(g, [8, 2], F32)
    Bcf = Buf()
    for s_ in range(2):
        p.dma("sp", cf[:, :, s_], I["cvec"][s_, :].rearrange("(k p) -> p k", p=128), W=[Bcf], allow_slow_non_contiguous=True)
    p.act(cf, cf, AF.Silu, [Bcf], [Bcf])
    wb = [alloc(g, [8, 512], F32) for _ in range(3)]
    Bw = [Buf() for _ in range(3)]
    mrow = alloc(g, [6 * D], F32)
    brow = alloc(g, [6 * D], F32)
    grow = alloc(g, [2, D], F32)
    tmp = alloc(g, [D], F32)
    Bm, Bb, Bg, Bt = Buf(), Buf(), Buf(), Buf()
    it = 0
    for l in ([] if g.cfg.get("nomod") else g.cfg.get("layers", list(range(DEPTH)))):
        p.dma("sp", brow[0:2, :], I["b_ada"][l, :].partition_broadcast(2), W=[Bb])
        p.dma("sp", grow[0:2, 0, :], I["g_norm_mix"][l, :].partition_broadcast(2), W=[Bg])
        p.dma("sp", grow[0:2, 1, :], I["g_norm_ffn"][l, :].partition_broadcast(2), W=[Bg])
        for n in range(12):
            w_, bw_ = wb[it % 3], Bw[it % 3]
            it += 1
            p.dma("sp", w_, I["w_ada"][l, :, n * 512:(n + 1) * 512].rearrange("(k p) n -> p k n", p=128), W=[bw_])
            ps, pb = bank(g)
            for k in range(8):
                p.mm(ps[0:2, :], cf[:, k, :], w_[:, k, :], k == 0, k == 7, [Bcf, bw_], [pb])
            p.tt("dve", mrow[0:2, n * 512:(n + 1) * 512], ps[0:2, :], brow[0:2, n * 512:(n + 1) * 512], ALU.add, [pb, Bb], [Bm])
        for j, (sc_i, sh_i, gt_i) in enumerate([(1, 0, 2), (4, 3, 5)]):
            p.stt(tmp[0:2, :], mrow[0:2, sc_i * D:(sc_i + 1) * D], 1.0, grow[0:2, j, :], ALU.add, ALU.mult, [Bm, Bg], [Bt])
            p.dma("sp", S["MODV"][l, :, 3 * j + 0, :], tmp[0:2, :], R=[Bt], W=[])
            p.dma("sp", S["MODV"][l, :, 3 * j + 1, :], mrow[0:2, sh_i * D:(sh_i + 1) * D], R=[Bm], W=[])
            p.dma("sp", S["MODV"][l, :, 3 * j + 2, :], mrow[0:2, gt_i * D:(gt_i + 1) * D], R=[Bm], W=[])


def load_mod(g, l, stream, which):
    p, S = g.p, g.S
    A = alloc(g, [D], F32)
    sh = alloc(g, [D], F32)
    B = Buf()
    p.dma("sp", A, S["MODV"][l, stream, 3 * which + 0, :].partition_broadcast(128), R=[], W=[B])
    p.dma("sp", sh, S["MODV"][l, stream, 3 * which + 1, :].partition_broadcast(128), R=[], W=[B])
    return A, sh, B


def load_gate(g, l, stream, which):
    p, S = g.p, g.S
    gt = alloc(g, [D], F32)
    B = Buf()
    p.dma("sp", gt, S["MODV"][l, stream, 3 * which + 2, :].partition_broadcast(128), R=[], W=[B])
    return gt, B


class NormBufs:
    def __init__(self, g, nbuf=2, nx=2):
        self.n = nbuf
        self.nx = nx
        self.x = [alloc(g, [D], F32) for _ in range(nx)]
        self.Bx = [Buf() for _ in range(nx)]
        self.tm = [alloc(g, [D], F32) for _ in range(nbuf)]
        self.Btm = [Buf() for _ in range(nbuf)]
        self.xn = [alloc(g, [D], BF16) for _ in range(nbuf)]
        self.Bxn = [Buf() for _ in range(nbuf)]
        self.hT = [alloc(g, [8, 128], BF16) for _ in range(nbuf)]
        self.BhT = [Buf() for _ in range(nbuf)]
        self.st = [alloc(g, [4], F32) for _ in range(nbuf)]
        self.Bst = [Buf() for _ in range(nbuf)]


def norm_a(g, nb, i, t, mods, ix=None):
    p, S = g.p, g.S
    ix = i if ix is None else ix
    x, Bx, xn, Bxn, st, Bst, tm, Btm = (nb.x[ix], nb.Bx[ix], nb.xn[i], nb.Bxn[i], nb.st[i], nb.Bst[i], nb.tm[i], nb.Btm[i])
    A, sh, Bmod = mods
    p.dma("sp", x, S["X"][t * 128:(t + 1) * 128, :], R=[g.DBT("X", t)], W=[Bx])
    p.act(tm, x, AF.Square, [Bx], [Btm, Bst], accum_out=st[:, 0:1])
    p.act(st[:, 1:2], st[:, 0:1], AF.Sqrt, [Bst], [Bst], scale=1.0 / D, bias=EPS)
    p.recip(st[:, 2:3], st[:, 1:2], [Bst], [Bst])
    p.stt(tm, x, st[:, 2:3], A, ALU.mult, ALU.mult, [Bx, Bst, Bmod, Btm], [Btm])
    p.tt("pool", xn, tm, sh, ALU.add, [Btm, Bmod], [Bxn])


def norm_b(g, nb, i):
    p = g.p
    xn, Bxn, hT, BhT = nb.xn[i], nb.Bxn[i], nb.hT[i], nb.BhT[i]
    ps, pb = bank(g)
    psb = ps[:, :].bitcast(BF16)
    for k in range(8):
        p.tr(psb[:, k * 128:(k + 1) * 128], xn[:, k * 128:(k + 1) * 128], g.identB, [Bxn, g.Bconst], [pb])
    p.copy("act", hT, psb.rearrange("p (a b) -> p a b", a=8), [pb], [BhT])
    return hT, BhT


def pipelined(tiles, stage0a, stage0b, stage1):
    n = len(tiles)
    if n == 0:
        return
    stage0a(0, tiles[0])
    if n > 1:
        stage0a(1, tiles[1])
    nxt = stage0b(0, tiles[0])
    for it, t in enumerate(tiles):
        cur = nxt
        if it + 2 < n:
            stage0a(it + 2, tiles[it + 2])
        if it + 1 < n:
            nxt = stage0b(it + 1, tiles[it + 1])
        stage1(it, t, cur)


def load_w_bf16(g, dst, src_ap, Bw, max_cols=2048):
    p = g.p
    K, N = dst.shape[1], dst.shape[2]
    step = max_cols
    for k in range(K):
        for n0 in range(0, N, step):
            n1 = min(N, n0 + step)
            p.dma("pool", dst[:, k, n0:n1], src_ap[:, k, n0:n1], W=[Bw], stream="wl")


def ffn_w_top(g):
    NWt = g.NW
    W1 = g.big[:, NWt - 32768:NWt - 16384].bitcast(BF16).rearrange("p (a b) -> p a b", a=8)
    W2 = g.big[:, NWt - 16384:NWt].bitcast(BF16).rearrange("p (a b) -> p a b", a=32)
    return W1, W2


def prefetch_ffn_w(g, l):
    W1, W2 = ffn_w_top(g)
    BW1, BW2 = Buf(), Buf()
    load_w_bf16(g, W1, g.I["w_ff1"][l].rearrange("(k p) n -> p k n", p=128), BW1)
    load_w_bf16(g, W2, g.I["w_ff2"][l].rearrange("(k p) n -> p k n", p=128), BW2)
    g.ffn_pref = (l, W1, W2, BW1, BW2)


def phase_ffn(g, l):
    p, I, S = g.p, g.I, g.S
    new_phase(g)
    tiles = tile_list(g, l, "ffn")
    pref = getattr(g, "ffn_pref", None)
    if pref is not None and pref[0] == l:
        _, W1, W2, BW1, BW2 = pref
    else:
        W1, W2 = ffn_w_top(g)
        BW1, BW2 = Buf(), Buf()
        load_w_bf16(g, W1, I["w_ff1"][l].rearrange("(k p) n -> p k n", p=128), BW1)
        load_w_bf16(g, W2, I["w_ff2"][l].rearrange("(k p) n -> p k n", p=128), BW2)
    g.top_reserved = 32768
    mods = [load_mod(g, l, s_, 1) for s_ in range(2)]
    gates = [load_gate(g, l, s_, 1) for s_ in range(2)]
    nb = NormBufs(g, nx=3)
    r_ = [alloc(g, [512], F32) for _ in range(2)]
    Br = [Buf() for _ in range(2)]
    aT = [alloc(g, [32, 128], BF16) for _ in range(2)]
    BaT = [Buf() for _ in range(2)]
    tmp = [alloc(g, [512], F32) for _ in range(2)]
    Btmp = [Buf() for _ in range(2)]
    ri = [0]

    def stage0a(it, t):
        norm_a(g, nb, it % 2, t, mods[1 if t < 2 else 0], ix=it % 3)

    def stage0b(it, t):
        return norm_b(g, nb, it % 2)

    def stage1(it, t, cur):
        i = it % 2
        s_ = 1 if t < 2 else 0
        hT, BhT = cur
        for mb in range(8):
            ps, pb = bank(g)
            for mm_ in range(4):
                m = mb * 4 + mm_
                for k in range(8):
                    p.mm(ps[:, mm_ * 128:(mm_ + 1) * 128], W1[:, k, m * 128:(m + 1) * 128], hT[:, k, :], k == 0, k == 7, [BW1, BhT], [pb])
            rr, brr = r_[ri[0] % 2], Br[ri[0] % 2]
            ri[0] += 1
            p.act(rr, ps[:, :], AF.Relu, [pb], [brr])
            eng = "dve"
            p.tt(eng, aT[i][:, mb * 4:(mb + 1) * 4, :], rr.rearrange("p (a b) -> p a b", a=4), rr.rearrange("p (a b) -> p a b", a=4), ALU.mult, [brr], [BaT[i]])
        gt, Bgt = gates[s_]
        for n in range(2):
            ps, pb = bank(g)
            for k in range(32):
                p.mm(ps[:, :], aT[i][:, k, :], W2[:, k, n * 512:(n + 1) * 512], k == 0, k == 31, [BaT[i], BW2], [pb])
            p.tt("dve", tmp[n], ps[:, :], gt[:, n * 512:(n + 1) * 512], ALU.mult, [pb, Bgt], [Btmp[n]])
            p.tt("pool", nb.x[it % 3][:, n * 512:(n + 1) * 512], tmp[n], nb.x[it % 3][:, n * 512:(n + 1) * 512], ALU.add, [Btmp[n], nb.Bx[it % 3]], [nb.Bx[it % 3]])
        p.dma("pool", S["X"][t * 128:(t + 1) * 128, :], nb.x[it % 3], R=[nb.Bx[it % 3]], W=[g.DBT("X", t)])

    pipelined(tiles, stage0a, stage0b, stage1)


def tile_list(g, l, kind):
    lim = g.cfg.get("ntiles")
    lat = list(range(2, NT))
    if lim:
        lat = lat[:lim]
    if kind in ("ffn", "mixout"):
        ctx = [0, 1] if l < 2 else []
    elif kind == "proj":
        ctx = [0, 1] if l < 3 else []
    else:
        ctx = []
    return ctx + lat


def phase_final(g):
    p, I, S = g.p, g.I, g.S
    new_phase(g)
    gf = alloc(g, [D], F32)
    Bg = Buf()
    p.dma("sp", gf, I["g_final"].partition_broadcast(128), W=[Bg])
    x = [alloc(g, [D], F32) for _ in range(2)]
    Bx = [Buf() for _ in range(2)]
    y = [alloc(g, [D], F32) for _ in range(2)]
    By = [Buf() for _ in range(2)]
    st = [alloc(g, [4], F32) for _ in range(2)]
    Bst = [Buf() for _ in range(2)]
    junk = alloc(g, [D], BF16)
    Bj = Buf()
    for it, t in enumerate(tile_list(g, 3, "lat")):
        i = it % 2
        p.dma("sp", x[i], S["X"][t * 128:(t + 1) * 128, :], R=[g.DBT("X", t)], W=[Bx[i]])
        p.act(junk, x[i], AF.Square, [Bx[i]], [Bj, Bst[i]], accum_out=st[i][:, 0:1])
        p.act(st[i][:, 1:2], st[i][:, 0:1], AF.Sqrt, [Bst[i]], [Bst[i]], scale=1.0 / D, bias=EPS)
        p.recip(st[i][:, 2:3], st[i][:, 1:2], [Bst[i]], [Bst[i]])
        p.stt(y[i], x[i], st[i][:, 2:3], gf, ALU.mult, ALU.mult, [Bx[i], Bst[i], Bg], [By[i]])
        p.dma("pool", g.out[(t - 2) * 128:(t - 1) * 128, :], y[i], R=[By[i]], W=[])


def phase_even(g, l):
    e = l // 2
    ctx_out = (l == 0)
    st = g.cfg.get("even_stages", "ABCND")
    if "A" in st:
        even_proj(g, l, e)
    if "B" in st:
        even_dnprep(g, l, e)
    if "C" in st:
        even_dnscan(g, l, e, ctx_out)
    if "N" in st:
        even_na(g, l, e, ctx_out)
    if "D" in st:
        even_out(g, l, e)


def dn_col0(t):
    return 1 + t * 128 if t < 2 else 259 + (t - 2) * 128


def even_proj(g, l, e):
    p, I, S = g.p, g.I, g.S
    new_phase(g)
    tiles = tile_list(g, l, "proj")
    W = alloc(g, [8, EVEN_IN], BF16)
    BW = Buf()
    load_w_bf16(g, W, I["w_in_even"][e].rearrange("(k p) n -> p k n", p=128), BW, max_cols=1800)
    mods = [load_mod(g, l, s_, 0) for s_ in range(2)]
    nb = NormBufs(g, nx=3)
    qk_sb = [alloc(g, [8, 128], BF16) for _ in range(2)]
    va_sb = [alloc(g, [8, 65], BF16) for _ in range(2)]
    dn_sb = [alloc(g, [12, 128], F32) for _ in range(2)]
    gt_sb = [alloc(g, [512], F32) for _ in range(2)]
    ba_sb = [alloc(g, [16], F32) for _ in range(2)]
    Bqk, Bva, Bdn, Bgt, Bba = [[Buf() for _ in range(2)] for _ in range(5)]
    for i in range(2):
        p.memset("pool", va_sb[i][:, :, 64:65], 1.0, [Bva[i]])
    def stage0a(it, t):
        norm_a(g, nb, it % 2, t, mods[1 if t < 2 else 0], ix=it % 3)

    def stage0b(it, t):
        return norm_b(g, nb, it % 2)

    def stage1(it, t, cur):
        i = it % 2
        s_ = 1 if t < 2 else 0
        hT, BhT = cur
        def fm_bank(col0, nch):
            ps, pb = bank(g)
            for cc in range(nch):
                for k in range(8):
                    p.mm(ps[:, cc * 128:(cc + 1) * 128], W[:, k, col0 + cc * 128: col0 + (cc + 1) * 128], hT[:, k, :], k == 0, k == 7, [BW, BhT], [pb])
            return ps, pb
        ps, pb = fm_bank(0, 4)
        p.act(qk_sb[i][:, 0:4, :], ps[:, :].rearrange("p (a b) -> p a b", a=4), AF.Copy, [pb], [Bqk[i]], scale=0.125)
        ps, pb = fm_bank(512, 4)
        p.copy("dve", qk_sb[i][:, 4:8, :], ps[:, :].rearrange("p (a b) -> p a b", a=4), [pb], [Bqk[i]])
        p.dma("pool", S["QK"][:, :, t * 128:(t + 1) * 128].rearrange("c p t -> p c t"), qk_sb[i], R=[Bqk[i]], W=[])
        ps, pb = bank(g)
        for k in range(8):
            p.mm(ps[:, :], hT[:, k, :], W[:, k, 1024:1536], k == 0, k == 7, [BhT, BW], [pb])
        p.copy("dve", va_sb[i][:, :, 0:64], ps[:, :].rearrange("p (a b) -> p a b", a=8), [pb], [Bva[i]])
        p.dma("pool", S["VA"][t * 128:(t + 1) * 128, :], va_sb[i].rearrange("p a b -> p (a b)"), R=[Bva[i]], W=[])
        for q in range(3):
            ps, pb = fm_bank(1536 + q * 512, 4)
            dst = dn_sb[i][:, q * 4:(q + 1) * 4, :]
            if q == 1:
                p.copy("dve", dst, ps[:, :].rearrange("p (a b) -> p a b", a=4), [pb], [Bdn[i]])
            else:
                p.copy("act", dst, ps[:, :].rearrange("p (a b) -> p a b", a=4), [pb], [Bdn[i]])
        c0 = dn_col0(t)
        p.dma("pool", S["DNRAW"][:, :, c0:c0 + 128].rearrange("c p t -> p c t"), dn_sb[i], R=[Bdn[i]], W=[])
        ps, pb = bank(g)
        for k in range(8):
            p.mm(ps[:, :], hT[:, k, :], W[:, k, 3072:3584], k == 0, k == 7, [BhT, BW], [pb])
        p.copy("act", gt_sb[i], ps[:, :], [pb], [Bgt[i]])
        p.dma("pool", S["GATE"][t * 128:(t + 1) * 128, :], gt_sb[i], R=[Bgt[i]], W=[])
        ps, pb = bank(g)
        for k in range(8):
            p.mm(ps[:, 0:16], hT[:, k, :], W[:, k, 3584:3600], k == 0, k == 7, [BhT, BW], [pb])
        p.copy("dve", ba_sb[i], ps[:, 0:16], [pb], [Bba[i]])
        p.dma("pool", S["BA"][t * 128:(t + 1) * 128, :], ba_sb[i], R=[Bba[i]], W=[])

    pipelined(tiles, stage0a, stage0b, stage1)


def even_dnprep(g, l, e):
    p, I, S = g.p, g.I, g.S
    new_phase(g)
    tiles = tile_list(g, l, "proj")
    Bc = Buf()
    cw = alloc(g, [3, 12], F32)
    cwr = alloc(g, [128], F32)
    Bcw = Buf()
    p.dma("sp", cwr[0:36, :], I["dn_conv"][e].rearrange("j (c p) -> (j c) p", p=128), W=[Bcw])
    ps, pb = bank(g)
    p.tr(ps[:, 0:36], cwr[0:36, :], g.identF[0:36, 0:36], [Bcw, g.Bconst], [pb])
    p.copy("dve", cw.rearrange("p a b -> p (a b)"), ps[:, 0:36], [pb], [Bc])
    tri = alloc(g, [4, 128], F32)
    p.dma("sp", tri, I["tri"].rearrange("m a b -> a m b"), W=[Bc])
    LE, GE, GT, LT = [tri[:, m_, :] for m_ in range(4)]
    rotm = alloc(g, [128], F32)
    p.dma("sp", rotm, I["rotm"][:, :], W=[Bc])
    dtb = alloc(g, [8], F32)
    nexpA = alloc(g, [8], F32)
    p.dma("sp", dtb, I["dn_dt_bias"][e, :].partition_broadcast(128), W=[Bc])
    p.dma("sp", nexpA, I["dn_a_log"][e, :].partition_broadcast(128), W=[Bc])
    p.act(nexpA, nexpA, AF.Exp, [Bc], [Bc])
    p.ts("dve", nexpA, nexpA, -1.0, None, ALU.mult, None, [Bc], [Bc])
    raw = [alloc(g, [12, 130], F32) for _ in range(2)]
    ba = [alloc(g, [16], F32) for _ in range(2)]
    cs = [alloc(g, [2, 128], F32) for _ in range(2)]
    Braw, Bba, Bcs = [[Buf() for _ in range(2)] for _ in range(3)]
    pk = [alloc(g, [8, 648], F32) for _ in range(2)]
    Bpk = [Buf() for _ in range(2)]
    cv = alloc(g, [12, 128], F32)
    tmpc = alloc(g, [128], F32)
    sqb = alloc(g, [1024], F32)
    rn = alloc(g, [1024], F32)
    qkr = alloc(g, [8, 128], F32)
    t1 = alloc(g, [8, 128], F32)
    ktok = alloc(g, [4, 128], F32)
    vtok = alloc(g, [4, 128], F32)
    sm = alloc(g, [12, 8], F32)
    lrep = alloc(g, [8, 128], F32)
    egB = alloc(g, [8, 128], F32)
    mmx = alloc(g, [8, 128], F32)
    dec = alloc(g, [8, 128], F32)
    decI = alloc(g, [8, 128], F32)
    decS = alloc(g, [8, 128], F32)
    aqk = alloc(g, [8, 128], F32)
    bv = alloc(g, [8, 128], F32)
    bek = alloc(g, [8, 128], F32)
    Pb = [alloc(g, [8, 128], F32) for _ in range(2)]
    Qb = [alloc(g, [8, 128], F32) for _ in range(2)]
    Nb = [alloc(g, [8, 128], F32) for _ in range(2)]
    Bcv, Btc, Bsq, Brn, Bqkr, Bt1, Bkt, Bvt, Bsm, Blr, BeB, Bmx, Bdec, BdI, BdS, Baqk, Bbv, Bbek = [Buf() for _ in range(18)]
    BP = [Buf() for _ in range(2)]
    BQ = [Buf() for _ in range(2)]
    BN = [Buf() for _ in range(2)]
    v4 = lambda ps: ps[:, :].rearrange("p (a b) -> p a b", a=4)
    for it, t in enumerate(tiles):
        i = it % 2
        lat = t >= 2
        c0 = dn_col0(t) - 1
        p.dma("sp", raw[i], S["DNRAW"][:, :, c0:c0 + 130].rearrange("c p t -> p c t"), R=[], W=[Braw[i]])
        p.dma("sp", ba[i], S["BA"][t * 128:(t + 1) * 128, :], R=[], W=[Bba[i]])
        if lat:
            p.dma("sp", cs[i][:, 0, :], I["cosT"][:, (t - 2) * 128:(t - 1) * 128], W=[Bcs[i]])
            p.dma("sp", cs[i][:, 1, :], I["sinT"][:, (t - 2) * 128:(t - 1) * 128], W=[Bcs[i]])
        for c in range(12):
            p.ts("dve", cv[:, c, :], raw[i][:, c, 0:128], cw[:, 0, c:c + 1], None, ALU.mult, None, [Braw[i], Bc], [Bcv])
            p.stt(cv[:, c, :], raw[i][:, c, 1:129], cw[:, 1, c:c + 1], cv[:, c, :], ALU.mult, ALU.add, [Braw[i], Bc, Bcv], [Bcv])
            p.stt(cv[:, c, :], raw[i][:, c, 2:130], cw[:, 2, c:c + 1], cv[:, c, :], ALU.mult, ALU.add, [Braw[i], Bc, Bcv], [Bcv])
        p.act(cv, cv, AF.Silu, [Bcv], [Bcv])
        if g.cfg.get("cutB", 99) <= 1:
            continue
        cvf = cv.rearrange("p a b -> p (a b)")
        p.act(sqb, cvf[:, 0:1024], AF.Square, [Bcv], [Bsq])
        for n in range(2):
            ps, pb = bank(g)
            p.mm(ps[:, :], g.onesF, sqb[:, n * 512:(n + 1) * 512], True, True, [g.Bconst, Bsq], [pb])
            if n == 0:
                p.act(rn[:, 0:512], ps[:, :], AF.Sqrt, [pb], [Brn], scale=128.0, bias=EPS * 128.0)
            else:
                p.act(rn[:, 512:1024], ps[:, :], AF.Sqrt, [pb], [Brn], scale=1.0, bias=EPS)
        p.recip(rn, rn, [Brn], [Brn])
        qk0 = cv[:, 0:8, :]
        p.tt("pool", qk0, qk0, rn.rearrange("p (a b) -> p a b", a=8), ALU.mult, [Bcv, Brn], [Bcv])
        if g.cfg.get("cutB", 99) <= 2:
            continue
        if lat:
            for n in range(2):
                ps, pb = bank(g)
                p.mm(ps[:, :], rotm, cvf[:, n * 512:(n + 1) * 512], True, True, [Bc, Bcv], [pb])
                p.tt("dve", t1[:, n * 4:(n + 1) * 4, :], v4(ps), cs[i][:, 1:2, :].broadcast_to([128, 4, 128]), ALU.mult, [pb, Bcs[i]], [Bt1])
            p.tt("pool", qkr, qk0, cs[i][:, 0:1, :].broadcast_to([128, 8, 128]), ALU.mult, [Bcv, Bcs[i]], [Bqkr])
            p.tt("pool", qkr, qkr, t1, ALU.add, [Bqkr, Bt1], [Bqkr])
            QK_, BQK_ = qkr, Bqkr
        else:
            QK_, BQK_ = qk0, Bcv
        qT = lambda h: QK_[:, h, :]
        kT = lambda h: QK_[:, 4 + h, :]
        if g.cfg.get("cutB", 99) <= 3:
            continue
        ps, pb = bank(g)
        for h in range(4):
            p.tr(ps[:, h * 128:(h + 1) * 128], kT(h), g.identF, [BQK_, g.Bconst], [pb])
        p.copy("act", ktok, v4(ps), [pb], [Bkt])
        ps, pb = bank(g)
        for h in range(4):
            p.tr(ps[:, h * 128:(h + 1) * 128], cv[:, 8 + h, :], g.identF, [Bcv, g.Bconst], [pb])
        p.copy("dve", vtok, v4(ps), [pb], [Bvt])
        if g.cfg.get("cutB", 99) <= 4:
            continue
        beta, nbeta, z, logg, gam, eg, be, glg, kds, glv = [sm[:, r_, :] for r_ in range(10)]
        p.act(beta, ba[i][:, 0:8], AF.Sigmoid, [Bba[i]], [Bsm])
        p.ts("dve", nbeta, beta, -1.0, None, ALU.mult, None, [Bsm], [Bsm])
        p.tt("pool", z, ba[i][:, 8:16], dtb, ALU.add, [Bba[i], Bc], [Bsm])
        p.act(z, z, AF.Exp, [Bsm], [Bsm])
        p.act(z, z, AF.Ln, [Bsm], [Bsm], bias=1.0)
        p.tt("pool", logg, z, nexpA, ALU.mult, [Bsm, Bc], [Bsm])
        ps, pb = bank(g)
        p.mm(ps[:, 0:4], LE, logg[:, 0:4], True, True, [Bc, Bsm], [pb])
        p.mm(ps[:, 4:8], GE, logg[:, 4:8], True, True, [Bc, Bsm], [pb])
        p.copy("dve", gam, ps[:, 0:8], [pb], [Bsm])
        if g.cfg.get("cutB", 99) <= 4.1:
            continue
        p.copy("pool", lrep, logg.unsqueeze(2).broadcast_to([128, 8, 128]), [Bsm], [Blr])
        gps = []
        for d_ in range(2):
            ps, pb = bank(g)
            for h in range(4):
                p.mm(ps[:, h * 128:(h + 1) * 128], lrep[:, d_ * 4 + h, :], LE if d_ == 0 else GE, True, True, [Blr, Bc], [pb])
            gps.append((ps, pb))
        if g.cfg.get("cutB", 99) <= 4.2:
            continue
        p.act(eg, gam, AF.Exp, [Bsm], [Bsm])
        p.tt("pool", be, beta, eg, ALU.mult, [Bsm], [Bsm])
        for d_ in range(2):
            ps, pb = gps[d_]
            last = 127 if d_ == 0 else 0
            p.act(egB[:, d_ * 4:(d_ + 1) * 4, :], v4(ps), AF.Exp, [pb], [BeB])
            p.copy("dve", glg[:, d_ * 4:(d_ + 1) * 4], v4(ps)[:, :, last], [pb], [Bsm])
            for h in range(4):
                dh = d_ * 4 + h
                p.ts("dve", mmx[:, dh, :], ps[:, h * 128:(h + 1) * 128], gam[:, dh:dh + 1], 0.0, ALU.subtract, ALU.max, [pb, Bsm], [Bmx])
        if g.cfg.get("cutB", 99) <= 4.3:
            continue
        p.tt("pool", kds, glg, gam, ALU.subtract, [Bsm], [Bsm])
        p.act(kds, kds, AF.Exp, [Bsm], [Bsm])
        p.act(glv, glg, AF.Exp, [Bsm], [Bsm])
        if g.cfg.get("cutB", 99) <= 4.4:
            continue
        p.act(dec, mmx, AF.Exp, [Bmx], [Bdec], scale=-1.0)
        for d_ in range(2):
            sl_ = slice(d_ * 4, (d_ + 1) * 4)
            p.tt("pool", decI[:, sl_, :], dec[:, sl_, :], (GE if d_ == 0 else LE).unsqueeze(1).broadcast_to([128, 4, 128]), ALU.mult, [Bdec, Bc], [BdI])
            p.tt("pool", decS[:, sl_, :], dec[:, sl_, :], (GT if d_ == 0 else LT).unsqueeze(1).broadcast_to([128, 4, 128]), ALU.mult, [Bdec, Bc], [BdS])
        if g.cfg.get("cutB", 99) <= 5:
            continue
        ps_kk, pb_kk = bank(g)
        for h in range(4):
            p.mm(ps_kk[:, h * 128:(h + 1) * 128], kT(h), kT(h), True, True, [BQK_], [pb_kk])
        ps_qk, pb_qk = bank(g)
        for h in range(4):
            p.mm(ps_qk[:, h * 128:(h + 1) * 128], qT(h), kT(h), True, True, [BQK_], [pb_qk])
        Q, P_, N_ = Qb[0], Pb[0], Nb[0]
        for d_ in range(2):
            for h in range(4):
                dh = d_ * 4 + h
                p.stt(Q[:, dh, :], ps_kk[:, h * 128:(h + 1) * 128], nbeta[:, dh:dh + 1], decS[:, dh, :], ALU.mult, ALU.mult, [pb_kk, Bsm, BdS], [BQ[0]])
            p.tt("dve", aqk[:, d_ * 4:(d_ + 1) * 4, :], v4(ps_qk), decI[:, d_ * 4:(d_ + 1) * 4, :], ALU.mult, [pb_qk, BdI], [Baqk])
        if g.cfg.get("cutB", 99) <= 6:
            continue
        for d_ in range(2):
            ps, pb = bank(g)
            for h in range(4):
                p.tr(ps[:, h * 128:(h + 1) * 128], Q[:, d_ * 4 + h, :], g.identF, [BQ[0], g.Bconst], [pb])
            p.copy("act", P_[:, d_ * 4:(d_ + 1) * 4, :], v4(ps), [pb], [BP[0]])
            ps, pb = bank(g)
            for h in range(4):
                p.tr(ps[:, h * 128:(h + 1) * 128], aqk[:, d_ * 4 + h, :], g.identF, [Baqk, g.Bconst], [pb])
            p.copy("dve", pk[i][:, d_ * 4:(d_ + 1) * 4, 256:384], v4(ps), [pb], [Bpk[i]])
        if g.cfg.get("cutB", 99) <= 7:
            continue
        p.tt("pool", N_, P_, g.identF.unsqueeze(1).broadcast_to([128, 8, 128]), ALU.add, [BP[0], g.Bconst], [BN[0]])
        cur = 0
        for lev in range(6):
            nxt = 1 - cur
            lastlev = (lev == 5)
            for d_ in range(2):
                sl_ = slice(d_ * 4, (d_ + 1) * 4)
                if not lastlev:
                    ps, pb = bank(g)
                    for h in range(4):
                        dh = d_ * 4 + h
                        p.mm(ps[:, h * 128:(h + 1) * 128], Qb[cur][:, dh, :], Pb[cur][:, dh, :], True, True, [BQ[cur], BP[cur]], [pb])
                    p.copy("act", Pb[nxt][:, sl_, :], v4(ps), [pb], [BP[nxt]])
                ps, pb = bank(g)
                for h in range(4):
                    dh = d_ * 4 + h
                    p.mm(ps[:, h * 128:(h + 1) * 128], Pb[cur][:, dh, :], Qb[cur][:, dh, :], True, True, [BQ[cur], BP[cur]], [pb])
                p.copy("act" if lastlev else "dve", Qb[nxt][:, sl_, :], v4(ps), [pb], [BQ[nxt]])
            for d_ in range(2):
                sl_ = slice(d_ * 4, (d_ + 1) * 4)
                ps, pb = bank(g)
                for h in range(4):
                    dh = d_ * 4 + h
                    p.mm(ps[:, h * 128:(h + 1) * 128], Qb[nxt][:, dh, :], Nb[cur][:, dh, :], True, True, [BQ[nxt], BN[cur]], [pb])
                p.tt("dve", Nb[nxt][:, sl_, :], v4(ps), Nb[cur][:, sl_, :], ALU.add, [pb, BN[cur]], [BN[nxt]])
            cur = nxt
        TT, BTT = Nb[cur], BN[cur]
        if g.cfg.get("cutB", 99) <= 8:
            continue
        for d_ in range(2):
            for h in range(4):
                dh = d_ * 4 + h
                p.act(bv[:, dh, :], vtok[:, h, :], AF.Copy, [Bvt, Bsm], [Bbv], scale=beta[:, dh:dh + 1])
                p.act(bek[:, dh, :], ktok[:, h, :], AF.Copy, [Bkt, Bsm], [Bbek], scale=be[:, dh:dh + 1])
                p.ts("dve", pk[i][:, dh, 512:640], ktok[:, h, :], kds[:, dh:dh + 1], None, ALU.mult, None, [Bkt, Bsm], [Bpk[i]])
            p.tt("pool", pk[i][:, d_ * 4:(d_ + 1) * 4, 384:512], QK_[:, 0:4, :], egB[:, d_ * 4:(d_ + 1) * 4, :], ALU.mult, [BQK_, BeB], [Bpk[i]])
        p.copy("pool", pk[i][:, :, 640:641], glv.unsqueeze(2), [Bsm], [Bpk[i]])
        for d_ in range(2):
            ps, pb = bank(g)
            for h in range(4):
                dh = d_ * 4 + h
                p.mm(ps[:, h * 128:(h + 1) * 128], TT[:, dh, :], bv[:, dh, :], True, True, [BTT, Bbv], [pb])
            p.copy("act", pk[i][:, d_ * 4:(d_ + 1) * 4, 0:128], v4(ps), [pb], [Bpk[i]])
            ps, pb = bank(g)
            for h in range(4):
                dh = d_ * 4 + h
                p.mm(ps[:, h * 128:(h + 1) * 128], bek[:, dh, :], TT[:, dh, :], True, True, [BTT, Bbek], [pb])
            p.copy("dve", pk[i][:, d_ * 4:(d_ + 1) * 4, 128:256], v4(ps), [pb], [Bpk[i]])
        if g.cfg.get("cutB", 99) <= 9:
            continue
        p.dma("pool", S["DNP"][t].rearrange("d p c -> p d c"), pk[i], R=[Bpk[i]], W=[])


def even_dnscan(g, l, e, ctx_out):
    p, I, S = g.p, g.I, g.S
    new_phase(g)
    lat_tiles = [t for t in tile_list(g, l, "proj") if t >= 2]
    Bc = Buf()
    goutb = alloc(g, [128], F32)
    p.dma("sp", goutb, I["dn_g_out"][e, :].partition_broadcast(128), W=[Bc])
    Sst = alloc(g, [8, 128], F32)
    BS = [Buf() for _ in range(8)]
    for dh in range(8):
        p.memset("pool", Sst[:, dh, :], 0.0, [BS[dh]])
    NPK = 6
    pk = [alloc(g, [648], F32) for _ in range(NPK)]
    Bpk = [Buf() for _ in range(NPK)]
    u_sb = [alloc(g, [128], F32) for _ in range(4)]
    Bu = [Buf() for _ in range(4)]
    o_sb = [alloc(g, [4, 128], F32) for _ in range(2)]
    Bo = [Buf() for _ in range(2)]
    of_sb = [alloc(g, [4, 128], F32) for _ in range(2)]
    Bof = [Buf() for _ in range(2)]
    gate = [alloc(g, [512], F32) for _ in range(2)]
    Bgate = [Buf() for _ in range(2)]
    y1 = alloc(g, [4, 128], F32)
    ydn = alloc(g, [512], BF16)
    zt = [alloc(g, [4, 128], BF16) for _ in range(2)]
    Bzt = [Buf() for _ in range(2)]
    st = alloc(g, [3, 4], F32)
    junk = alloc(g, [128], BF16)
    By1, Bydn, Bst, Bj = Buf(), Buf(), Buf(), Buf()
    ipk = 0
    iu = 0
    for d_ in range(2):
        order = [0, 1] + lat_tiles if d_ == 0 else [1, 0] + lat_tiles[::-1]
        for it, t in enumerate(order):
            i = it % 2
            want_o = (t >= 2) or ctx_out
            if d_ == 1 and want_o:
                p.dma("sp", of_sb[i].rearrange("p a b -> p (a b)"), S["OF"][t * 128:(t + 1) * 128, :], R=[g.DBT("OF", t)], W=[Bof[i]])
                p.dma("sp", gate[i], S["GATE"][t * 128:(t + 1) * 128, :], R=[], W=[Bgate[i]])
            for h in range(4):
                dh = d_ * 4 + h
                pk_, bpk_ = pk[ipk % NPK], Bpk[ipk % NPK]
                ipk += 1
                p.dma("sp", pk_, S["DNP"][t, dh], R=[], W=[bpk_])
                u_, bu_ = u_sb[iu % 4], Bu[iu % 4]
                iu += 1
                ps1, pb1 = bank(g)
                p.mm(ps1[:, 0:128], pk_[:, 128:256], Sst[:, dh, :], True, True, [bpk_, BS[dh]], [pb1])
                p.tt("dve", u_, pk_[:, 0:128], ps1[:, 0:128], ALU.subtract, [bpk_, pb1], [bu_])
                if want_o:
                    ps2, pb2 = bank(g)
                    p.mm(ps2[:, 0:128], pk_[:, 384:512], Sst[:, dh, :], True, False, [bpk_, BS[dh]], [pb2])
                    p.mm(ps2[:, 0:128], pk_[:, 256:384], u_, False, True, [bpk_, bu_], [pb2])
                ps3, pb3 = bank(g)
                p.mm(ps3[:, 0:128], pk_[:, 512:640], u_, True, True, [bpk_, bu_], [pb3])
                p.stt(Sst[:, dh, :], Sst[:, dh, :], pk_[:, 640:641], ps3[:, 0:128], ALU.mult, ALU.add, [BS[dh], bpk_, pb3], [BS[dh]])
                if want_o:
                    if d_ == 0:
                        p.copy("act", o_sb[i][:, h, :], ps2[:, 0:128], [pb2], [Bo[i]])
                    else:
                        p.tt("dve", o_sb[i][:, h, :], ps2[:, 0:128], of_sb[i][:, h, :], ALU.add, [pb2, Bof[i]], [Bo[i]])
            if not want_o:
                continue
            if d_ == 0:
                p.dma("pool", S["OF"][t * 128:(t + 1) * 128, :], o_sb[i].rearrange("p a b -> p (a b)"), R=[Bo[i]], W=[g.DBT("OF", t)])
                continue
            for h in range(4):
                p.act(junk, o_sb[i][:, h, :], AF.Square, [Bo[i]], [Bj, Bst], accum_out=st[:, 0, h:h + 1])
            p.act(st[:, 1, :], st[:, 0, :], AF.Sqrt, [Bst], [Bst], scale=1.0 / 128, bias=EPS)
            p.recip(st[:, 2, :], st[:, 1, :], [Bst], [Bst])
            p.act(gate[i], gate[i], AF.Silu, [Bgate[i]], [Bgate[i]])
            for h in range(4):
                p.stt(y1[:, h, :], o_sb[i][:, h, :], st[:, 2, h:h + 1], goutb, ALU.mult, ALU.mult, [Bo[i], Bst, Bc], [By1])
            p.tt("pool", ydn, y1.rearrange("p a b -> p (a b)"), gate[i], ALU.mult, [By1, Bgate[i]], [Bydn])
            ps, pb = bank(g)
            psb = ps[:, :].bitcast(BF16)
            for h in range(4):
                p.tr(psb[:, h * 128:(h + 1) * 128], ydn[:, h * 128:(h + 1) * 128], g.identB, [Bydn, g.Bconst], [pb])
            p.copy("act", zt[i], psb[:, 0:512].rearrange("p (a b) -> p a b", a=4), [pb], [Bzt[i]])
            p.dma("pool", S["ZT"][4:8, :, t * 128:(t + 1) * 128].rearrange("c p t -> p c t"), zt[i], R=[Bzt[i]], W=[])


def even_na(g, l, e, ctx_out):
    p, I, S = g.p, g.I, g.S
    new_phase(g)
    nrows = 64
    lim = g.cfg.get("ntiles")
    if lim:
        nrows = g.cfg.get("narows") or 2 * lim
    Bc = Buf()
    KT = alloc(g, [4, NTOK], BF16)
    p.dma("sp", KT, S["QK"][4:8, :, :].rearrange("c p t -> p c t"), R=[], W=[Bc])
    VAs = alloc(g, [NT, 520], BF16)
    p.dma("sp", VAs, S["VA"].rearrange("(t p) f -> p t f", p=128), R=[], W=[Bc])
    tabf = alloc(g, [8, 16, 64], F32)
    maskf = alloc(g, [64], F32)
    TT = alloc(g, [8, 16, 64], BF16)
    p.memset("pool", tabf, 0.0, [Bc])
    for h in range(8):
        p.dma("sp", tabf[0:64, h, 0:15, :], I["rpb_tab"][e, h].rearrange("b k c -> k b c"), W=[Bc])
        p.dma("sp", tabf[64:128, h, 0:14, :], I["rpb_tab"][e, h, 1:15].rearrange("b k c -> k b c"), W=[Bc])
    p.dma("sp", maskf[0:64, :], I["namask"][:, :], W=[Bc])
    p.dma("sp", maskf[64:128, :], I["namask"][:, :], W=[Bc])
    for h in range(8):
        p.tt("pool", TT[:, h, :, :], tabf[:, h, :, :], maskf.unsqueeze(1).broadcast_to([128, 16, 64]), ALU.add, [Bc], [Bc])
    qT = [alloc(g, [4, 128], BF16) for _ in range(2)]
    BqT = [Buf() for _ in range(2)]
    PT = [alloc(g, [512], BF16) for _ in range(3)]
    BPT = [Buf() for _ in range(3)]
    att = [alloc(g, [8, 64], BF16) for _ in range(2)]
    Batt = [Buf() for _ in range(2)]
    rinv = [alloc(g, [8], F32) for _ in range(2)]
    Brinv = [Buf() for _ in range(2)]
    zt = [alloc(g, [4, 128], BF16) for _ in range(2)]
    Bzt = [Buf() for _ in range(2)]
    ipt = 0
    stc = [0]

    def st_bank():
        i_ = 4 + stc[0] % 3
        stc[0] += 1
        return g.ps[i_], g.PB[i_]

    def oa_banks(k):
        b0 = (k % 2) * 2
        return [(g.ps[b0], g.PB[b0]), (g.ps[b0 + 1], g.PB[b0 + 1])]

    def finish(oa, i, tq):
        for bnk in range(2):
            ps, pb = oa[bnk]
            v = ps[:, 0:260].rearrange("p (a b) -> p a b", a=4)
            p.recip(rinv[i][:, bnk * 4:(bnk + 1) * 4], v[:, :, 64], [pb], [Brinv[i]])
            p.tt("dve", att[i][:, bnk * 4:(bnk + 1) * 4, :], v[:, :, 0:64],
                 rinv[i][:, bnk * 4:(bnk + 1) * 4].unsqueeze(2).broadcast_to([128, 4, 64]), ALU.mult, [pb, Brinv[i]], [Batt[i]])
        ps, pb = g.ps[7], g.PB[7]
        psb = ps[:, :].bitcast(BF16)
        af = att[i].rearrange("p a b -> p (a b)")
        for c in range(4):
            p.tr(psb[:, c * 128:(c + 1) * 128], af[:, c * 128:(c + 1) * 128], g.identB, [Batt[i], g.Bconst], [pb])
        p.copy("act", zt[i], psb[:, 0:512].rearrange("p (a b) -> p a b", a=4), [pb], [Bzt[i]])
        p.dma("pool", S["ZT"][0:4, :, tq * 128:(tq + 1) * 128].rearrange("c p t -> p c t"), zt[i], R=[Bzt[i]], W=[])

    if ctx_out:
        qc = alloc(g, [4, 256], BF16)
        Bqc = Buf()
        p.dma("sp", qc, S["QK"][0:4, :, 0:256].rearrange("c p t -> p c t"), R=[], W=[Bqc])
        PTc = [alloc(g, [512], BF16) for _ in range(2)]
        BPTc = [Buf() for _ in range(2)]
        oa = [oa_banks(0), oa_banks(1)]
        for h in range(8):
            pr, pb_ = h // 2, (h % 2) * 64
            ps, pb = st_bank()
            for j in range(2):
                p.mm(ps[:, j * 256:(j + 1) * 256], KT[pb_:pb_ + 64, pr, j * 128:(j + 1) * 128], qc[pb_:pb_ + 64, pr, :], True, True, [Bc, Bqc], [pb])
            P_, BP_ = PTc[h % 2], BPTc[h % 2]
            p.act(P_, ps[:, :], AF.Exp, [pb], [BP_])
            for qt in range(2):
                ops_, opb = oa[qt][h // 4]
                hh = h % 4
                for j in range(2):
                    p.mm(ops_[:, hh * 65:(hh + 1) * 65], P_[:, j * 256 + qt * 128: j * 256 + (qt + 1) * 128], VAs[:, j, h * 65:(h + 1) * 65],
                         j == 0, j == 1, [BP_, Bc], [opb])
        for qt in range(2):
            finish(oa[qt], qt, qt)
    for r in range(nrows):
        tq = 2 + r // 2
        iq = (r // 2) % 2
        hq = r % 2
        if hq == 0:
            p.dma("sp", qT[iq], S["QK"][0:4, :, tq * 128:(tq + 1) * 128].rearrange("c p t -> p c t"), R=[], W=[BqT[iq]])
            oa = oa_banks(r // 2)
        rs = min(max(r - 4, 0), 56)
        units = []
        wr = rs
        while wr < rs + 8:
            if wr % 2 == 0 and wr + 1 < rs + 8:
                units.append((wr, 2))
                wr += 2
            else:
                units.append((wr, 1))
                wr += 1
        nloc = (rs + 7) // 2 - rs // 2 + 1
        for h in range(8):
            pr, pb_ = h // 2, (h % 2) * 64
            ps, pb = st_bank()
            q_ap = qT[iq][pb_:pb_ + 64, pr, hq * 64:(hq + 1) * 64]
            geo = []
            for (wr, n) in units:
                slot = wr // 2 - rs // 2
                col = 256 + wr * 64
                if n == 2:
                    lo, hi, b = 0, 128, wr - r + 7
                elif wr % 2 == 1:
                    lo, hi, b = 64, 128, wr - r + 6
                else:
                    lo, hi, b = 0, 64, wr - r + 7
                geo.append((wr, slot, lo, hi))
                out = ps[lo:hi, slot * 64:(slot + 1) * 64]
                p.mm(out, KT[pb_:pb_ + 64, pr, col:col + (hi - lo)], q_ap, True, False, [Bc, BqT[iq]], [pb])
                p.mm(out, g.identB[:, lo:hi], TT[:, h, b, :], False, True, [g.Bconst, Bc], [pb])
            for j in range(2):
                slot = nloc + j
                p.mm(ps[:, slot * 64:(slot + 1) * 64], KT[pb_:pb_ + 64, pr, j * 128:(j + 1) * 128], q_ap, True, True, [Bc, BqT[iq]], [pb])
            ncol = (nloc + 2) * 64
            P_, BP_ = PT[ipt % 3], BPT[ipt % 3]
            ipt += 1
            p.act(P_[:, 0:ncol], ps[:, 0:ncol], AF.Exp, [pb], [BP_])
            ops_, opb = oa[h // 4]
            hh = h % 4
            out = ops_[hq * 64:(hq + 1) * 64, hh * 65:(hh + 1) * 65]
            nmm = len(geo) + 2
            for ii, (wr, slot, lo, hi) in enumerate(geo):
                p.mm(out, P_[lo:hi, slot * 64:(slot + 1) * 64], VAs[lo:hi, 2 + wr // 2, h * 65:(h + 1) * 65], ii == 0, False, [BP_, Bc], [opb])
            for j in range(2):
                slot = nloc + j
                p.mm(out, P_[:, slot * 64:(slot + 1) * 64], VAs[:, j, h * 65:(h + 1) * 65], False, j == 1, [BP_, Bc], [opb])
        if hq == 1:
            finish(oa, iq, tq)


def even_out(g, l, e):
    p, I, S = g.p, g.I, g.S
    new_phase(g)
    if "noffn" not in g.cfg:
        g.top_reserved = 32768
        prefetch_ffn_w(g, l)
    tiles = tile_list(g, l, "mixout")
    Wo = alloc(g, [8, 1024], BF16)
    BWo = Buf()
    load_w_bf16(g, Wo, I["w_out_even"][e].rearrange("(k p) n -> p k n", p=128), BWo)
    gates = [load_gate(g, l, s_, 0) for s_ in range(2)]
    x = [alloc(g, [D], F32) for _ in range(2)]
    Bx = [Buf() for _ in range(2)]
    zt = [alloc(g, [8, 128], BF16) for _ in range(2)]
    Bzt = [Buf() for _ in range(2)]
    tmp = [alloc(g, [512], F32) for _ in range(2)]
    Btmp = [Buf() for _ in range(2)]
    for it, t in enumerate(tiles):
        i = it % 2
        s_ = 1 if t < 2 else 0
        p.dma("sp", x[i], S["X"][t * 128:(t + 1) * 128, :], R=[g.DBT("X", t)], W=[Bx[i]])
        p.dma("sp", zt[i], S["ZT"][:, :, t * 128:(t + 1) * 128].rearrange("c p t -> p c t"), R=[], W=[Bzt[i]])
        pss = []
        for n in range(2):
            ps, pb = bank(g)
            for k in range(8):
                p.mm(ps[:, :], zt[i][:, k, :], Wo[:, k, n * 512:(n + 1) * 512], k == 0, k == 7, [Bzt[i], BWo], [pb])
            pss.append((ps, pb))
        gt, Bgt = gates[s_]
        resid_update(g, pss, x[i], Bx[i], gt, Bgt, tmp, Btmp)
        p.dma("pool", S["X"][t * 128:(t + 1) * 128, :], x[i], R=[Bx[i]], W=[g.DBT("X", t)])


def resid_update(g, ps_list, x, Bx, gt, Bgt, tmp, Btmp):
    p = g.p
    for n, (ps, pb) in enumerate(ps_list):
        p.tt("dve", tmp[n], ps[:, :], gt[:, n * 512:(n + 1) * 512], ALU.mult, [pb, Bgt], [Btmp[n]])
        p.tt("pool", x[:, n * 512:(n + 1) * 512], tmp[n], x[:, n * 512:(n + 1) * 512], ALU.add, [Btmp[n], Bx], [Bx])


def phase_odd(g, l):
    p, I, S = g.p, g.I, g.S
    o = l // 2
    tiles = tile_list(g, l, "mixout")
    w_in = I["w_in_odd"][o].rearrange("(k p) n -> p k n", p=128)
    new_phase(g)
    Wv = alloc(g, [8, 3072], BF16)
    BWv = Buf()
    load_w_bf16(g, Wv, w_in[:, :, 3072:6144], BWv, max_cols=1024)
    wsT = alloc(g, [8, 128], BF16)
    bsT = alloc(g, [8], F32)
    gvb = alloc(g, [3072], F32)
    Bc = Buf()
    p.dma("pool", wsT, I["gm_wsT"][o], W=[Bc], stream="wl")
    p.dma("sp", bsT, I["gm_bsT"][o], W=[Bc])
    p.dma("sp", gvb, I["gm_g_v"][o, :].partition_broadcast(128), W=[Bc])
    mods = [load_mod(g, l, s_, 0) for s_ in range(2)]
    nb = NormBufs(g, nx=3)
    vf = [alloc(g, [3072], F32) for _ in range(2)]
    Bvf = [Buf() for _ in range(2)]
    vn = [alloc(g, [3072], BF16) for _ in range(2)]
    Bvn = [Buf() for _ in range(2)]
    junk = alloc(g, [3072], BF16)
    Bj = Buf()
    st = [alloc(g, [4], F32) for _ in range(2)]
    Bst = [Buf() for _ in range(2)]
    mix = [alloc(g, [3072], F32) for _ in range(2)]
    Bmix = [Buf() for _ in range(2)]
    tmpg = [alloc(g, [384], F32) for _ in range(2)]
    Btg = [Buf() for _ in range(2)]
    def stage0a(it, t):
        norm_a(g, nb, it % 2, t, mods[1 if t < 2 else 0], ix=it % 3)

    def stage0b(it, t):
        return norm_b(g, nb, it % 2)

    def stage1(it, t, cur):
        i = it % 2
        s_ = 1 if t < 2 else 0
        hT, BhT = cur
        for n in range(6):
            ps, pb = bank(g)
            for k in range(8):
                p.mm(ps[:, :], hT[:, k, :], Wv[:, k, n * 512:(n + 1) * 512], k == 0, k == 7, [BhT, BWv], [pb])
            p.act(vf[i][:, n * 512:(n + 1) * 512], ps[:, :], AF.Gelu, [pb], [Bvf[i]])
        p.act(junk, vf[i], AF.Square, [Bvf[i]], [Bj, Bst[i]], accum_out=st[i][:, 0:1])
        p.act(st[i][:, 1:2], st[i][:, 0:1], AF.Sqrt, [Bst[i]], [Bst[i]], scale=1.0 / 3072, bias=EPS)
        p.recip(st[i][:, 2:3], st[i][:, 1:2], [Bst[i]], [Bst[i]])
        p.ts("dve", vn[i], vf[i], st[i][:, 2:3], None, ALU.mult, None, [Bvf[i], Bst[i]], [Bvn[i]])
        for gi in range(8):
            ps, pb = bank(g)
            p.mm(ps[:, 0:384], wsT[:, gi, :], vn[i][:, gi * 384:(gi + 1) * 384], True, True, [Bc, Bvn[i]], [pb])
            tg, btg = tmpg[gi % 2], Btg[gi % 2]
            p.tt("dve", tg, ps[:, 0:384], gvb[:, gi * 384:(gi + 1) * 384], ALU.mult, [pb, Bc], [btg])
            p.ts("dve", mix[i][:, gi * 384:(gi + 1) * 384], tg, bsT[:, gi:gi + 1], None, ALU.add, None, [btg, Bc], [Bmix[i]])
        p.dma("pool", S["MIX"][t * 128:(t + 1) * 128, :], mix[i], R=[Bmix[i]], W=[])
    pipelined(tiles, stage0a, stage0b, stage1)
    new_phase(g)
    Wu = alloc(g, [8, 3072], BF16)
    Wo = alloc(g, [24, 1024], BF16)
    BWu, BWo = Buf(), Buf()
    load_w_bf16(g, Wu, w_in[:, :, 0:3072], BWu, max_cols=1024)
    load_w_bf16(g, Wo, I["w_out_odd"][o].rearrange("(k p) n -> p k n", p=128), BWo)
    mods = [load_mod(g, l, s_, 0) for s_ in range(2)]
    gates = [load_gate(g, l, s_, 0) for s_ in range(2)]
    nb = NormBufs(g, nx=3)
    mix = [alloc(g, [3072], F32) for _ in range(2)]
    Bmix = [Buf() for _ in range(2)]
    uf = [alloc(g, [512], F32) for _ in range(2)]
    Buf_ = [Buf() for _ in range(2)]
    sb = [alloc(g, [3072], BF16) for _ in range(2)]
    Bsb = [Buf() for _ in range(2)]
    sT = [alloc(g, [24, 128], BF16) for _ in range(2)]
    BsT = [Buf() for _ in range(2)]
    tmp = [alloc(g, [512], F32) for _ in range(2)]
    Btmp = [Buf() for _ in range(2)]
    uic = [0]

    def stage0a(it, t):
        norm_a(g, nb, it % 2, t, mods[1 if t < 2 else 0], ix=it % 3)

    def stage0b(it, t):
        p.dma("sp", mix[it % 2], S["MIX"][t * 128:(t + 1) * 128, :], R=[], W=[Bmix[it % 2]])
        return norm_b(g, nb, it % 2)

    def stage1(it, t, cur):
        i = it % 2
        s_ = 1 if t < 2 else 0
        hT, BhT = cur
        for n in range(6):
            ps, pb = bank(g)
            for k in range(8):
                p.mm(ps[:, :], hT[:, k, :], Wu[:, k, n * 512:(n + 1) * 512], k == 0, k == 7, [BhT, BWu], [pb])
            u_, bu_ = uf[uic[0] % 2], Buf_[uic[0] % 2]
            uic[0] += 1
            p.act(u_, ps[:, :], AF.Gelu, [pb], [bu_])
            p.tt("pool", sb[i][:, n * 512:(n + 1) * 512], u_, mix[i][:, n * 512:(n + 1) * 512], ALU.mult, [bu_, Bmix[i]], [Bsb[i]])
        for q in range(3):
            ps, pb = bank(g)
            psb = ps[:, :].bitcast(BF16)
            for kk in range(8):
                k = q * 8 + kk
                p.tr(psb[:, kk * 128:(kk + 1) * 128], sb[i][:, k * 128:(k + 1) * 128], g.identB, [Bsb[i], g.Bconst], [pb])
            dst = sT[i][:, q * 8:(q + 1) * 8, :]
            src = psb.rearrange("p (a b) -> p a b", a=8)
            if q % 2 == 0:
                p.copy("act", dst, src, [pb], [BsT[i]])
            else:
                p.copy("dve", dst, src, [pb], [BsT[i]])
        pss = []
        for n in range(2):
            ps, pb = bank(g)
            for k in range(24):
                p.mm(ps[:, :], sT[i][:, k, :], Wo[:, k, n * 512:(n + 1) * 512], k == 0, k == 23, [BsT[i], BWo], [pb])
            pss.append((ps, pb))
        gt, Bgt = gates[s_]
        resid_update(g, pss, nb.x[it % 3], nb.Bx[it % 3], gt, Bgt, tmp, Btmp)
        p.dma("pool", S["X"][t * 128:(t + 1) * 128, :], nb.x[it % 3], R=[nb.Bx[it % 3]], W=[g.DBT("X", t)])


    pipelined(tiles, stage0a, stage0b, stage1)


def make_in_maps(inputs):
    f = lambda a: np.ascontiguousarray(np.asarray(a, dtype=np.float32))
    shared = {}
    for k in ["w_ada", "b_ada", "g_norm_mix", "g_norm_ffn", "w_in_even", "w_out_even", "dn_conv", "dn_g_out",
              "w_in_odd", "gm_g_v", "w_out_odd", "w_ff1", "w_ff2", "g_final"]:
        shared[k] = f(inputs[k])
    shared["rpb_tab"] = rpb_table(f(inputs["na_rpb"]))
    shared["dn_a_log"] = f(inputs["dn_a_log"]).reshape(2, 8)
    shared["dn_dt_bias"] = f(inputs["dn_dt_bias"]).reshape(2, 8)
    shared["gm_wsT"] = np.ascontiguousarray(f(inputs["gm_ws"]).transpose(0, 3, 1, 2))
    shared["gm_bsT"] = np.ascontiguousarray(f(inputs["gm_bs"]).transpose(0, 2, 1))
    shared.update(host_consts())
    x = f(inputs["x"])
    ctx = f(inputs["ctx"])
    c = f(inputs["c"])
    c_ctx = f(inputs["c_ctx"])
    maps = []
    for b in range(8):
        m = dict(shared)
        m["x"] = x[b]
        m["ctx"] = ctx[b]
        m["cvec"] = np.ascontiguousarray(np.stack([c[b], c_ctx]))
        maps.append(m)
    return maps


def kernel(**inputs):
    nc, g = build()
    maps = make_in_maps(inputs)
    res = run_bass_kernel_spmd(nc, maps, core_ids=list(range(8)))
    return np.stack([np.asarray(r["out"], dtype=np.float32) for r in res.results], axis=0)
```
